# Optimizing a Trainium2 kernel written in Bass

```python
import math
import jax, jax.numpy as jnp
from jax import lax
import numpy as np

D_MODEL = 1024
BATCH = 4
SEQ = 8192
DEPTH = 2

HEAD_DIM = 64
RWKV_HEADS = (D_MODEL // 2) // HEAD_DIM
FOX_HEADS = (D_MODEL // 2) // HEAD_DIM
RWKV_WIDTH = RWKV_HEADS * HEAD_DIM
FOX_WIDTH = FOX_HEADS * HEAD_DIM
MIX_WIDTH = RWKV_WIDTH + FOX_WIDTH
DECAY_LORA = int(max(32, round(1.8 * RWKV_WIDTH ** 0.5 / 32) * 32))
AAA_LORA = int(max(32, round(1.8 * RWKV_WIDTH ** 0.5 / 32) * 32))
GATE_LORA = int(max(32, round(0.6 * RWKV_WIDTH ** 0.8 / 32) * 32))
RWKV_COLS = 3 * RWKV_WIDTH + DECAY_LORA + AAA_LORA + GATE_LORA
FOX_COLS = 3 * FOX_WIDTH + FOX_HEADS
IN_COLS = RWKV_COLS + FOX_COLS
FOX_BLOCK = 128
CONV_WIDTH = 31
D_FF = 2816
N_EXPERTS = 8
TOP_K = 2
D_FF_EXPERT = 3584
LN_EPS = 1e-5
GN_EPS = 64e-5
DEEPNORM_ALPHA = (2.0 * DEPTH) ** 0.25
DEEPNORM_BETA = (8.0 * DEPTH) ** -0.25
N_EVEN = (DEPTH + 1) // 2
N_ODD = DEPTH // 2

kernel_name = "hybrid_rwkv7_fox_conformer_moe_deepnorm"


def layer_norm(x, g, b, eps=LN_EPS):
    xf = x.astype(jnp.float32)
    mu = jnp.mean(xf, axis=-1, keepdims=True)
    var = jnp.mean(jnp.square(xf - mu), axis=-1, keepdims=True)
    return ((xf - mu) * lax.rsqrt(var + eps) * g + b).astype(x.dtype)


def token_shift(p, mu):
    prev = jnp.pad(p, ((0, 0), (1, 0), (0, 0)))[:, :-1]
    return p + mu * (prev - p)


def rwkv7_time_mix(p, mu, w0, w_up, a0, a_up, g_up, k_k, k_a, r_k, gn_g, gn_b):
    B, T, _ = p.shape
    f32 = jnp.float32
    p = token_shift(p, mu)
    o1 = RWKV_WIDTH
    o2 = 2 * RWKV_WIDTH
    o3 = 3 * RWKV_WIDTH
    o4 = o3 + DECAY_LORA
    o5 = o4 + AAA_LORA
    r, k, v, wd, ad, gd = jnp.split(p, [o1, o2, o3, o4, o5], axis=-1)
    w = -jax.nn.softplus(-(w0 + jnp.tanh(wd) @ w_up).astype(f32)) - 0.5
    decay = jnp.exp(-jnp.exp(w))
    a = jax.nn.sigmoid((a0 + ad @ a_up).astype(f32))
    g = jax.nn.sigmoid(gd) @ g_up

    hs = lambda t: t.astype(f32).reshape(B, T, RWKV_HEADS, HEAD_DIM)
    r, k, v, decay, a = hs(r), hs(k), hs(v), hs(decay), hs(a)
    kk = k * k_k
    kk = kk / jnp.maximum(jnp.linalg.norm(kk, axis=-1, keepdims=True), 1e-12)
    k = k * (1.0 + (a - 1.0) * k_a)
    b = kk * a

    def step(S, inp):
        r_t, w_t, k_t, v_t, kk_t, b_t = inp
        sa = jnp.einsum('bhij,bhj->bhi', S, -kk_t)
        S = (S * w_t[:, :, None, :] + sa[..., None] * b_t[:, :, None, :]
             + v_t[..., None] * k_t[:, :, None, :])
        y_t = jnp.einsum('bhij,bhj->bhi', S, r_t)
        return S, y_t

    tm = lambda t: jnp.transpose(t, (1, 0, 2, 3))
    S0 = jnp.zeros((B, RWKV_HEADS, HEAD_DIM, HEAD_DIM), f32)
    _, ys = lax.scan(step, S0, (tm(r), tm(decay), tm(k), tm(v), tm(kk), tm(b)))
    y = jnp.transpose(ys, (1, 0, 2, 3))

    mu_y = jnp.mean(y, axis=-1, keepdims=True)
    var_y = jnp.mean(jnp.square(y - mu_y), axis=-1, keepdims=True)
    y = (y - mu_y) * lax.rsqrt(var_y + GN_EPS) * gn_g + gn_b
    y = y + jnp.sum(r * k * r_k, axis=-1, keepdims=True) * v
    y = y.reshape(B, T, RWKV_WIDTH) * g.astype(f32)
    return y.astype(p.dtype)


def forgetting_attention(p, b_f):
    B, T, _ = p.shape
    f32 = jnp.float32
    q, k, v, f_logit = jnp.split(p, [FOX_WIDTH, 2 * FOX_WIDTH, 3 * FOX_WIDTH], axis=-1)
    heads = lambda t: jnp.transpose(t.reshape(B, T, FOX_HEADS, HEAD_DIM), (0, 2, 1, 3))
    q, k, v = heads(q), heads(k), heads(v)
    log_f = jax.nn.log_sigmoid(f_logit.astype(f32) + b_f)
    c = jnp.transpose(jnp.cumsum(log_f, axis=1), (0, 2, 1))
    scale = 1.0 / math.sqrt(HEAD_DIM)
    key_pos = jnp.arange(T)

    def block(i):
        start = i * FOX_BLOCK
        qb = lax.dynamic_slice_in_dim(q, start, FOX_BLOCK, axis=2)
        cb = lax.dynamic_slice_in_dim(c, start, FOX_BLOCK, axis=2)
        s = jnp.einsum('bhqd,bhkd->bhqk', qb, k, preferred_element_type=f32) * scale
        s = s + cb[..., None] - c[:, :, None, :]
        q_pos = start + jnp.arange(FOX_BLOCK)
        s = jnp.where(key_pos[None, :] <= q_pos[:, None], s, -jnp.inf)
        pr = jax.nn.softmax(s, axis=-1)
        return jnp.einsum('bhqk,bhkd->bhqd', pr.astype(v.dtype), v)

    out = lax.map(block, jnp.arange(T // FOX_BLOCK))
    out = jnp.transpose(out, (1, 0, 3, 2, 4)).reshape(B, T, FOX_WIDTH)
    return out


def conformer_conv(x, w_pw1, b_pw1, w_dw, b_dw, ln_g, ln_b, w_pw2, b_pw2):
    h = x @ w_pw1 + b_pw1
    h = h[..., :D_MODEL] * jax.nn.sigmoid(h[..., D_MODEL:])
    h = lax.conv_general_dilated(
        h, w_dw[:, None, :].astype(h.dtype), window_strides=(1,),
        padding=[(CONV_WIDTH - 1, 0)], dimension_numbers=('NWC', 'WIO', 'NWC'),
        feature_group_count=D_MODEL) + b_dw
    h = jax.nn.silu(layer_norm(h, ln_g, ln_b))
    return h @ w_pw2 + b_pw2


def swiglu(x, w_gate, w_up, w_down):
    return (jax.nn.silu(x @ w_gate) * (x @ w_up)) @ w_down


def moe_swiglu(x, w_router, w_gate, w_up, w_down):
    B, T, D = x.shape
    xt = x.reshape(B * T, D)
    logits = (xt @ w_router).astype(jnp.float32)
    top_v, top_i = lax.top_k(logits, TOP_K)
    top_w = jax.nn.softmax(top_v, axis=-1)
    gates = jnp.sum(jax.nn.one_hot(top_i, N_EXPERTS, dtype=jnp.float32) * top_w[..., None], axis=1)
    gates = gates.astype(x.dtype)
    y = jnp.zeros_like(xt)
    for e in range(N_EXPERTS):
        y = y + gates[:, e:e + 1] * swiglu(xt, w_gate[e], w_up[e], w_down[e])
    return y.reshape(B, T, D)


def setup_inputs(seed: int = 0) -> dict:
    key = jax.random.key(seed)
    ks = iter(jax.random.split(key, 64))
    f32 = jnp.float32

    def nrm(shape, scale):
        return jax.random.normal(next(ks), shape, f32) * scale

    def uni(shape, lo, hi):
        return jax.random.uniform(next(ks), shape, f32, lo, hi)

    LE, LO, H, N, D = N_EVEN, N_ODD, RWKV_HEADS, HEAD_DIM, D_MODEL
    beta = DEEPNORM_BETA
    col_scale = jnp.concatenate([
        jnp.ones((2 * RWKV_WIDTH,), f32), jnp.full((RWKV_WIDTH,), beta, f32),
        jnp.ones((DECAY_LORA + AAA_LORA + GATE_LORA,), f32),
        jnp.ones((2 * FOX_WIDTH,), f32), jnp.full((FOX_WIDTH,), beta, f32),
        jnp.ones((FOX_HEADS,), f32)])

    inp = {}
    inp["x"] = nrm((BATCH, SEQ, D), 1.0)
    inp["mix_w_in"] = nrm((LE, D, IN_COLS), D ** -0.5) * col_scale
    inp["rwkv_mu"] = uni((LE, RWKV_COLS), 0.0, 1.0)
    inp["rwkv_w0"] = uni((LE, RWKV_WIDTH), -6.0, -1.0)
    inp["rwkv_w_up"] = nrm((LE, DECAY_LORA, RWKV_WIDTH), 0.5 * DECAY_LORA ** -0.5)
    inp["rwkv_a0"] = nrm((LE, RWKV_WIDTH), 0.1)
    inp["rwkv_a_up"] = nrm((LE, AAA_LORA, RWKV_WIDTH), 0.5 * AAA_LORA ** -0.5)
    inp["rwkv_g_up"] = nrm((LE, GATE_LORA, RWKV_WIDTH), GATE_LORA ** -0.5)
    inp["rwkv_k_k"] = 0.85 + nrm((LE, H, N), 0.05)
    inp["rwkv_k_a"] = 1.0 + nrm((LE, H, N), 0.05)
    inp["rwkv_r_k"] = nrm((LE, H, N), 0.1)
    inp["rwkv_gn_g"] = 1.0 + nrm((LE, H, N), 0.02)
    inp["rwkv_gn_b"] = nrm((LE, H, N), 0.02)
    inp["fox_b_f"] = uni((LE, FOX_HEADS), 1.0, 5.0)
    inp["mix_w_out"] = nrm((LE, MIX_WIDTH, D), beta * MIX_WIDTH ** -0.5)
    inp["mix_ln_g"] = 1.0 + nrm((LE, D), 0.02)
    inp["mix_ln_b"] = nrm((LE, D), 0.02)
    inp["ffn_w_gate"] = nrm((LE, D, D_FF), beta * D ** -0.5)
    inp["ffn_w_up"] = nrm((LE, D, D_FF), beta * D ** -0.5)
    inp["ffn_w_down"] = nrm((LE, D_FF, D), beta * D_FF ** -0.5)
    inp["ffn_ln_g"] = 1.0 + nrm((LE, D), 0.02)
    inp["ffn_ln_b"] = nrm((LE, D), 0.02)
    inp["conv_w_pw1"] = nrm((LO, D, 2 * D), D ** -0.5)
    inp["conv_b_pw1"] = nrm((LO, 2 * D), 0.02)
    inp["conv_w_dw"] = nrm((LO, CONV_WIDTH, D), CONV_WIDTH ** -0.5)
    inp["conv_b_dw"] = nrm((LO, D), 0.02)
    inp["conv_ln_g"] = 1.0 + nrm((LO, D), 0.02)
    inp["conv_ln_b"] = nrm((LO, D), 0.02)
    inp["conv_w_pw2"] = nrm((LO, D, D), beta * D ** -0.5)
    inp["conv_b_pw2"] = nrm((LO, D), 0.02)
    inp["conv_post_ln_g"] = 1.0 + nrm((LO, D), 0.02)
    inp["conv_post_ln_b"] = nrm((LO, D), 0.02)
    inp["moe_w_router"] = nrm((LO, D, N_EXPERTS), D ** -0.5)
    inp["moe_w_gate"] = nrm((LO, N_EXPERTS, D, D_FF_EXPERT), beta * D ** -0.5)
    inp["moe_w_up"] = nrm((LO, N_EXPERTS, D, D_FF_EXPERT), beta * D ** -0.5)
    inp["moe_w_down"] = nrm((LO, N_EXPERTS, D_FF_EXPERT, D), beta * D_FF_EXPERT ** -0.5)
    inp["moe_ln_g"] = 1.0 + nrm((LO, D), 0.02)
    inp["moe_ln_b"] = nrm((LO, D), 0.02)
    return inp


def reference(x, mix_w_in, rwkv_mu, rwkv_w0, rwkv_w_up, rwkv_a0, rwkv_a_up, rwkv_g_up,
              rwkv_k_k, rwkv_k_a, rwkv_r_k, rwkv_gn_g, rwkv_gn_b, fox_b_f, mix_w_out,
              mix_ln_g, mix_ln_b, ffn_w_gate, ffn_w_up, ffn_w_down, ffn_ln_g, ffn_ln_b,
              conv_w_pw1, conv_b_pw1, conv_w_dw, conv_b_dw, conv_ln_g, conv_ln_b,
              conv_w_pw2, conv_b_pw2, conv_post_ln_g, conv_post_ln_b,
              moe_w_router, moe_w_gate, moe_w_up, moe_w_down, moe_ln_g, moe_ln_b):
    alpha = DEEPNORM_ALPHA
    for i in range(DEPTH):
        j = i // 2
        if i % 2 == 0:
            p = x @ mix_w_in[j]
            y_rwkv = rwkv7_time_mix(p[..., :RWKV_COLS], rwkv_mu[j], rwkv_w0[j], rwkv_w_up[j],
                                    rwkv_a0[j], rwkv_a_up[j], rwkv_g_up[j], rwkv_k_k[j],
                                    rwkv_k_a[j], rwkv_r_k[j], rwkv_gn_g[j], rwkv_gn_b[j])
            y_fox = forgetting_attention(p[..., RWKV_COLS:], fox_b_f[j])
            mixed = jnp.concatenate([y_rwkv, y_fox], axis=-1) @ mix_w_out[j]
            x = layer_norm(alpha * x + mixed, mix_ln_g[j], mix_ln_b[j])
            x = layer_norm(alpha * x + swiglu(x, ffn_w_gate[j], ffn_w_up[j], ffn_w_down[j]),
                           ffn_ln_g[j], ffn_ln_b[j])
        else:
            conv = conformer_conv(x, conv_w_pw1[j], conv_b_pw1[j], conv_w_dw[j], conv_b_dw[j],
                                  conv_ln_g[j], conv_ln_b[j], conv_w_pw2[j], conv_b_pw2[j])
            x = layer_norm(alpha * x + conv, conv_post_ln_g[j], conv_post_ln_b[j])
            moe = moe_swiglu(x, moe_w_router[j], moe_w_gate[j], moe_w_up[j], moe_w_down[j])
            x = layer_norm(alpha * x + moe, moe_ln_g[j], moe_ln_b[j])
    return x
```

```python
import numpy as np
import concourse.bass as bass
import concourse.mybir as mybir
from concourse.bass_utils import run_bass_kernel_spmd

F32 = mybir.dt.float32
BF16 = mybir.dt.bfloat16
AF = mybir.ActivationFunctionType
ALU = mybir.AluOpType
AX = mybir.AxisListType
ENGS = ['sp', 'act', 'dve', 'pool', 'pe']
NEAR = 2


class Rec:
    __slots__ = ('eng', 'fn', 'deps', 'dma', 'key', 'sig', 'cnt', 'dsem', 'dval', 'small', 'idx', 'rows')

    def __init__(self, eng, fn, dma, key):
        self.eng = eng; self.fn = fn; self.dma = dma; self.key = key
        self.deps = (); self.sig = False; self.cnt = None; self.dsem = None; self.dval = None; self.small = False; self.rows = None


class Sched:
    def __init__(self, nc, n_dma_sems=72):
        self.nc = nc
        self._ctx = []
        self.esem = {}
        for e in ENGS:
            c = nc.semaphore("es_" + e); self._ctx.append(c); self.esem[e] = c.__enter__()
        self.dsems = []
        for i in range(n_dma_sems):
            c = nc.semaphore("ds_%d" % i); self._ctx.append(c); self.dsems.append(c.__enter__())
        self.dval = [0] * n_dma_sems
        self.ecount = {e: 0 for e in ENGS}
        self.waited = {e: {} for e in ENGS}
        self.recs = {e: [] for e in ENGS}
        self.lastw = {}
        self.rd = {}
        self.nextsem = 0
        self.ninstr = 0
        self.alias = {}
        self.banks = set()
        self.small = False

    def set_alias(self, al):
        self.alias = dict(al); self.banks = set(al.values())

    def close(self):
        for c in reversed(self._ctx):
            c.__exit__(None, None, None)

    def op(self, eng, fn, r=(), w=(), dma=False, key=None, rows=None):
        rec = Rec(eng, fn, dma, key)
        rec.rows = rows
        rec.small = self.small
        rec.idx = len(self.recs[eng])
        al = self.alias
        if al:
            r2 = [al.get(x, x) for x in r]; w2 = [al.get(x, x) for x in w]
            bk = self.banks
            w = w2 + [x for x in r2 if x in bk]
            r = [x for x in r2 if x not in bk]
        deps = set()
        for b in r:
            x = self.lastw.get(b)
            if x is not None: deps.add(x)
        for b in w:
            x = self.lastw.get(b)
            if x is not None: deps.add(x)
            d = self.rd.get(b)
            if d:
                for x in d.values(): deps.add(x)
        rec.deps = deps
        for b in r:
            d = self.rd.setdefault(b, {})
            if dma:
                d[('dma', id(rec))] = rec
            else:
                d[eng] = rec
        for b in w:
            self.lastw[b] = rec
            self.rd[b] = {}
        self.recs[eng].append(rec)
        return rec

    def dma_in(self, eng, out, in_, dst, src=()):
        dst = [dst] if isinstance(dst, str) else list(dst)
        src = [src] if isinstance(src, str) else list(src)
        return self.op(eng, lambda e: e.dma_start(out=out, in_=in_), r=src, w=dst, dma=True, key=eng + ':' + dst[0])

    def dma_out(self, eng, out, in_, src, dst=()):
        src = [src] if isinstance(src, str) else list(src)
        dst = [dst] if isinstance(dst, str) else list(dst)
        return self.op(eng, lambda e: e.dma_start(out=out, in_=in_), r=src, w=dst, dma=True, key=eng + ':st:' + src[0])

    def flush(self):
        nc = self.nc
        for e in ENGS:
            for rec in self.recs[e]:
                for d in rec.deps:
                    if (not d.dma) and (d.eng != rec.eng or rec.dma or (d.eng != 'pe' and (d.small or rec.idx - d.idx <= NEAR))
                                        or (d.eng == 'pe' and d.rows is not None and rec.rows is not None and d.rows != rec.rows)):
                        d.sig = True
        for e in ENGS:
            for rec in reversed(self.recs[e]):
                if not rec.dma:
                    rec.sig = True
                    break
        keymap = {}
        for e in ENGS:
            c = self.ecount[e]
            for rec in self.recs[e]:
                if rec.dma:
                    k = rec.key
                    if k not in keymap:
                        keymap[k] = self.nextsem % len(self.dsems)
                        self.nextsem += 1
                        assert len(keymap) <= len(self.dsems), "too many dma keys"
                    si = keymap[k]
                    self.dval[si] += 16
                    rec.dsem = self.dsems[si]; rec.dval = self.dval[si]
                else:
                    if rec.sig:
                        c += 1
                        rec.cnt = c
            self.ecount[e] = c
        final_e = dict(self.ecount)
        final_d = list(self.dval)
        recs = self.recs; esem = self.esem; waited = self.waited; dsems = self.dsems

        def mk(ename):
            def run(e):
                wd = waited[ename]
                for rec in recs[ename]:
                    for d in rec.deps:
                        if d.dma:
                            s, v = d.dsem, d.dval
                        elif (d.eng != ename or rec.dma or (d.eng != 'pe' and (d.small or rec.idx - d.idx <= NEAR))
                              or (d.eng == 'pe' and d.rows is not None and rec.rows is not None and d.rows != rec.rows)):
                            s, v = esem[d.eng], d.cnt
                        else:
                            continue
                        if wd.get(s.num, 0) < v:
                            e.wait_ge(s, v)
                            wd[s.num] = v
                    ins = rec.fn(e)
                    if rec.dma:
                        ins.then_inc(rec.dsem, 16)
                    elif rec.sig:
                        ins.then_inc(esem[ename], 1)
                for o in ENGS:
                    if o != ename and wd.get(esem[o].num, 0) < final_e[o]:
                        e.wait_ge(esem[o], final_e[o]); wd[esem[o].num] = final_e[o]
                for i, s in enumerate(dsems):
                    if wd.get(s.num, 0) < final_d[i]:
                        e.wait_ge(s, final_d[i]); wd[s.num] = final_d[i]
            return run

        with nc.Block() as block:
            block.sync(mk('sp'))
            block.scalar(mk('act'))
            block.vector(mk('dve'))
            block.gpsimd(mk('pool'))
            block.tensor(mk('pe'))
        for e in ENGS:
            self.ninstr += len(self.recs[e])
        self.recs = {e: [] for e in ENGS}
        self.lastw = {}
        self.rd = {}


def _l(x):
    return [x] if isinstance(x, str) else list(x)


def mm(S, out, lhsT, rhs, start, stop, r, w, tp=None):
    kw = {}
    if tp is not None:
        kw['tile_position'] = tp
    return S.op('pe', lambda e: e.matmul(out, lhsT=lhsT, rhs=rhs, start=start, stop=stop, **kw), r=_l(r), w=_l(w),
                rows=(lhsT.base_partition(), lhsT.shape[0]))


def tr(S, out, in_, ident, r, w):
    return S.op('pe', lambda e: e.transpose(out, in_, ident), r=_l(r) + ['ident'], w=_l(w),
                rows=(in_.base_partition(), in_.shape[0]))


def act(S, out, in_, func, r, w, bias=None, scale=None, accum=None):
    kw = {}
    if bias is not None: kw['bias'] = bias
    if scale is not None: kw['scale'] = scale
    if accum is not None: kw['accum_out'] = accum
    return S.op('act', lambda e: e.activation(out=out, in_=in_, func=func, **kw), r=_l(r), w=_l(w))


def tt(S, eng, out, a, b, op, r, w):
    return S.op(eng, lambda e: e.tensor_tensor(out=out, in0=a, in1=b, op=op), r=_l(r), w=_l(w))


def ts(S, eng, out, a, s1, s2, op0, op1, r, w, accum=None):
    kw = {}
    if op1 is not None: kw['op1'] = op1
    if accum is not None: kw['accum_out'] = accum
    return S.op(eng, lambda e: e.tensor_scalar(out=out, in0=a, scalar1=s1, scalar2=s2, op0=op0, **kw), r=_l(r), w=_l(w))


def stt(S, eng, out, a, s, b, op0, op1, r, w, accum=None):
    kw = {}
    if accum is not None: kw['accum_out'] = accum
    return S.op(eng, lambda e: e.scalar_tensor_tensor(out=out, in0=a, scalar=s, in1=b, op0=op0, op1=op1, **kw), r=_l(r), w=_l(w))


def cp(S, eng, out, in_, r, w):
    if eng == 'act':
        return S.op('act', lambda e: e.activation(out=out, in_=in_, func=AF.Copy), r=_l(r), w=_l(w))
    return S.op(eng, lambda e: e.tensor_copy(out=out, in_=in_), r=_l(r), w=_l(w))


def mset(S, eng, ap, val, w):
    return S.op(eng, lambda e: e.memset(ap, val), w=_l(w))


class Ctx:
    _n = [0]

    def __init__(self, nc):
        self.nc = nc; self.ctxs = []
        Ctx._n[0] += 1
        self.pfx = "c%d_" % Ctx._n[0]

    def sb(self, name, shape, dt):
        c = self.nc.sbuf_tensor(self.pfx + "s_" + name, shape, dt); self.ctxs.append(c); return c.__enter__()

    def ps(self, name, shape, dt):
        c = self.nc.psum_tensor(self.pfx + "p_" + name, shape, dt); self.ctxs.append(c); return c.__enter__()

    def close(self):
        for c in reversed(self.ctxs):
            c.__exit__(None, None, None)
        self.ctxs = []


C0 = 0.6065306597126334
O_R, O_K, O_V, O_WD, O_AD, O_GD = 0, 512, 1024, 1536, 1568, 1600
O_FQ, O_FK, O_FV, O_FF = 1696, 2208, 2720, 3232


def phase12(S, nc, W, D):
    import os
    STOP = float(os.environ.get('P12_STOP', '99'))
    OWN = W // 2
    NT = W // 512
    Y0 = OWN - 128
    cx = Ctx(nc)
    win = cx.sb("win", [128, 8, 3240], BF16)
    xt = [cx.sb("xt%d" % i, [128, 8, 512], BF16) for i in range(2)]
    c12 = cx.sb("c12", [128, 48], F32)
    om = cx.sb("om", [128, 14], F32)
    omka = cx.sb("omka", [128, 4], F32)
    cm32 = cx.sb("cm32", [128, 5 * 128 + 512], F32)
    cm16 = cx.sb("cm16", [128, 384], BF16)
    lw = cx.sb("lw", [64, 512], BF16)
    gu = cx.sb("gu", [96, 512], BF16)
    bfc = cx.sb("bfc", [8, 2], F32)
    carry = cx.sb("carry", [128, 16], F32)
    SH = [cx.sb("SH%d" % i, [128, 513], F32) for i in range(2)]
    LT = cx.sb("LT", [128, 512], F32)
    LA = cx.sb("LA", [64, 512], BF16)
    SG = cx.sb("SG", [96, 512], BF16)
    fsb = [cx.sb("fsb%d" % i, [128, 512], BF16) for i in range(2)]
    fcb = [cx.sb("fcb%d" % i, [8, 512], F32) for i in range(2)]
    fsq = cx.sb("fsq", [128, 512], BF16)
    fnr = cx.sb("fnr", [128, 512], F32)
    fct = cx.sb("fct", [8, 512], F32)
    ones8 = cx.sb("ones8", [8, 512], F32)
    names32 = ['RS', 'KS', 'VS', 'CL', 'LD', 'E1', 'A', 'KK', 'T1']
    w32 = {k: cx.sb(k, [128, 512], F32) for k in names32}
    names16d = ['Rt', 'At', 'Bt', 'Kt', 'Bb', 'Kb', 'Vb']
    w16d = {k: [cx.sb("%s%d" % (k, i), [128, 512], BF16) for i in range(2)] for k in names16d}
    w16 = {k: cx.sb(k, [128, 512], BF16) for k in ['RKb', 'SQb']}
    PL3 = [cx.sb("PL3_%d" % i, [128, 8], F32) for i in range(3)]
    GF3 = [cx.sb("GF3_%d" % i, [128, 512], F32) for i in range(3)]
    BON3 = [cx.sb("BON3_%d" % i, [128, 512], F32) for i in range(3)]
    P = []
    for s in range(2):
        P.append(dict(
            TM4=cx.sb("TM4_%d" % s, [128, 4, 4, 128], BF16),
            X6=cx.sb("X6_%d" % s, [128, 4, 2, 128], BF16),
            QT=cx.sb("QT_%d" % s, [128, 512], BF16),
            Y0T=cx.sb("Y0T_%d" % s, [128, 512], F32),
            MTs=cx.sb("MTs_%d" % s, [128, 8, 128], BF16),
            YF=cx.sb("YF_%d" % s, [128, 512], F32),
            G1=cx.sb("G1_%d" % s, [128, 512], F32),
            G2=cx.sb("G2_%d" % s, [128, 512], BF16),
            YO=cx.sb("YO_%d" % s, [128, 512], BF16),
        ))
    GS = []
    for gi in range(2):
        GS.append(dict(
            NN=[cx.sb("NN%d_g%d" % (i, gi), [128, 4, 2, 128], BF16) for i in range(2)],
            A3=cx.sb("A3_g%d" % gi, [128, 3, 4, 128], BF16),
            X=[cx.sb("X%d_g%d" % (i, gi), [128, 4, 128], BF16) for i in range(2)],
        ))
    H32 = [cx.sb("H32_%d" % g, [128, 128], F32) for g in range(4)]
    Hb = [cx.sb("Hb_%d" % g, [128, 128], BF16) for g in range(4)]
    HT = cx.sb("HT", [128, 128], F32)
    HA = [cx.sb("HA_%d" % g, [128, 128], F32) for g in range(4)]
    Gs = [cx.sb("Gs_%d" % i, [128, 8, 128], F32) for i in range(2)]
    pp = cx.ps("pp", [128, 512], F32)
    pmisc = cx.ps("pmisc", [128, 512], F32)
    phb = cx.ps("phb", [128, 512], F32)
    pb3 = cx.ps("pb3", [128, 512], F32)
    ptrb = cx.ps("ptrb", [128, 1024], BF16)
    pb5 = cx.ps("pb5", [128, 512], F32)
    pb6 = cx.ps("pb6", [128, 512], F32)
    pb7 = cx.ps("pb7", [128, 512], F32)
    ph = phb[:, 0:128]
    py = phb[:, 128:256]
    GB = [(pp, pmisc, pb5), (pb6, pb7, pb3)]
    GBN = [('pp', 'pmisc', 'pb5'), ('pb6', 'pb7', 'pb3')]
    S.set_alias({'pp': 'B0', 'pmisc': 'B1', 'ph': 'B2', 'pb3': 'B3', 'py': 'B2', 'ptr': 'B4', 'pb5': 'B5', 'pb6': 'B6', 'pb7': 'B7'})

    m2 = cm32[:, 0:256]
    m3 = cm32[:, 128:512]
    blk = cm32[:, 512:640]
    seg = cm32[:, 640:1152]
    ident = cm16[:, 0:128]
    onesb = cm16[:, 128:256]
    ones64 = cm16[:, 256:384]

    for c in range(8):
        S.dma_in('pool', win[:, c, :], D['w_in'][c * 128:(c + 1) * 128, :], 'win')
    S.dma_in('sp', c12[:], D['c12'], 'c12')
    S.dma_in('sp', cm32[:], D['cm32'], 'cm32')
    S.dma_in('pool', cm16[:], D['cm16'], ['ident', 'cm16'])
    S.dma_in('pool', lw[:], D['lw'], 'lw')
    S.dma_in('pool', gu[:], D['gu'], 'gu')
    S.dma_in('sp', bfc[:, 0:1], D['bf'], 'bfc')
    S.small = True
    ts(S, 'dve', bfc[:, 1:2], bfc[:, 0:1], -1.0, None, ALU.mult, None, 'bfc', 'bfc')
    ts(S, 'dve', om[:], c12[:, 0:14], -1.0, 1.0, ALU.mult, ALU.add, 'c12', 'om')
    ts(S, 'dve', omka[:], c12[:, 26:30], -1.0, 1.0, ALU.mult, ALU.add, 'c12', 'omka')
    mset(S, 'pool', carry[:], 0.0, 'carry')
    mset(S, 'pool', ones8[:], 1.0, 'ones8')
    mset(S, 'pool', fcb[1][:], 0.0, 'fcb1')
    for g in range(4):
        mset(S, 'pool', H32[g][:], 0.0, 'H32_%d' % g)
        mset(S, 'pool', Hb[g][:], 0.0, 'Hb_%d' % g)
    S.small = False
    mu = c12[:, 0:14]
    shc = [0]

    def shift_evac(psum_ap, R, grp, out_ap, pname, oname):
        i = shc[0] % 2; shc[0] += 1
        sh = SH[i]; shn = 'SH%d' % i
        cp(S, 'pool', sh[0:R, 0:1], carry[0:R, grp:grp + 1], 'carry', shn)
        act(S, sh[0:R, 1:513], psum_ap, AF.Copy, [pname, 'c12'], shn, scale=mu[0:R, grp:grp + 1])
        cp(S, 'pool', carry[0:R, grp:grp + 1], sh[0:R, 512:513], shn, 'carry')
        stt(S, 'dve', out_ap, psum_ap, om[0:R, grp:grp + 1], sh[0:R, 0:512], ALU.mult, ALU.add, [pname, shn, 'om'], oname)

    def proj(ps_ap, M, col0, xtile, xname, pname):
        for c in range(8):
            mm(S, ps_ap, win[:, c, col0:col0 + M], xtile[:, c, :], c == 0, c == 7, ['win', xname], pname)

    def tile_prologue(n):
        xs = n % 2
        xtile = xt[xs]; xname = 'xt%d' % xs
        S.dma_in('pool', xtile[:], D['xT'][:, n * 512:(n + 1) * 512].rearrange("(c p) t -> p c t", p=128), xname)
        t0 = n * 512
        fi = 0
        for kind, col0, dst in (('q', O_FQ, 'fq'), ('k', O_FK, 'fk')):
            if kind == 'q' and t0 + 512 <= Y0:
                continue
            for g in range(4):
                proj(pp[:, :], 128, col0 + g * 128, xtile, xname, 'pp')
                fb = fsb[fi % 2]; fn = 'fsb%d' % (fi % 2); fi += 1
                if kind == 'q':
                    act(S, fb[:], pp[:, :], AF.Copy, 'pp', fn, scale=0.125)
                    c0 = max(0, Y0 - t0)
                    tq = t0 + c0 - Y0
                else:
                    cp(S, 'dve', fb[:], pp[:, :], 'pp', fn)
                    c0 = 0
                    tq = t0
                tt(S, 'pool', fsq[:], fb[:], fb[:], ALU.mult, fn, 'fsq')
                mm(S, pmisc[:, :], onesb, fsq[:], True, True, ['cm16', 'fsq'], 'pmisc')
                cp(S, 'act', fnr[:], pmisc[:, :], 'pmisc', 'fnr')
                for h in range(2):
                    S.dma_out('sp', D[dst][2 * g + h, :, tq:tq + 512 - c0], fb[h * 64:(h + 1) * 64, c0:512], fn, '%s_%d' % (dst, n))
                    S.dma_out('sp', D[dst[1] + 'n'][2 * g + h:2 * g + h + 1, tq:tq + 512 - c0], fnr[h * 64:h * 64 + 1, c0:512], 'fnr', '%sn_%d' % (dst, n))
                yield
        for j in range(4):
            for c in range(8):
                mm(S, pp[:, :], xtile[:, c, j * 128:(j + 1) * 128], win[:, c, O_FV:O_FV + 512], c == 0, c == 7, ['win', xname], 'pp')
            fb = fsb[fi % 2]; fn = 'fsb%d' % (fi % 2); fi += 1
            cp(S, 'act', fb[:], pp[:, :], 'pp', fn)
            S.dma_out('sp', D['fv'][t0 + j * 128:t0 + (j + 1) * 128, :], fb[:], fn, 'fv_%d' % n)
            yield
        proj(pmisc[0:8, :], 8, O_FF, xtile, xname, 'pmisc')
        act(S, fct[:], pmisc[0:8, :], AF.Exp, ['pmisc', 'bfc'], 'fct', bias=bfc[:, 1:2], scale=-1.0)
        act(S, fct[:], fct[:], AF.Ln, 'fct', 'fct', bias=1.0)
        cs = n % 2
        S.op('dve', lambda e: e.tensor_tensor_scan(out=fcb[cs][:], data0=ones8[:], data1=fct[:],
                                                   initial=fcb[1 - cs][:, 511:512], op0=ALU.mult, op1=ALU.subtract),
             r=['ones8', 'fct', 'fcb%d' % (1 - cs)], w=['fcb%d' % cs])
        S.dma_out('sp', D['fc'][:, t0:t0 + 512], fcb[cs][:], 'fcb%d' % cs, 'fc_%d' % n)
        yield
        proj(pp[0:64, :], 64, O_WD, xtile, xname, 'pp')
        shift_evac(pp[0:64, :], 64, 12, LT[0:64, :], 'pp', 'LT')
        act(S, LA[0:32, :], LT[0:32, :], AF.Tanh, 'LT', 'LA')
        cp(S, 'pool', LA[32:64, :], LT[32:64, :], 'LT', 'LA')
        yield
        proj(pp[0:96, :], 96, O_GD, xtile, xname, 'pp')
        shift_evac(pp[0:96, :], 96, 13, LT[0:96, :], 'pp', 'LT')
        act(S, SG[:], LT[0:96, :], AF.Sigmoid, 'LT', 'SG')
        yield

    def unit_prep(n, g, ps, u3):
        Pd = dict(P[ps]); sfx = '_%d' % ps
        Pd['PL'] = PL3[u3]; Pd['GF'] = GF3[u3]; Pd['BON'] = BON3[u3]
        s3 = '_t%d' % u3
        xtile = xt[n % 2]; xname = 'xt%d' % (n % 2)
        RS, KS, VS, CL, LD, E1, A, KK, T1 = [w32[k] for k in names32]
        Rt, At, Bt, Kt, Bb, Kb, Vb = [w16d[k][ps] for k in names16d]
        RKb, SQb = w16['RKb'], w16['SQb']
        t0 = n * 512
        gs = slice(g * 128, (g + 1) * 128)
        for (col0, grp, dst, dn) in ((O_R, g, RS, 'RS'), (O_K, 4 + g, KS, 'KS'), (O_V, 8 + g, VS, 'VS')):
            proj(pp[:, :], 128, col0 + g * 128, xtile, xname, 'pp')
            shift_evac(pp[:, :], 128, grp, dst[:], 'pp', dn)
            yield
        mm(S, pmisc[:, :], lw[0:32, gs], LA[0:32, :], True, True, ['lw', 'LA'], 'pmisc')
        act(S, LD[:], pmisc[:, :], AF.Sigmoid, ['pmisc', 'c12'], 'LD', bias=c12[:, 14 + g:15 + g])
        mm(S, pmisc[:, :], lw[32:64, gs], LA[32:64, :], True, True, ['lw', 'LA'], 'pmisc')
        act(S, A[:], pmisc[:, :], AF.Sigmoid, ['pmisc', 'c12'], 'A', bias=c12[:, 18 + g:19 + g])
        mm(S, pmisc[:, :], gu[0:96, gs], SG[0:96, :], True, True, ['gu', 'SG'], 'pmisc')
        cp(S, 'act', Pd['GF'][:], pmisc[:, :], 'pmisc', 'GF' + s3)
        yield
        S.op('dve', lambda e: e.tensor_tensor_scan(out=CL[:], data0=seg, data1=LD[:], initial=0.0, op0=ALU.mult, op1=ALU.add),
             r=['cm32', 'LD'], w=['CL'])
        tt(S, 'pool', LD[:], CL[:], LD[:], ALU.subtract, ['CL', 'LD'], 'LD')
        act(S, E1[:], CL[:], AF.Exp, 'CL', 'E1', scale=-C0)
        act(S, LD[:], LD[:], AF.Exp, 'LD', 'LD', scale=-C0)
        act(S, CL[:], CL[:], AF.Exp, 'CL', 'CL', scale=C0)
        cp(S, 'pool', Pd['PL'][:], E1[:].rearrange("p (c t) -> p c t", t=64)[:, :, 63], 'E1', 'PL' + s3)
        yield
        act(S, KK[:], KS[:], AF.Copy, ['KS', 'c12'], 'KK', scale=c12[:, 22 + g:23 + g])
        tt(S, 'pool', SQb[:], KK[:], KK[:], ALU.mult, 'KK', 'SQb')
        mm(S, pmisc[:, :], onesb, SQb[:], True, True, ['cm16', 'SQb'], 'pmisc')
        act(S, T1[:], pmisc[:, :], AF.Ln, 'pmisc', 'T1', bias=1e-24)
        act(S, T1[:], T1[:], AF.Exp, 'T1', 'T1', scale=-0.5)
        tt(S, 'dve', KK[:], KK[:], T1[:], ALU.mult, ['KK', 'T1'], 'KK')
        yield
        ts(S, 'dve', T1[:], A[:], c12[:, 26 + g:27 + g], omka[:, g:g + 1], ALU.mult, ALU.add, ['A', 'c12', 'omka'], 'T1')
        tt(S, 'pool', KS[:], KS[:], T1[:], ALU.mult, ['KS', 'T1'], 'KS')
        stt(S, 'dve', RKb[:], RS[:], c12[:, 30 + g:31 + g], KS[:], ALU.mult, ALU.mult, ['RS', 'KS', 'c12'], 'RKb')
        mm(S, pmisc[:, :], onesb, RKb[:], True, True, ['cm16', 'RKb'], 'pmisc')
        tt(S, 'dve', Pd['BON'][:], pmisc[:, :], VS[:], ALU.mult, ['pmisc', 'VS'], 'BON' + s3)
        yield
        tt(S, 'pool', A[:], KK[:], A[:], ALU.mult, ['KK', 'A'], 'A')
        tt(S, 'dve', Rt[:], RS[:], E1[:], ALU.mult, ['RS', 'E1'], 'Rt' + str(ps))
        stt(S, 'dve', At[:], KK[:], -1.0, LD[:], ALU.mult, ALU.mult, ['KK', 'LD'], 'At' + str(ps))
        tt(S, 'dve', Bt[:], A[:], CL[:], ALU.mult, ['A', 'CL'], 'Bt' + str(ps))
        tt(S, 'pool', Kt[:], KS[:], CL[:], ALU.mult, ['KS', 'CL'], 'Kt' + str(ps))
        yield
        tt(S, 'dve', E1[:].rearrange("p (c t) -> p c t", t=64), CL[:].rearrange("p (c t) -> p c t", t=64),
           Pd['PL'][:, :].unsqueeze(2).to_broadcast([128, 8, 64]), ALU.mult, ['CL', 'PL' + s3], 'E1')
        tt(S, 'pool', Bb[:], A[:], E1[:], ALU.mult, ['A', 'E1'], 'Bb' + str(ps))
        tt(S, 'dve', Kb[:], KS[:], E1[:], ALU.mult, ['KS', 'E1'], 'Kb' + str(ps))
        cp(S, 'pool', Vb[:], VS[:], 'VS', 'Vb' + str(ps))

    def unit_stages(n, g, ps, u3):
        Pd = dict(P[ps]); sfx = '_%d' % ps
        Pd['PL'] = PL3[u3]; Pd['GF'] = GF3[u3]; Pd['BON'] = BON3[u3]
        Rt, At, Bt, Kt, Bb, Kb, Vb = [w16d[k][ps] for k in names16d]
        RtN, AtN, BtN, KtN, BbN, KbN, VbN = [k + str(ps) for k in names16d]
        t0 = n * 512
        TM4, X6 = Pd['TM4'], Pd['X6']
        if STOP <= 3:
            return
        need_g = [(t0 + (2 * gi + 1) * 128) >= Y0 for gi in range(2)]

        def chains(gi):
            for h in range(2):
                for pl in range(2):
                    yield h * 2 + pl, 2 * gi + pl, h

        def stage_a(gi):
            ba, bb_, bc = GB[gi]; an_, bn_, cn_ = GBN[gi]
            G = GS[gi]; gn_ = '_g%d' % gi
            for pl in range(2):
                p = 2 * gi + pl
                pc = slice(p * 128, (p + 1) * 128)
                for i, (src, sn) in enumerate(((At, AtN), (Vb, VbN), (Bb, BbN), (Kb, KbN))):
                    tr(S, ptrb[:, pl * 512 + i * 128:pl * 512 + (i + 1) * 128], src[:, pc], ident, sn, 'ptr')
            cp(S, 'act', TM4[:, 2 * gi:2 * gi + 2, :, :], ptrb[:, :].rearrange("q (a i t) -> q a i t", a=2, i=4), 'ptr', 'TM4' + sfx)
            STGA = float(os.environ.get('P12_STAGE', '9'))
            if STGA <= 0.2:
                return
            for ch, p, h in chains(gi):
                hs = slice(h * 64, (h + 1) * 64); pc = slice(p * 128, (p + 1) * 128)
                bk, bkn = (ba, an_) if ch < 2 else (bb_, bn_)
                off = (ch % 2) * 256
                mm(S, bk[:, off:off + 128], At[hs, pc], Bt[hs, pc], True, True, [AtN, BtN], bkn)
                mm(S, bk[:, off + 128:off + 256], Bt[hs, pc], At[hs, pc], True, True, [AtN, BtN], bkn)
            EV = int(os.environ.get('P12_EV', '1'))
            if EV == 1:
                for c_ in range(2):
                    tt(S, 'dve', G['NN'][0][:, c_].rearrange("q a t -> q (a t)"), ba[:, c_ * 256:(c_ + 1) * 256], m2, ALU.mult, [an_, 'cm32'], 'NN0' + gn_)
                    tt(S, 'dve', G['NN'][0][:, 2 + c_].rearrange("q a t -> q (a t)"), bb_[:, c_ * 256:(c_ + 1) * 256], m2, ALU.mult, [bn_, 'cm32'], 'NN0' + gn_)
            ntyp = 3 if need_g[gi] else 1
            if STGA <= 0.4:
                return
            for ch, p, h in chains(gi):
                hs = slice(h * 64, (h + 1) * 64); pc = slice(p * 128, (p + 1) * 128)
                cc = slice(ch * 128, (ch + 1) * 128)
                if need_g[gi]:
                    mm(S, ba[:, cc], Kt[hs, pc], At[hs, pc], True, True, [KtN, AtN], an_)
                    mm(S, bb_[:, cc], Bt[hs, pc], Rt[hs, pc], True, True, [BtN, RtN], bn_)
                    mm(S, bc[:, cc], Kt[hs, pc], Rt[hs, pc], True, True, [KtN, RtN], cn_)
                else:
                    bk, bkn = (ba, an_) if h == 0 else (bb_, bn_)
                    mm(S, bk[:, cc], Kt[hs, pc], At[hs, pc], True, True, [KtN, AtN], bkn)
            if need_g[gi]:
                for k_, (bk, bkn) in enumerate(((ba, an_), (bb_, bn_), (bc, cn_))):
                    mk = m3[:, 0:128] if k_ == 0 else m3[:, 128:256]
                    tt(S, 'dve', G['A3'][:, k_, :, :], bk[:, :].rearrange("q (c t) -> q c t", c=4), mk.unsqueeze(1).to_broadcast([128, 4, 128]),
                       ALU.mult, [bkn, 'cm32'], 'A3' + gn_)
            else:
                for h_, (bk, bkn) in enumerate(((ba, an_), (bb_, bn_))):
                    tt(S, 'dve', G['A3'][:, 0, 2 * h_:2 * h_ + 2, :], bk[:, h_ * 256:(h_ + 1) * 256].rearrange("q (c t) -> q c t", c=2),
                       m3[:, 0:128].unsqueeze(1).to_broadcast([128, 2, 128]), ALU.mult, [bkn, 'cm32'], 'A3' + gn_)
            if STGA <= 0.6:
                return
            for ch, p, h in chains(gi):
                hs = slice(h * 64, (h + 1) * 64)
                mm(S, ba[:, ch * 128 + 64:(ch + 1) * 128], G['A3'][:, 0, ch, :], TM4[:, p, 1, hs], True, True, ['A3' + gn_, 'TM4' + sfx], an_)
            cp(S, 'act', G['X'][0][:, :, 64:128], ba[:, :].rearrange("q (c t) -> q c t", c=4)[:, :, 64:128], an_, 'X0' + gn_)
            cp(S, 'dve', G['X'][0][:, :, 0:64].rearrange("q (h a) c -> q a h c", h=2),
               TM4[:, 2 * gi:2 * gi + 2, 0, :].rearrange("q a (h c) -> q a h c", h=2), 'TM4' + sfx, 'X0' + gn_)

        def stage_b(gi, lvl):
            ba, bb_, bc = GB[gi]; an_, bn_, cn_ = GBN[gi]
            G = GS[gi]; gn_ = '_g%d' % gi
            cur = lvl % 2
            nn = G['NN'][cur]; nnn = 'NN%d%s' % (cur, gn_)
            xc = G['X'][cur]; xcn = 'X%d%s' % (cur, gn_)
            for ch, p, h in chains(gi):
                cc = slice(ch * 128, (ch + 1) * 128)
                mm(S, ba[:, cc], ident, xc[:, ch, :], True, False, [xcn, 'ident'], an_)
                mm(S, ba[:, cc], nn[:, ch, 1, :], xc[:, ch, :], False, True, [xcn, nnn], an_)
            if lvl < 5:
                cp(S, 'act', G['X'][1 - cur][:, :, :], ba[:, :].rearrange("q (c t) -> q c t", c=4), an_, 'X%d%s' % (1 - cur, gn_))
                for ch, p, h in chains(gi):
                    bk, bkn = (bb_, bn_) if ch < 2 else (bc, cn_)
                    off = (ch % 2) * 256
                    mm(S, bk[:, off:off + 128], nn[:, ch, 1, :], nn[:, ch, 0, :], True, True, nnn, bkn)
                    mm(S, bk[:, off + 128:off + 256], nn[:, ch, 0, :], nn[:, ch, 1, :], True, True, nnn, bkn)
                nx = G['NN'][1 - cur]; nxn = 'NN%d%s' % (1 - cur, gn_)
                cp(S, 'dve', nx[:, 0:2].rearrange("q c a t -> q c (a t)"), bb_[:, :].rearrange("q (c x) -> q c x", c=2), bn_, nxn)
                cp(S, 'dve', nx[:, 2:4].rearrange("q c a t -> q c (a t)"), bc[:, :].rearrange("q (c x) -> q c x", c=2), cn_, nxn)
            else:
                for pl in range(2):
                    p = 2 * gi + pl
                    cp(S, 'act', X6[:, p, :, :].rearrange("q k (h c) -> q h k c", h=2),
                       ba[:, :].rearrange("q (h a k c) -> q h a k c", h=2, a=2, k=2)[:, :, pl, :, :], an_, 'X6' + sfx)

        def stage_c(gi):
            ba, bb_, bc = GB[gi]; an_, bn_, cn_ = GBN[gi]
            G = GS[gi]; gn_ = '_g%d' % gi
            gc = slice(gi * 256, (gi + 1) * 256)
            if need_g[gi]:
                for ch, p, h in chains(gi):
                    hs = slice(h * 64, (h + 1) * 64)
                    pl = p - 2 * gi
                    mm(S, ba[hs, pl * 128:(pl + 1) * 128], X6[:, p, 0, hs], G['A3'][:, 1, ch, :], True, True, ['X6' + sfx, 'A3' + gn_], an_, tp=(0, h * 64))
                    mm(S, ba[hs, 256 + pl * 128:256 + (pl + 1) * 128], X6[:, p, 1, hs], G['A3'][:, 1, ch, :], True, False,
                       ['X6' + sfx, 'A3' + gn_], an_, tp=(0, h * 64))
                    mm(S, ba[hs, 256 + pl * 128:256 + (pl + 1) * 128], TM4[:, p, 1, hs], G['A3'][:, 2, ch, :], False, True,
                       ['TM4' + sfx, 'A3' + gn_], an_, tp=(0, h * 64))
                tt(S, 'dve', Pd['QT'][:, gc], ba[:, 0:256], Rt[:, gc], ALU.add, [an_, RtN], 'QT' + sfx)
                cp(S, 'act', Pd['Y0T'][:, gc], ba[:, 256:512], an_, 'Y0T' + sfx)
            for c2 in range(2):
                bk, bkn = (bb_, bn_) if c2 == 0 else (bc, cn_)
                cs_ = slice(c2 * 64, (c2 + 1) * 64)
                for pl in range(2):
                    p = 2 * gi + pl
                    mm(S, bk[:, pl * 128:(pl + 1) * 128], X6[cs_, p, 0, :], TM4[cs_, p, 2, :], True, True, ['X6' + sfx, 'TM4' + sfx], bkn)
            blk2 = blk.unsqueeze(1).to_broadcast([128, 2, 128])
            for c2 in range(2):
                bk, bkn = (bb_, bn_) if c2 == 0 else (bc, cn_)
                tt(S, 'dve', Pd['MTs'][:, 4 * gi:4 * gi + 4, :].rearrange("q (a c) t -> q a c t", c=2)[:, :, c2, :],
                   bk[:, 0:256].rearrange("q (a t) -> q a t", a=2), blk2, ALU.mult, [bkn, 'cm32'], 'MTs' + sfx)
            for c2 in range(2):
                bk, bkn = (bb_, bn_) if c2 == 0 else (bc, cn_)
                cs_ = slice(c2 * 64, (c2 + 1) * 64)
                for pl in range(2):
                    p = 2 * gi + pl
                    mm(S, bk[:, pl * 128:(pl + 1) * 128], TM4[cs_, p, 2, :], X6[cs_, p, 1, :], True, False, ['TM4' + sfx, 'X6' + sfx], bkn)
                    mm(S, bk[:, pl * 128:(pl + 1) * 128], TM4[cs_, p, 3, :], TM4[cs_, p, 1, :], False, True, ['TM4' + sfx], bkn)
            for c2 in range(2):
                bk, bkn = (bb_, bn_) if c2 == 0 else (bc, cn_)
                tt(S, 'dve', Gs[ps][:, 4 * gi:4 * gi + 4, :].rearrange("q (a c) t -> q a c t", c=2)[:, :, c2, :],
                   bk[:, 0:256].rearrange("q (a t) -> q a t", a=2), blk2, ALU.mult, [bkn, 'cm32'], 'Gs' + sfx)

        STG = float(os.environ.get('P12_STAGE', '9'))
        stage_a(0)
        yield
        stage_a(1)
        yield
        for lvl in range(6 if STG > 1 else 0):
            stage_b(0, lvl)
            stage_b(1, lvl)
            yield
        if STG > 2:
            stage_c(0)
            stage_c(1)

    def unit_serial(n, g, ps, u3):
        Pd = dict(P[ps]); sfx = '_%d' % ps
        Pd['PL'] = PL3[u3]; Pd['GF'] = GF3[u3]; Pd['BON'] = BON3[u3]
        s3 = '_t%d' % u3
        TM4, X6 = Pd['TM4'], Pd['X6']
        t0 = n * 512
        hn = 'Hb_%d' % g; h32n = 'H32_%d' % g
        any_y = False
        for p in range(4):
            pc = slice(p * 128, (p + 1) * 128)
            need_y = (t0 + p * 128) >= Y0
            for c2 in range(2):
                cs_ = slice(c2 * 64, (c2 + 1) * 64)
                ci = p * 2 + c2
                if need_y:
                    mm(S, py[:, c2 * 64:(c2 + 1) * 64], Hb[g][:], Pd['QT'][:, ci * 64:(ci + 1) * 64], True, True, [hn, 'QT' + sfx], 'py')
                S.small = True
                stt(S, 'dve', HA[g][:], H32[g][:], Pd['PL'][:, ci:ci + 1], Gs[ps][:, ci, :], ALU.mult, ALU.add, [h32n, 'PL' + s3, 'Gs' + sfx], 'HA%d' % g)
                mm(S, ph, Pd['MTs'][:, ci, :], Hb[g][:], True, True, ['MTs' + sfx, hn], 'ph')
                tt(S, 'dve', Hb[g][:], HA[g][:], ph, ALU.add, ['HA%d' % g, 'ph'], hn)
                tt(S, 'dve', H32[g][:], HA[g][:], ph, ALU.add, ['HA%d' % g, 'ph'], h32n)
                S.small = False
                if c2 == 1 and need_y:
                    tt(S, 'dve', Pd['YF'][:, pc], py, Pd['Y0T'][:, pc], ALU.add, ['py', 'Y0T' + sfx], 'YF' + sfx)
                    any_y = True
                yield
        if any_y:
            c0 = max(0, Y0 - t0)
            cc = slice(c0, 512)
            YF, G1, G2, YO = Pd['YF'], Pd['G1'], Pd['G2'], Pd['YO']
            cp(S, 'act', G2[:, cc], YF[:, cc], 'YF' + sfx, 'G2' + sfx)
            mm(S, pmisc[:, cc], ones64, G2[:, cc], True, True, ['cm16', 'G2' + sfx], 'pmisc')
            tt(S, 'dve', YF[:, cc], YF[:, cc], pmisc[:, cc], ALU.subtract, ['YF' + sfx, 'pmisc'], 'YF' + sfx)
            tt(S, 'pool', G2[:, cc], YF[:, cc], YF[:, cc], ALU.mult, 'YF' + sfx, 'G2' + sfx)
            mm(S, pmisc[:, cc], ones64, G2[:, cc], True, True, ['cm16', 'G2' + sfx], 'pmisc')
            act(S, G1[:, cc], pmisc[:, cc], AF.Ln, 'pmisc', 'G1' + sfx, bias=64e-5)
            act(S, G1[:, cc], G1[:, cc], AF.Exp, 'G1' + sfx, 'G1' + sfx, scale=-0.5)
            tt(S, 'dve', YF[:, cc], YF[:, cc], G1[:, cc], ALU.mult, ['YF' + sfx, 'G1' + sfx], 'YF' + sfx)
            ts(S, 'pool', YF[:, cc], YF[:, cc], c12[:, 34 + g:35 + g], c12[:, 38 + g:39 + g], ALU.mult, ALU.add, ['YF' + sfx, 'c12'], 'YF' + sfx)
            tt(S, 'pool', YF[:, cc], YF[:, cc], Pd['BON'][:, cc], ALU.add, ['YF' + sfx, 'BON' + s3], 'YF' + sfx)
            tt(S, 'dve', YO[:, cc], YF[:, cc], Pd['GF'][:, cc], ALU.mult, ['YF' + sfx, 'GF' + s3], 'YO' + sfx)
            S.dma_out('sp', D['ymix'][g * 128:(g + 1) * 128, t0 + c0 - Y0:t0 + 512 - Y0], YO[:, cc], 'YO' + sfx, 'ymix_%d_%d' % (n, g))

    conv = []
    if 'moe_wg16' in D:
        for e_ in range(8):
            for fc in range(28):
                conv.append((D['moe_wg16'][e_, fc], D['moe_wg'][e_, fc]))
                conv.append((D['moe_wu16'][e_, fc], D['moe_wu'][e_, fc]))
            for fo in range(8):
                conv.append((D['moe_wd16'][e_, fo], D['moe_wd'][e_, fo]))
    per_tile = (len(conv) + NT - 1) // NT
    cvi = [0]
    units = [(n, g) for n in range(NT) for g in range(4)]

    def prep_gen(ui):
        n, g = units[ui]
        if g == 0:
            for _ in tile_prologue(n):
                yield
        for _ in unit_prep(n, g, ui % 2, ui % 3):
            yield

    def step(gen, k):
        if gen is None:
            return None
        for _ in range(k):
            try:
                next(gen)
            except StopIteration:
                return None
        return gen

    KPREP = 3
    prev = None
    pr = prep_gen(0)
    for _ in pr:
        pass
    for ui, (n, g) in enumerate(units):
        stg = unit_stages(n, g, ui % 2, ui % 3)
        pr = prep_gen(ui + 1) if ui + 1 < len(units) else None
        for _ in stg:
            prev = step(prev, 1)
            pr = step(pr, KPREP)
        if prev is not None:
            for _ in prev:
                pass
        if pr is not None:
            for _ in pr:
                pass
        prev = unit_serial(n, g, ui % 2, ui % 3)
    if prev is not None:
        for _ in prev:
            pass
    S.flush()
    cx.close()


def _masks():
    t = np.arange(128)
    same = (t[:, None] // 64) == (t[None, :] // 64)
    m_sl = (same & (t[None, :] < t[:, None])).astype(np.float32)
    m_su = m_sl.T.copy()
    m_u = (same & (t[:, None] <= t[None, :])).astype(np.float32)
    blk = same.astype(np.float32)
    seg = np.ones((128, 512), np.float32); seg[:, ::64] = 0.0
    cm32 = np.concatenate([m_sl, m_su, m_u, m_u, blk, seg], 1)
    cm16 = np.concatenate([np.eye(128, dtype=np.float32), blk, blk / 64.0], 1)
    return np.ascontiguousarray(cm32), np.ascontiguousarray(cm16)


def _c12(inp):
    c = np.zeros((128, 48), np.float32)
    mu = inp["rwkv_mu"][0]
    c[:, 0:12] = mu[0:1536].reshape(12, 128).T
    c[0:64, 12] = mu[1536:1600]
    c[0:96, 13] = mu[1600:1696]
    c[:, 14:18] = inp["rwkv_w0"][0].reshape(4, 128).T
    c[:, 18:22] = inp["rwkv_a0"][0].reshape(4, 128).T
    for i, k in enumerate(["rwkv_k_k", "rwkv_k_a", "rwkv_r_k", "rwkv_gn_g", "rwkv_gn_b"]):
        c[:, 22 + 4 * i:26 + 4 * i] = inp[k][0].reshape(4, 128).T
    return c


def phase3(S, nc, W, D):
    OWN = W // 2; OT = OWN + 128; Y0 = OWN - 128; NKB = W // 128
    cx = Ctx(nc)
    vall = cx.sb("vall", [128, NKB, 512], BF16)
    ka = [cx.sb("ka%d" % i, [128, W], BF16) for i in range(2)]
    qa = [cx.sb("qa%d" % i, [128, OT], BF16) for i in range(2)]
    va = [cx.sb("va%d" % i, [128, NKB, 128], BF16) for i in range(2)]
    PT = [cx.sb("PT%d" % i, [128, 512], BF16) for i in range(4)]
    tri = cx.sb("tri", [128, 128], BF16)
    ones1 = cx.sb("ones1", [65, 64], F32)
    rinv = cx.sb("rinv", [65, 512], F32)
    osb = cx.sb("osb", [64, 512], F32)
    yo = [cx.sb("yo%d" % i, [64, 512], BF16) for i in range(2)]
    CW = min(1024, W)
    CF = cx.sb("CF", [8, CW], F32)
    R1 = cx.sb("R1", [8, CW], F32)
    C16 = cx.sb("C16", [8, 6, CW], BF16)
    KN = cx.sb("KN", [8, CW], F32)
    kmx = cx.sb("kmx", [8, 16], F32)
    kms = cx.sb("kms", [8, 2], F32)
    QW = OT // 4 if OT % 4 == 0 and OT > 1056 else OT
    QN = cx.sb("QN", [8, QW], F32)
    NMb = cx.sb("NMb", [8, QW], BF16)
    st = [cx.ps("st%d" % i, [128, 512], F32) for i in range(4)]
    accb = [cx.ps("acc%d" % i, [128, 512], F32) for i in range(2)]
    pbc = cx.ps("pbc", [128, 512], F32)
    S.set_alias({'st0': 'B0', 'st1': 'B1', 'st2': 'B2', 'st3': 'B3', 'acc0': 'B4', 'acc1': 'B5', 'pbc': 'B6'})
    S.small = True
    S.dma_in('pool', tri[:], D['ctri'], 'tri')
    mset(S, 'dve', ones1[:], 1.0, 'ones1')
    mset(S, 'dve', kmx[:], 0.0, 'kmx')
    for i in range(2):
        mset(S, 'pool', va[i][:, :, 65:128], 0.0, 'va%d' % i)
        mset(S, 'pool', va[i][:, :, 64:65], 1.0, 'va%d' % i)
        mset(S, 'dve', ka[i][64:128, :], 0.0, 'ka%d' % i)
        mset(S, 'dve', qa[i][64:128, :], 1.0, 'qa%d' % i)
    nch = W // CW
    for ch in range(nch):
        cs_ = slice(ch * CW, (ch + 1) * CW)
        S.dma_in('sp', CF[:], D['fc'][:, cs_], 'CF', 'fc')
        cp(S, 'dve', C16[:, 0, :], CF[:], 'CF', 'C16')
        tt(S, 'dve', R1[:], CF[:], C16[:, 0, :], ALU.subtract, ['CF', 'C16'], 'R1')
        cp(S, 'dve', C16[:, 1, :], R1[:], 'R1', 'C16')
        tt(S, 'dve', R1[:], R1[:], C16[:, 1, :], ALU.subtract, ['R1', 'C16'], 'R1')
        cp(S, 'dve', C16[:, 2, :], R1[:], 'R1', 'C16')
        ts(S, 'dve', C16[:, 3:6, :], C16[:, 0:3, :], -1.0, None, ALU.mult, None, 'C16', 'C16')
        S.dma_out('sp', D['cs'][:, :, cs_], C16[:], 'C16', 'cs')
        S.dma_in('sp', KN[:], D['kn'][:, cs_], 'KN', 'kn')
        S.op('dve', lambda e, ch=ch: e.reduce_max(out=kmx[:, ch:ch + 1], in_=KN[:], axis=AX.X), r=['KN'], w=['kmx'])
    S.op('dve', lambda e: e.reduce_max(out=kms[:, 0:1], in_=kmx[:], axis=AX.X), r=['kmx'], w=['kms'])
    ts(S, 'dve', kms[:, 1:2], kms[:, 0:1], 1.0 / 16.0, None, ALU.mult, None, 'kms', 'kms')
    for qc in range(OT // QW):
        S.dma_in('sp', QN[:], D['qn'][:, qc * QW:(qc + 1) * QW], 'QN', 'qn')
        ts(S, 'dve', NMb[:], QN[:], -4.0, kms[:, 1:2], ALU.mult, ALU.subtract, ['QN', 'kms'], 'NMb')
        S.dma_out('sp', D['negm'][:, qc * QW:(qc + 1) * QW], NMb[:], 'NMb', 'negm')
    S.small = False
    fvv = D['fv'].rearrange("(b p) d -> p b d", p=128)
    nvc = max(1, NKB // 8)
    for i in range(0, NKB, nvc):
        S.dma_in('sp', vall[:, i:i + nvc, :], fvv[:, i:i + nvc, :], 'vall', 'fv')
    qtiles = [(0, 128)] + [(128 + i * 512, 512) for i in range(OWN // 512)]
    import os
    STOP3 = float(os.environ.get('P3_STOP', '99'))
    it = [0]
    ti = 0
    conv = []
    if 'moe_wg16' in D:
        for e_ in range(8):
            for fc in range(28):
                conv.append((D['moe_wg16'][e_, fc], D['moe_wg'][e_, fc]))
                conv.append((D['moe_wu16'][e_, fc], D['moe_wu'][e_, fc]))
            for fo in range(8):
                conv.append((D['moe_wd16'][e_, fo], D['moe_wd'][e_, fo]))
    cvi = [0]
    n_items = 8 * sum((Y0 + q0 + nq) // 128 for (q0, nq) in qtiles)
    cv_every = max(1, n_items // (len(conv) + 1)) if conv else 1
    for h in range(8 if STOP3 > 1 else 0):
        sl = h % 2
        kan, qan, van = 'ka%d' % sl, 'qa%d' % sl, 'va%d' % sl
        V3 = int(os.environ.get('P3_VAR', '31'))
        if V3 & 1:
            mset(S, 'dve', ka[sl][64:72, :], 1.0, kan)
            mset(S, 'dve', qa[sl][64:72, :], 1.0, qan)
        if V3 & 2:
            S.dma_in('sp', ka[sl][0:64, :], D['fk'][h], kan, 'fk')
            S.dma_in('sp', qa[sl][0:64, :], D['fq'][h], qan, 'fq')
        if V3 & 4:
            S.dma_in('sp', ka[sl][67:70, :], D['cs'][h, 3:6, :], kan, 'cs')
            S.dma_in('sp', qa[sl][64:67, :], D['cs'][h, 0:3, Y0:W], qan, 'cs')
            S.dma_in('sp', qa[sl][70:71, :], D['negm'][h:h + 1, :], qan, 'negm')
        if V3 & 8:
            S.dma_in('pool', ka[sl][71:72, :], D['kmask'], kan)
        if V3 & 16:
            cp(S, 'pool', va[sl][:, :, 0:64], vall[:, :, h * 64:(h + 1) * 64], 'vall', van)
        items = []
        for (q0, nq) in (qtiles if STOP3 > 2 else []):
            kb_end = (Y0 + q0 + nq) // 128
            kb_d0 = (Y0 + q0) // 128
            tix = ti; ti += 1
            for kb in range(kb_end):
                items.append((q0, nq, kb, kb_end, kb_d0, tix))
        LOOK = 3
        pend = []

        def emit_scores(item):
            q0, nq, kb, kb_end, kb_d0, tix = item
            if it[0] % cv_every == 0 and cvi[0] < len(conv):
                S.dma_in('pool', conv[cvi[0]][0], conv[cvi[0]][1], 'cv%d' % (cvi[0] % 8))
                cvi[0] += 1
            j = kb - kb_d0
            c0 = max(j, 0) * 128
            i4 = it[0] % 4; it[0] += 1
            stb = st[i4]; ptb = PT[i4]; sn = 'st%d' % i4; pn = 'PT%d' % i4
            mm(S, stb[:, c0:nq], ka[sl][0:128, kb * 128:(kb + 1) * 128], qa[sl][0:128, q0 + c0:q0 + nq], True, True, [kan, qan], sn)
            act(S, ptb[:, c0:nq], stb[:, c0:nq], AF.Exp, sn, pn)
            if j >= 0:
                tt(S, 'pool', ptb[:, c0:c0 + 128], ptb[:, c0:c0 + 128], tri[:], ALU.mult, [pn, 'tri'], pn)
            return (ptb, pn, c0)

        def emit_pv(item, sc_):
            q0, nq, kb, kb_end, kb_d0, tix = item
            ptb, pn, c0 = sc_
            an = 'acc%d' % (tix % 2)
            acc = accb[tix % 2]
            mm(S, acc[0:128, c0:nq], va[sl][:, kb, :], ptb[:, c0:nq], kb == 0, kb == kb_end - 1, [van, pn], an)
            if kb == kb_end - 1:
                S.small = True
                ts(S, 'dve', rinv[64:65, 0:nq], acc[64:65, 0:nq], 1e-30, None, ALU.add, None, an, 'rinv')
                S.op('dve', lambda e, nq=nq: e.reciprocal(out=rinv[64:65, 0:nq], in_=rinv[64:65, 0:nq]), r=['rinv'], w=['rinv'])
                S.small = False
                mm(S, pbc[0:64, 0:nq], ones1[64:65, 0:64], rinv[64:65, 0:nq], True, True, ['ones1', 'rinv'], 'pbc')
                cp(S, 'act', osb[:, 0:nq], acc[0:64, 0:nq], an, 'osb')
                yb = yo[tix % 2]; yn = 'yo%d' % (tix % 2)
                tt(S, 'dve', yb[:, 0:nq], osb[:, 0:nq], pbc[0:64, 0:nq], ALU.mult, ['osb', 'pbc'], yn)
                S.dma_out('sp', D['ymix'][512 + h * 64:512 + (h + 1) * 64, q0:q0 + nq], yb[:, 0:nq], yn, 'ymixf_%d_%d' % (h, q0))

        for idx, item in enumerate(items):
            pend.append((item, emit_scores(item)))
            if len(pend) > LOOK:
                i0, s0 = pend.pop(0)
                emit_pv(i0, s0)
        for i0, s0 in pend:
            emit_pv(i0, s0)
    while cvi[0] < len(conv):
        S.dma_in('pool', conv[cvi[0]][0], conv[cvi[0]][1], 'cv%d' % (cvi[0] % 8))
        cvi[0] += 1
    S.flush()
    cx.close()


ALPHA = 4.0 ** 0.25
L_MIXG, L_MIXB, L_FFNG, L_FFNB, L_BPW1, L_BDW, L_CLG, L_CLB, L_BPW2, L_CPG, L_CPB, L_MOEG, L_MOEB, L_WDW = \
    0, 8, 16, 24, 32, 48, 56, 64, 72, 80, 88, 96, 104, 112
CLN_COLS = 112 + 31 * 8


def ln_fm_gen(S, x, xn, NQ, g, b, T16, mean, var, ones_k, psA, psB, out16, o16n, eps=1e-5, silu=False):
    cp(S, 'act', T16, x, xn, 'T16')
    for c in range(8):
        mm(S, psA, ones_k, T16[:, c, :], c == 0, c == 7, ['ones_k', 'T16'], 'psA')
    yield
    cp(S, 'act', mean, psA, 'psA', 'mean')
    tt(S, 'pool', T16, x, x, ALU.mult, xn, 'T16')
    for c in range(8):
        mm(S, psB, ones_k, T16[:, c, :], c == 0, c == 7, ['ones_k', 'T16'], 'psB')
    yield
    tt(S, 'pool', var, mean, mean, ALU.mult, 'mean', 'var')
    tt(S, 'dve', var, psB, var, ALU.subtract, ['psB', 'var'], 'var')
    act(S, var, var, AF.Ln, 'var', 'var', bias=eps)
    act(S, var, var, AF.Exp, 'var', 'var', scale=-0.5)
    yield
    mb = mean.unsqueeze(1).to_broadcast([128, 8, NQ])
    vb = var.unsqueeze(1).to_broadcast([128, 8, NQ])
    tt(S, 'dve', x, x, mb, ALU.subtract, [xn, 'mean'], xn)
    tt(S, 'pool', x, x, vb, ALU.mult, [xn, 'var'], xn)
    yield
    for c in range(8):
        if silu:
            ts(S, 'dve', x[:, c, :], x[:, c, :], g[:, c:c + 1], b[:, c:c + 1], ALU.mult, ALU.add, [xn, 'cln'], xn)
            act(S, out16[:, c, :], x[:, c, :], AF.Silu, xn, o16n)
        else:
            act(S, out16[:, c, :], x[:, c, :], AF.Identity, [xn, 'cln'], o16n, bias=b[:, c:c + 1], scale=g[:, c:c + 1])
            ts(S, 'dve', x[:, c, :], x[:, c, :], g[:, c:c + 1], b[:, c:c + 1], ALU.mult, ALU.add, [xn, 'cln'], xn)
        if c % 2 == 1:
            yield


def ln_fm(*a, **k):
    for _ in ln_fm_gen(*a, **k):
        pass


def gstep(gen, k=1):
    if gen is None:
        return None
    for _ in range(k):
        try:
            next(gen)
        except StopIteration:
            return None
    return gen


def gdrain(gen):
    if gen is not None:
        for _ in gen:
            pass
    return None


def tiles_of(total, size):
    out = []
    t = 0
    while t < total:
        n = min(size, total - t)
        out.append((t, n)); t += n
    return out


def load_w_bf16(S, dst, src_ap, name, kchunks):
    for c in range(kchunks):
        S.dma_in('pool', dst[:, c, :], src_ap[c * 128:(c + 1) * 128, :], name)


def phase4a(S, nc, W, D):
    OWN = W // 2; OT = OWN + 128; Y0 = OWN - 128
    cx = Ctx(nc)
    wo = cx.sb("wo", [128, 8, 1024], BF16)
    cln = cx.sb("cln", [128, CLN_COLS], F32)
    ones_k = cx.sb("ones_k", [128, 128], BF16)
    ym = [cx.sb("ym%d" % i, [128, 8, 512], BF16) for i in range(2)]
    xr = [cx.sb("xr%d" % i, [128, 8, 512], F32) for i in range(2)]
    xo = [cx.sb("xo%d" % i, [128, 8, 512], BF16) for i in range(2)]
    T16 = cx.sb("T16", [128, 8, 512], BF16)
    mean = cx.sb("mean", [128, 512], F32)
    var = cx.sb("var", [128, 512], F32)
    pm = [cx.ps("pm%d" % i, [128, 512], F32) for i in range(4)]
    psA = cx.ps("psA", [128, 512], F32)
    psB = cx.ps("psB", [128, 512], F32)
    S.set_alias({'pm0': 'B0', 'pm1': 'B1', 'pm2': 'B2', 'pm3': 'B3', 'psA': 'B4', 'psB': 'B5'})
    load_w_bf16(S, wo, D['w_out'], 'wo', 8)
    S.dma_in('sp', cln[:], D['cln'], 'cln')
    mset(S, 'dve', ones_k[:], 1.0 / 1024.0, 'ones_k')
    k = 0
    for ti, (t0, NQ) in enumerate(tiles_of(OT, 512)):
        s = ti % 2
        ymn, xrn, xon = 'ym%d' % s, 'xr%d' % s, 'xo%d' % s
        S.dma_in('sp', ym[s][:, :, 0:NQ], D['ymix'][:, t0:t0 + NQ].rearrange("(c p) t -> p c t", p=128), ymn, 'ymix')
        S.dma_in('sp', xr[s][:, :, 0:NQ], D['xT'][:, Y0 + t0:Y0 + t0 + NQ].rearrange("(c p) t -> p c t", p=128), xrn)
        for fo in range(8):
            pb = pm[k % 4]; pn = 'pm%d' % (k % 4); k += 1
            for c in range(8):
                mm(S, pb[:, 0:NQ], wo[:, c, fo * 128:(fo + 1) * 128], ym[s][:, c, 0:NQ], c == 0, c == 7, ['wo', ymn], pn)
            stt(S, 'dve', xr[s][:, fo, 0:NQ], xr[s][:, fo, 0:NQ], ALPHA, pb[:, 0:NQ], ALU.mult, ALU.add, [xrn, pn], xrn)
        ln_fm(S, xr[s][:, :, 0:NQ], xrn, NQ, cln[:, L_MIXG:L_MIXG + 8], cln[:, L_MIXB:L_MIXB + 8], T16[:, :, 0:NQ], mean[:, 0:NQ], var[:, 0:NQ],
              ones_k[:], psA[:, 0:NQ], psB[:, 0:NQ], xo[s][:, :, 0:NQ], xon)
        S.dma_out('sp', D['xaT'][:, t0:t0 + NQ].rearrange("(c p) t -> p c t", p=128), xr[s][:, :, 0:NQ], xrn, 'xaT_%d' % ti)
        S.dma_out('sp', D['xabT'][:, t0:t0 + NQ].rearrange("(c p) t -> p c t", p=128), xo[s][:, :, 0:NQ], xon, 'xabT_%d' % ti)
    S.flush()
    cx.close()


def phase4b(S, nc, W, D):
    OWN = W // 2; OT = OWN + 128
    import os
    TS = int(os.environ.get('TS_OVR', '384'))
    cx = Ctx(nc)
    wg = cx.sb("wg", [128, 8, 2816], BF16)
    wu = cx.sb("wu", [128, 8, 2816], BF16)
    wd = cx.sb("wd", [128, 22, 1024], BF16)
    cln = cx.sb("cln", [128, CLN_COLS], F32)
    ones_k = cx.sb("ones_k", [128, 128], BF16)
    xr = [cx.sb("xr%d" % i, [128, 8, TS], F32) for i in range(2)]
    xb = [cx.sb("xb%d" % i, [128, 8, TS], BF16) for i in range(2)]
    xo = cx.sb("xo", [128, 8, TS], BF16)
    hT = cx.sb("hT", [128, 22, TS], BF16)
    T16 = cx.sb("T16", [128, 8, TS], BF16)
    sg = [cx.sb("sg%d" % i, [128, TS], F32) for i in range(2)]
    mean = cx.sb("mean", [128, TS], F32)
    var = cx.sb("var", [128, TS], F32)
    pg = [cx.ps("pg%d" % i, [128, 512], F32) for i in range(3)]
    pu = [cx.ps("pu%d" % i, [128, 512], F32) for i in range(3)]
    psA = cx.ps("psA", [128, 512], F32)
    psB = cx.ps("psB", [128, 512], F32)
    S.set_alias({'pg0': 'B0', 'pg1': 'B1', 'pg2': 'B2', 'pu0': 'B3', 'pu1': 'B4', 'pu2': 'B5', 'psA': 'B6', 'psB': 'B7'})
    load_w_bf16(S, wg, D['w_gate'], 'wg', 8)
    load_w_bf16(S, wu, D['w_up'], 'wu', 8)
    load_w_bf16(S, wd, D['w_down'], 'wd', 22)
    S.dma_in('sp', cln[:], D['cln'], 'cln')
    mset(S, 'dve', ones_k[:], 1.0 / 1024.0, 'ones_k')
    k = [0]
    tl = tiles_of(OT, TS)

    def load(ti):
        t0, NQ = tl[ti]; s_ = ti % 2
        S.dma_in('sp', xr[s_][:, :, 0:NQ], D['xaT'][:, t0:t0 + NQ].rearrange("(c p) t -> p c t", p=128), 'xr%d' % s_, 'xaT')
        S.dma_in('sp', xb[s_][:, :, 0:NQ], D['xabT'][:, t0:t0 + NQ].rearrange("(c p) t -> p c t", p=128), 'xb%d' % s_, 'xabT')

    def gate_up(ti):
        t0, NQ = tl[ti]; s_ = ti % 2
        xbn = 'xb%d' % s_
        for fc in range(22):
            i3 = k[0] % 3; k[0] += 1
            gb, ub = pg[i3], pu[i3]; gn, un = 'pg%d' % i3, 'pu%d' % i3
            for c in range(8):
                mm(S, gb[:, 0:NQ], wg[:, c, fc * 128:(fc + 1) * 128], xb[s_][:, c, 0:NQ], c == 0, c == 7, ['wg', xbn], gn)
            for c in range(8):
                mm(S, ub[:, 0:NQ], wu[:, c, fc * 128:(fc + 1) * 128], xb[s_][:, c, 0:NQ], c == 0, c == 7, ['wu', xbn], un)
            sgb = sg[fc % 2]; sgn = 'sg%d' % (fc % 2)
            act(S, sgb[:, 0:NQ], gb[:, 0:NQ], AF.Silu, gn, sgn)
            tt(S, 'dve', hT[:, fc, 0:NQ], sgb[:, 0:NQ], ub[:, 0:NQ], ALU.mult, [sgn, un], 'hT')
            if fc % 2 == 1:
                yield

    def down(ti):
        t0, NQ = tl[ti]; s_ = ti % 2
        xrn = 'xr%d' % s_
        for fo in range(8):
            i3 = k[0] % 3; k[0] += 1
            gb = pg[i3]; gn = 'pg%d' % i3
            for fc in range(22):
                mm(S, gb[:, 0:NQ], wd[:, fc, fo * 128:(fo + 1) * 128], hT[:, fc, 0:NQ], fc == 0, fc == 21, ['wd', 'hT'], gn)
            stt(S, 'dve', xr[s_][:, fo, 0:NQ], xr[s_][:, fo, 0:NQ], ALPHA, gb[:, 0:NQ], ALU.mult, ALU.add, [xrn, gn], xrn)

    def ln_store(ti):
        t0, NQ = tl[ti]; s_ = ti % 2
        xrn = 'xr%d' % s_
        for _ in ln_fm_gen(S, xr[s_][:, :, 0:NQ], xrn, NQ, cln[:, L_FFNG:L_FFNG + 8], cln[:, L_FFNB:L_FFNB + 8], T16[:, :, 0:NQ],
                           mean[:, 0:NQ], var[:, 0:NQ], ones_k[:], psA[:, 0:NQ], psB[:, 0:NQ], xo[:, :, 0:NQ], 'xo'):
            yield
        S.dma_out('sp', D['x1T'][:, t0:t0 + NQ].rearrange("(c p) t -> p c t", p=128), xr[s_][:, :, 0:NQ], xrn, 'x1T_%d' % ti)
        S.dma_out('sp', D['x1bT'][:, t0:t0 + NQ].rearrange("(c p) t -> p c t", p=128), xo[:, :, 0:NQ], 'xo', 'x1bT_%d' % ti)

    load(0)
    lnp = None
    for ti in range(len(tl)):
        for _ in gate_up(ti):
            lnp = gstep(lnp, 1)
        lnp = gdrain(lnp)
        if ti + 1 < len(tl):
            load(ti + 1)
        down(ti)
        lnp = ln_store(ti)
    gdrain(lnp)
    S.flush()
    cx.close()


def _cln(inp):
    c = np.zeros((128, CLN_COLS), np.float32)
    def col8(v): return v.reshape(-1, 128).T
    c[:, L_MIXG:L_MIXG + 8] = col8(inp["mix_ln_g"][0]); c[:, L_MIXB:L_MIXB + 8] = col8(inp["mix_ln_b"][0])
    c[:, L_FFNG:L_FFNG + 8] = col8(inp["ffn_ln_g"][0]); c[:, L_FFNB:L_FFNB + 8] = col8(inp["ffn_ln_b"][0])
    c[:, L_BPW1:L_BPW1 + 16] = col8(inp["conv_b_pw1"][0]); c[:, L_BDW:L_BDW + 8] = col8(inp["conv_b_dw"][0])
    c[:, L_CLG:L_CLG + 8] = col8(inp["conv_ln_g"][0]); c[:, L_CLB:L_CLB + 8] = col8(inp["conv_ln_b"][0])
    c[:, L_BPW2:L_BPW2 + 8] = col8(inp["conv_b_pw2"][0])
    c[:, L_CPG:L_CPG + 8] = col8(inp["conv_post_ln_g"][0]); c[:, L_CPB:L_CPB + 8] = col8(inp["conv_post_ln_b"][0])
    c[:, L_MOEG:L_MOEG + 8] = col8(inp["moe_ln_g"][0]); c[:, L_MOEB:L_MOEB + 8] = col8(inp["moe_ln_b"][0])
    wdw = inp["conv_w_dw"][0]
    for j in range(31):
        c[:, L_WDW + j * 8:L_WDW + (j + 1) * 8] = col8(wdw[j])
    return c


def phase5(S, nc, W, D):
    OWN = W // 2; OT = OWN + 128
    import os
    TS = int(os.environ.get('TS_OVR', '384'))
    cx = Ctx(nc)
    w1 = cx.sb("w1", [128, 8, 2048], BF16)
    w2 = cx.sb("w2", [128, 8, 1024], BF16)
    dg = cx.sb("dg", [128, 31, 8, 128], BF16)
    idb = cx.sb("idb", [128, 128], BF16)
    cln = cx.sb("cln", [128, CLN_COLS], F32)
    hm = cx.sb("hm", [128, 1], F32)
    ones_k = cx.sb("ones_k", [128, 128], BF16)
    hb = [cx.sb("hb%d" % i, [128, 8, 30 + TS], BF16) for i in range(2)]
    xb = [cx.sb("xb%d" % i, [128, 8, TS], BF16) for i in range(2)]
    x1 = [cx.sb("x1_%d" % i, [128, 8, TS], F32) for i in range(3)]
    xc = cx.sb("xc", [128, 8, TS], F32)
    hc = cx.sb("hc", [128, 8, TS], BF16)
    T16 = cx.sb("T16", [128, 8, TS], BF16)
    sg = [cx.sb("sg%d" % i, [128, TS], F32) for i in range(2)]
    mean = cx.sb("mean", [128, TS], F32)
    var = cx.sb("var", [128, TS], F32)
    pv = [cx.ps("pv%d" % i, [128, 512], F32) for i in range(3)]
    pg = [cx.ps("pg%d" % i, [128, 512], F32) for i in range(3)]
    psA = cx.ps("psA", [128, 512], F32)
    psB = cx.ps("psB", [128, 512], F32)
    S.set_alias({'pv0': 'B0', 'pv1': 'B1', 'pv2': 'B2', 'pg0': 'B3', 'pg1': 'B4', 'pg2': 'B5', 'psA': 'B6', 'psB': 'B7'})
    load_w_bf16(S, w1, D['w_pw1'], 'w1', 8)
    load_w_bf16(S, w2, D['w_pw2'], 'w2', 8)
    S.dma_in('pool', idb[:], D['cm16'][:, 0:128], 'idb')
    S.dma_in('sp', cln[:], D['cln'], 'cln')
    S.dma_in('sp', hm[:], D['hmask'], 'hm')
    mset(S, 'dve', ones_k[:], 1.0 / 1024.0, 'ones_k')
    q = 0
    for j in range(31):
        for c in range(8):
            if q % 2 == 0:
                ts(S, 'dve', dg[:, j, c, :], idb[:], cln[:, L_WDW + j * 8 + c:L_WDW + j * 8 + c + 1], None, ALU.mult, None, ['idb', 'cln'], 'dg')
            else:
                act(S, dg[:, j, c, :], idb[:], AF.Copy, ['idb', 'cln'], 'dg', scale=cln[:, L_WDW + j * 8 + c:L_WDW + j * 8 + c + 1])
            q += 1
    k = [0]
    tl = tiles_of(OWN, TS)

    def glu(xbt, xbn, NQ, hbt, hbn):
        for fo in range(8):
            i3 = k[0] % 3; k[0] += 1
            vb, gb = pv[i3], pg[i3]; vn, gn = 'pv%d' % i3, 'pg%d' % i3
            for c in range(8):
                mm(S, vb[:, 0:NQ], w1[:, c, fo * 128:(fo + 1) * 128], xbt[:, c, 0:NQ], c == 0, c == 7, ['w1', xbn], vn)
            for c in range(8):
                mm(S, gb[:, 0:NQ], w1[:, c, 1024 + fo * 128:1024 + (fo + 1) * 128], xbt[:, c, 0:NQ], c == 0, c == 7, ['w1', xbn], gn)
            sgb = sg[fo % 2]; sgn = 'sg%d' % (fo % 2)
            act(S, sgb[:, 0:NQ], gb[:, 0:NQ], AF.Sigmoid, [gn, 'cln'], sgn, bias=cln[:, L_BPW1 + 8 + fo:L_BPW1 + 9 + fo])
            stt(S, 'dve', hbt[:, fo, 30:30 + NQ], vb[:, 0:NQ], cln[:, L_BPW1 + fo:L_BPW1 + fo + 1], sgb[:, 0:NQ], ALU.add, ALU.mult,
                [vn, sgn, 'cln'], hbn)
            yield

    def load(ti):
        t0, NQ = tl[ti]; s_ = ti % 2; s3_ = ti % 3
        S.dma_in('sp', xb[s_][:, :, 0:NQ], D['x1bT'][:, 128 + t0:128 + t0 + NQ].rearrange("(c p) t -> p c t", p=128), 'xb%d' % s_, 'x1bT')
        S.dma_in('sp', x1[s3_][:, :, 0:NQ], D['x1T'][:, 128 + t0:128 + t0 + NQ].rearrange("(c p) t -> p c t", p=128), 'x1_%d' % s3_, 'x1T')

    def glu_tile(ti):
        t0, NQ = tl[ti]; s_ = ti % 2
        for _ in glu(xb[s_], 'xb%d' % s_, NQ, hb[s_], 'hb%d' % s_):
            yield
        cp(S, 'pool', hb[1 - s_][:, :, 0:30], hb[s_][:, :, NQ:NQ + 30], 'hb%d' % s_, 'hb%d' % (1 - s_))

    def dw(ti):
        t0, NQ = tl[ti]; s_ = ti % 2
        hbn = 'hb%d' % s_
        for c in range(8):
            i3 = k[0] % 3; k[0] += 1
            vb = pv[i3]; vn = 'pv%d' % i3
            for j in range(31):
                mm(S, vb[:, 0:NQ], dg[:, j, c, :], hb[s_][:, c, j:j + NQ], j == 0, j == 30, ['dg', hbn], vn)
            act(S, xc[:, c, 0:NQ], vb[:, 0:NQ], AF.Identity, [vn, 'cln'], 'xc', bias=cln[:, L_BDW + c:L_BDW + c + 1])
            yield

    def ln1(ti):
        t0, NQ = tl[ti]
        return ln_fm_gen(S, xc[:, :, 0:NQ], 'xc', NQ, cln[:, L_CLG:L_CLG + 8], cln[:, L_CLB:L_CLB + 8], T16[:, :, 0:NQ], mean[:, 0:NQ],
                         var[:, 0:NQ], ones_k[:], psA[:, 0:NQ], psB[:, 0:NQ], hc[:, :, 0:NQ], 'hc', silu=True)

    def pw2(ti):
        t0, NQ = tl[ti]; s_ = ti % 3
        x1n = 'x1_%d' % s_
        act(S, x1[s_][:, :, 0:NQ], x1[s_][:, :, 0:NQ], AF.Copy, x1n, x1n, scale=ALPHA)
        for fo in range(8):
            i3 = k[0] % 3; k[0] += 1
            gb = pg[i3]; gn = 'pg%d' % i3
            for c in range(8):
                mm(S, gb[:, 0:NQ], w2[:, c, fo * 128:(fo + 1) * 128], hc[:, c, 0:NQ], c == 0, c == 7, ['w2', 'hc'], gn)
            stt(S, 'dve', x1[s_][:, fo, 0:NQ], gb[:, 0:NQ], cln[:, L_BPW2 + fo:L_BPW2 + fo + 1], x1[s_][:, fo, 0:NQ], ALU.add, ALU.add,
                [gn, 'cln', x1n], x1n)

    def ln2_store(ti):
        t0, NQ = tl[ti]; s_ = ti % 3
        x1n = 'x1_%d' % s_
        for _ in ln_fm_gen(S, x1[s_][:, :, 0:NQ], x1n, NQ, cln[:, L_CPG:L_CPG + 8], cln[:, L_CPB:L_CPB + 8], T16[:, :, 0:NQ], mean[:, 0:NQ],
                           var[:, 0:NQ], ones_k[:], psA[:, 0:NQ], psB[:, 0:NQ], hc[:, :, 0:NQ], 'hc'):
            yield
        S.dma_out('sp', D['x2T'][:, t0:t0 + NQ].rearrange("(c p) t -> p c t", p=128), x1[s_][:, :, 0:NQ], x1n, 'x2T_%d' % ti)
        S.dma_out('sp', D['x2bT'][:, t0:t0 + NQ].rearrange("(c p) t -> p c t", p=128), hc[:, :, 0:NQ], 'hc', 'x2bT_%d' % ti)

    S.dma_in('sp', xb[1][:, :, 0:128], D['x1bT'][:, 0:128].rearrange("(c p) t -> p c t", p=128), 'xb1', 'x1bT')
    for _ in glu(xb[1], 'xb1', 128, hb[1], 'hb1'):
        pass
    ts(S, 'dve', hb[0][:, :, 0:30], hb[1][:, :, 128:158], hm[:, 0:1], None, ALU.mult, None, ['hb1', 'hm'], 'hb0')
    load(0)
    gdrain(glu_tile(0))
    l2 = None
    for ti in range(len(tl)):
        if ti + 1 < len(tl):
            load(ti + 1)
        for _ in dw(ti):
            l2 = gstep(l2, 1)
        l2 = gdrain(l2)
        l1 = ln1(ti)
        if ti + 1 < len(tl):
            for _ in glu_tile(ti + 1):
                l1 = gstep(l1, 1)
        gdrain(l1)
        pw2(ti)
        l2 = ln2_store(ti)
    gdrain(l2)
    S.flush()
    cx.close()


def phase6(S, nc, W, D):
    OWN = W // 2
    TB = min(1024, OWN)
    NH = TB // 512
    cx = Ctx(nc)
    cln = cx.sb("cln", [128, CLN_COLS], F32)
    ones_k = cx.sb("ones_k", [128, 128], BF16)
    id32 = cx.sb("id32", [128, 128], F32)
    sel = cx.sb("sel", [8, 8, 128], F32)
    wr = cx.sb("wr", [128, 8, 8], F32)
    xb = cx.sb("xb", [128, 8, TB], BF16)
    hT = cx.sb("hT", [128, 28, TB], BF16)
    yacc = cx.sb("yacc", [128, 8, TB], F32)
    gF = cx.sb("gF", [8, TB], F32)
    gbc = [cx.sb("gbc%d" % i, [128, TB], F32) for i in range(2)]
    NWB = 4
    wgb = [cx.sb("wgb%d" % i, [128, 8, 128], BF16) for i in range(NWB)]
    wub = [cx.sb("wub%d" % i, [128, 8, 128], BF16) for i in range(NWB)]
    wdb = [cx.sb("wdb%d" % i, [128, 28, 128], BF16) for i in range(3)]
    sg = [cx.sb("sg%d" % i, [128, 512], F32) for i in range(2)]
    tmp = [cx.sb("tmp%d" % i, [128, 512], F32) for i in range(2)]
    mean = cx.sb("mean", [128, 512], F32)
    var = cx.sb("var", [128, 512], F32)
    lt = cx.sb("lt", [128, 8], F32)
    m8 = cx.sb("m8", [128, 8], F32)
    msk = cx.sb("msk", [128, 8], F32)
    ex = cx.sb("ex", [128, 8], F32)
    sc = cx.sb("sc", [128, 4], F32)
    pg = [cx.ps("pg%d" % i, [128, 512], F32) for i in range(3)]
    pu = [cx.ps("pu%d" % i, [128, 512], F32) for i in range(3)]
    psA = cx.ps("psA", [128, 512], F32)
    psB = cx.ps("psB", [128, 512], F32)
    S.set_alias({'pg0': 'B0', 'pg1': 'B1', 'pg2': 'B2', 'pu0': 'B3', 'pu1': 'B4', 'pu2': 'B5', 'psA': 'B6', 'psB': 'B7'})
    S.dma_in('sp', cln[:], D['cln'], 'cln')
    S.dma_in('sp', id32[:], D['id32'], 'id32')
    S.dma_in('sp', sel[:], D['sel'], 'sel')
    S.dma_in('sp', wr[:], D['w_router'].rearrange("(c p) e -> p c e", p=128), 'wr')
    mset(S, 'dve', ones_k[:], 1.0 / 1024.0, 'ones_k')
    xr32 = hT[:].rearrange("p a t -> p (a t)").bitcast(F32)[:, 0:8 * 512].rearrange("p (c t) -> p c t", t=512)
    k = [0]
    wi = [0]
    wj = [0]
    for ti, (t0, NT_) in enumerate(tiles_of(OWN, TB)):
        nh = NT_ // 512
        for nb in range(nh):
            c0 = t0 + nb * 512
            S.dma_in('sp', xr32, D['x2T'][:, c0:c0 + 512].rearrange("(c p) t -> p c t", p=128), 'hT', 'x2T')
            for tb_ in range(4):
                S.small = True
                for c in range(8):
                    mm(S, psA[:, 0:8], xr32[:, c, tb_ * 128:(tb_ + 1) * 128], wr[:, c, :], c == 0, c == 7, ['hT', 'wr'], 'psA')
                cp(S, 'dve', lt[:], psA[:, 0:8], 'psA', 'lt')
                S.op('dve', lambda e: e.max(out=m8[:], in_=lt[:]), r=['lt'], w=['m8'])
                ts(S, 'dve', msk[:], lt[:], m8[:, 1:2], None, ALU.is_ge, None, ['lt', 'm8'], 'msk')
                ts(S, 'dve', sc[:, 0:1], m8[:, 0:1], -1.0, None, ALU.mult, None, 'm8', 'sc')
                act(S, ex[:], lt[:], AF.Exp, ['lt', 'sc'], 'ex', bias=sc[:, 0:1])
                tt(S, 'dve', ex[:], ex[:], msk[:], ALU.mult, ['ex', 'msk'], 'ex')
                S.op('dve', lambda e: e.reduce_sum(out=sc[:, 1:2], in_=ex[:], axis=AX.X), r=['ex'], w=['sc'])
                S.op('dve', lambda e: e.reciprocal(out=sc[:, 2:3], in_=sc[:, 1:2]), r=['sc'], w=['sc'])
                ts(S, 'dve', ex[:], ex[:], sc[:, 2:3], None, ALU.mult, None, ['ex', 'sc'], 'ex')
                S.op('pe', lambda e: e.transpose(psB[0:8, 0:128], ex[:], id32[:]), r=['ex', 'id32'], w=['psB'])
                col = nb * 512 + tb_ * 128
                cp(S, 'dve', gF[:, col:col + 128], psB[0:8, 0:128], 'psB', 'gF')
                S.small = False
            act(S, yacc[:, :, nb * 512:(nb + 1) * 512], xr32, AF.Copy, 'hT', 'yacc', scale=ALPHA)
        if 'dbg_gF' in D:
            S.dma_out('sp', D['dbg_gF'][:, t0:t0 + NT_], gF[:, 0:NT_], 'gF', 'dbg_gF')
        S.dma_in('sp', xb[:, :, 0:NT_], D['x2bT'][:, t0:t0 + NT_].rearrange("(c p) t -> p c t", p=128), 'xb', 'x2bT')
        for e_ in range(8):
            gb_ = gbc[e_ % 2]; gbn = 'gbc%d' % (e_ % 2)
            for nb in range(nh):
                mm(S, psA[:, 0:512], sel[:, e_, :], gF[:, nb * 512:(nb + 1) * 512], True, True, ['sel', 'gF'], 'psA')
                cp(S, 'act', gb_[:, nb * 512:(nb + 1) * 512], psA[:, 0:512], 'psA', gbn)
            for fc in range(28):
                s2 = wi[0] % NWB; wi[0] += 1
                S.dma_in('sp', wgb[s2][:], D['moe_wg16'][e_, fc], 'wgb%d' % s2)
                S.dma_in('sp', wub[s2][:], D['moe_wu16'][e_, fc], 'wub%d' % s2)
                for nb in range(nh):
                    i3 = k[0] % 3; k[0] += 1
                    gb, ub = pg[i3], pu[i3]; gn, un = 'pg%d' % i3, 'pu%d' % i3
                    ns = slice(nb * 512, (nb + 1) * 512)
                    for c in range(8):
                        mm(S, gb[:, :], wgb[s2][:, c, :], xb[:, c, ns], c == 0, c == 7, ['wgb%d' % s2, 'xb'], gn)
                    for c in range(8):
                        mm(S, ub[:, :], wub[s2][:, c, :], xb[:, c, ns], c == 0, c == 7, ['wub%d' % s2, 'xb'], un)
                    sgb = sg[k[0] % 2]; sgn = 'sg%d' % (k[0] % 2)
                    act(S, sgb[:], gb[:, :], AF.Silu, gn, sgn)
                    tt(S, 'dve', hT[:, fc, ns], sgb[:], ub[:, :], ALU.mult, [sgn, un], 'hT')
            for fo in range(8):
                s2 = wj[0] % 3; wj[0] += 1
                S.dma_in('sp', wdb[s2][:], D['moe_wd16'][e_, fo], 'wdb%d' % s2)
                for nb in range(nh):
                    i3 = k[0] % 3; k[0] += 1
                    gb = pg[i3]; gn = 'pg%d' % i3
                    ns = slice(nb * 512, (nb + 1) * 512)
                    for fc in range(28):
                        mm(S, gb[:, :], wdb[s2][:, fc, :], hT[:, fc, ns], fc == 0, fc == 27, ['wdb%d' % s2, 'hT'], gn)
                    tb2 = tmp[k[0] % 2]; tn = 'tmp%d' % (k[0] % 2)
                    tt(S, 'dve', tb2[:], gb[:, :], gb_[:, ns], ALU.mult, [gn, gbn], tn)
                    tt(S, 'pool', yacc[:, fo, ns], yacc[:, fo, ns], tb2[:], ALU.add, ['yacc', tn], 'yacc')
        if 'dbg_y' in D:
            S.dma_out('sp', D['dbg_y'][:, t0:t0 + NT_].rearrange("(c p) t -> p c t", p=128), yacc[:, :, 0:NT_], 'yacc', 'dbg_y')
        for nb in range(nh):
            ns = slice(nb * 512, (nb + 1) * 512)
            ln_fm(S, yacc[:, :, ns], 'yacc', 512, cln[:, L_MOEG:L_MOEG + 8], cln[:, L_MOEB:L_MOEB + 8], xb[:, :, ns], mean[:], var[:],
                  ones_k[:], psA[:, 0:512], psB[:, 0:512], xb[:, :, ns], 'xb')
        S.dma_out('sp', D['outT'][:, t0:t0 + NT_].rearrange("(c p) t -> p c t", p=128), yacc[:, :, 0:NT_], 'yacc', 'outT_%d' % ti)
    S.flush()
    cx.close()


def _moe_layout(inp):
    wg = inp["moe_w_gate"][0]; wu = inp["moe_w_up"][0]; wd = inp["moe_w_down"][0]
    wg_l = np.ascontiguousarray(wg.reshape(8, 8, 128, 28, 128).transpose(0, 3, 2, 1, 4))
    wu_l = np.ascontiguousarray(wu.reshape(8, 8, 128, 28, 128).transpose(0, 3, 2, 1, 4))
    wd_l = np.ascontiguousarray(wd.reshape(8, 28, 128, 8, 128).transpose(0, 3, 2, 1, 4))
    return wg_l, wu_l, wd_l


def _sel():
    s = np.zeros((8, 8, 128), np.float32)
    for e in range(8):
        s[e, e, :] = 1.0
    return s


def build_program(W):
    OWN = W // 2; OT = OWN + 128
    nc = bass.Bass("TRN2", target_bir_lowering=False)

    def dt_(name, shape, dt, kind="Internal"):
        return nc.dram_tensor(name, shape, dt, kind=kind).ap()
    EI = "ExternalInput"
    D = {}
    D['xT'] = dt_("xT", [1024, W], F32, EI); D['w_in'] = dt_("w_in", [1024, 3240], F32, EI)
    D['c12'] = dt_("c12", [128, 48], F32, EI); D['cm32'] = dt_("cm32", [128, 1152], F32, EI); D['cm16'] = dt_("cm16", [128, 384], F32, EI)
    D['lw'] = dt_("lw", [64, 512], F32, EI); D['gu'] = dt_("gu", [96, 512], F32, EI); D['bf'] = dt_("bf", [8, 1], F32, EI)
    D['ctri'] = dt_("ctri", [128, 128], F32, EI); D['kmask'] = dt_("kmask", [1, W], F32, EI)
    D['w_out'] = dt_("w_out", [1024, 1024], F32, EI); D['cln'] = dt_("cln", [128, CLN_COLS], F32, EI)
    D['w_gate'] = dt_("w_gate", [1024, 2816], F32, EI); D['w_up'] = dt_("w_up", [1024, 2816], F32, EI); D['w_down'] = dt_("w_down", [2816, 1024], F32, EI)
    D['w_pw1'] = dt_("w_pw1", [1024, 2048], F32, EI); D['w_pw2'] = dt_("w_pw2", [1024, 1024], F32, EI); D['hmask'] = dt_("hmask", [128, 1], F32, EI)
    D['id32'] = dt_("id32", [128, 128], F32, EI); D['sel'] = dt_("sel", [8, 8, 128], F32, EI); D['w_router'] = dt_("w_router", [1024, 8], F32, EI)
    D['moe_wg'] = dt_("moe_wg", [8, 28, 128, 8, 128], F32, EI); D['moe_wu'] = dt_("moe_wu", [8, 28, 128, 8, 128], F32, EI)
    D['moe_wd'] = dt_("moe_wd", [8, 8, 128, 28, 128], F32, EI)
    D['moe_wg16'] = dt_("moe_wg16", [8, 28, 128, 8, 128], BF16); D['moe_wu16'] = dt_("moe_wu16", [8, 28, 128, 8, 128], BF16)
    D['moe_wd16'] = dt_("moe_wd16", [8, 8, 128, 28, 128], BF16)
    D['fq'] = dt_("fq", [8, 64, OT], BF16); D['fk'] = dt_("fk", [8, 64, W], BF16); D['fv'] = dt_("fv", [W, 512], BF16)
    D['qn'] = dt_("qn", [8, OT], F32); D['kn'] = dt_("kn", [8, W], F32); D['cs'] = dt_("cs", [8, 6, W], BF16); D['negm'] = dt_("negm", [8, OT], BF16)
    D['fc'] = dt_("fc", [8, W], F32); D['ymix'] = dt_("ymix", [1024, OT], BF16)
    D['xaT'] = dt_("xaT", [1024, OT], F32); D['xabT'] = dt_("xabT", [1024, OT], BF16)
    D['x1T'] = dt_("x1T", [1024, OT], F32); D['x1bT'] = dt_("x1bT", [1024, OT], BF16)
    D['x2T'] = dt_("x2T", [1024, OWN], F32); D['x2bT'] = dt_("x2bT", [1024, OWN], BF16)
    D['outT'] = dt_("outT", [1024, OWN], F32, "ExternalOutput")
    S = Sched(nc)
    phase12(S, nc, W, D)
    phase3(S, nc, W, D)
    phase4a(S, nc, W, D)
    phase4b(S, nc, W, D)
    phase5(S, nc, W, D)
    phase6(S, nc, W, D)
    S.close()
    return nc


def make_in_maps(inputs, W, cores):
    OWN = W // 2
    inp = {k: np.asarray(v) for k, v in inputs.items()}
    cm32, cm16 = _masks()
    t = np.arange(128)
    ctri = (t[:, None] <= t[None, :]).astype(np.float32)
    wg_l, wu_l, wd_l = _moe_layout(inp)
    shared = dict(
        w_in=np.ascontiguousarray(inp["mix_w_in"][0]), c12=_c12(inp), cm32=cm32, cm16=cm16,
        lw=np.ascontiguousarray(np.concatenate([inp["rwkv_w_up"][0], inp["rwkv_a_up"][0]], 0)),
        gu=np.ascontiguousarray(inp["rwkv_g_up"][0]), bf=np.ascontiguousarray(inp["fox_b_f"][0].reshape(8, 1)),
        ctri=ctri, w_out=np.ascontiguousarray(inp["mix_w_out"][0]), cln=_cln(inp),
        w_gate=np.ascontiguousarray(inp["ffn_w_gate"][0]), w_up=np.ascontiguousarray(inp["ffn_w_up"][0]),
        w_down=np.ascontiguousarray(inp["ffn_w_down"][0]), w_pw1=np.ascontiguousarray(inp["conv_w_pw1"][0]),
        w_pw2=np.ascontiguousarray(inp["conv_w_pw2"][0]), id32=np.eye(128, dtype=np.float32), sel=_sel(),
        w_router=np.ascontiguousarray(inp["moe_w_router"][0]), moe_wg=wg_l, moe_wu=wu_l, moe_wd=wd_l)
    maps = []
    x = inp["x"]
    for (b, hf) in cores:
        km = np.zeros((1, W), np.float32)
        if hf == 1:
            xw = x[b, 0:W]
        else:
            xw = np.concatenate([np.zeros((OWN, 1024), np.float32), x[b, 0:OWN]], 0)
            km[:, :OWN] = -30000.0
        m = dict(shared)
        m['xT'] = np.ascontiguousarray(xw.T)
        m['kmask'] = km
        m['hmask'] = np.full((128, 1), float(hf), np.float32)
        maps.append(m)
    return maps


_NC_CACHE = {}


def kernel(**inputs):
    x = np.asarray(inputs["x"])
    B, T, Dm = x.shape
    W = T
    OWN = W // 2
    if W not in _NC_CACHE:
        _NC_CACHE[W] = build_program(W)
    nc = _NC_CACHE[W]
    cores = [(b, hf) for b in range(B) for hf in range(2)]
    maps = make_in_maps(inputs, W, cores)
    res = run_bass_kernel_spmd(nc, maps, core_ids=list(range(len(cores))))
    out = np.empty((B, T, Dm), np.float32)
    for i, (b, hf) in enumerate(cores):
        out[b, hf * OWN:(hf + 1) * OWN, :] = np.asarray(res.results[i]["outT"]).T
    return out
```

```python
import numpy as np
import concourse.bass as bass
import concourse.mybir as mybir
from concourse.bass_utils import run_bass_kernel_spmd

F32 = mybir.dt.float32
BF16 = mybir.dt.bfloat16
AF = mybir.ActivationFunctionType
ALU = mybir.AluOpType
AX = mybir.AxisListType
ENGS = ['sp', 'act', 'dve', 'pool', 'pe']
NEAR = 2


class Rec:
    __slots__ = ('eng', 'fn', 'deps', 'dma', 'key', 'sig', 'cnt', 'dsem', 'dval', 'small', 'idx', 'rows')

    def __init__(self, eng, fn, dma, key):
        self.eng = eng; self.fn = fn; self.dma = dma; self.key = key
        self.deps = (); self.sig = False; self.cnt = None; self.dsem = None; self.dval = None; self.small = False; self.rows = None


class Sched:
    def __init__(self, nc, n_dma_sems=72):
        self.nc = nc
        self._ctx = []
        self.esem = {}
        for e in ENGS:
            c = nc.semaphore("es_" + e); self._ctx.append(c); self.esem[e] = c.__enter__()
        self.dsems = []
        for i in range(n_dma_sems):
            c = nc.semaphore("ds_%d" % i); self._ctx.append(c); self.dsems.append(c.__enter__())
        self.dval = [0] * n_dma_sems
        self.ecount = {e: 0 for e in ENGS}
        self.waited = {e: {} for e in ENGS}
        self.recs = {e: [] for e in ENGS}
        self.lastw = {}
        self.rd = {}
        self.nextsem = 0
        self.ninstr = 0
        self.alias = {}
        self.banks = set()
        self.small = False

    def set_alias(self, al):
        self.alias = dict(al); self.banks = set(al.values())

    def close(self):
        for c in reversed(self._ctx):
            c.__exit__(None, None, None)

    def op(self, eng, fn, r=(), w=(), dma=False, key=None, rows=None):
        rec = Rec(eng, fn, dma, key)
        rec.rows = rows
        rec.small = self.small
        rec.idx = len(self.recs[eng])
        al = self.alias
        if al:
            r2 = [al.get(x, x) for x in r]; w2 = [al.get(x, x) for x in w]
            bk = self.banks
            w = w2 + [x for x in r2 if x in bk]
            r = [x for x in r2 if x not in bk]
        deps = set()
        for b in r:
            x = self.lastw.get(b)
            if x is not None: deps.add(x)
        for b in w:
            x = self.lastw.get(b)
            if x is not None: deps.add(x)
            d = self.rd.get(b)
            if d:
                for x in d.values(): deps.add(x)
        rec.deps = deps
        for b in r:
            d = self.rd.setdefault(b, {})
            if dma:
                d[('dma', id(rec))] = rec
            else:
                d[eng] = rec
        for b in w:
            self.lastw[b] = rec
            self.rd[b] = {}
        self.recs[eng].append(rec)
        return rec

    def dma_in(self, eng, out, in_, dst, src=()):
        dst = [dst] if isinstance(dst, str) else list(dst)
        src = [src] if isinstance(src, str) else list(src)
        return self.op(eng, lambda e: e.dma_start(out=out, in_=in_), r=src, w=dst, dma=True, key=eng + ':' + dst[0])

    def dma_out(self, eng, out, in_, src, dst=()):
        src = [src] if isinstance(src, str) else list(src)
        dst = [dst] if isinstance(dst, str) else list(dst)
        return self.op(eng, lambda e: e.dma_start(out=out, in_=in_), r=src, w=dst, dma=True, key=eng + ':st:' + src[0])

    def flush(self):
        nc = self.nc
        for e in ENGS:
            for rec in self.recs[e]:
                for d in rec.deps:
                    if (not d.dma) and (d.eng != rec.eng or rec.dma or (d.eng != 'pe' and (d.small or rec.idx - d.idx <= NEAR))
                                        or (d.eng == 'pe' and d.rows is not None and rec.rows is not None and d.rows != rec.rows)):
                        d.sig = True
        for e in ENGS:
            for rec in reversed(self.recs[e]):
                if not rec.dma:
                    rec.sig = True
                    break
        keymap = {}
        for e in ENGS:
            c = self.ecount[e]
            for rec in self.recs[e]:
                if rec.dma:
                    k = rec.key
                    if k not in keymap:
                        keymap[k] = self.nextsem % len(self.dsems)
                        self.nextsem += 1
                        assert len(keymap) <= len(self.dsems), "too many dma keys"
                    si = keymap[k]
                    self.dval[si] += 16
                    rec.dsem = self.dsems[si]; rec.dval = self.dval[si]
                else:
                    if rec.sig:
                        c += 1
                        rec.cnt = c
            self.ecount[e] = c
        final_e = dict(self.ecount)
        final_d = list(self.dval)
        recs = self.recs; esem = self.esem; waited = self.waited; dsems = self.dsems

        def mk(ename):
            def run(e):
                wd = waited[ename]
                for rec in recs[ename]:
                    for d in rec.deps:
                        if d.dma:
                            s, v = d.dsem, d.dval
                        elif (d.eng != ename or rec.dma or (d.eng != 'pe' and (d.small or rec.idx - d.idx <= NEAR))
                              or (d.eng == 'pe' and d.rows is not None and rec.rows is not None and d.rows != rec.rows)):
                            s, v = esem[d.eng], d.cnt
                        else:
                            continue
                        if wd.get(s.num, 0) < v:
                            e.wait_ge(s, v)
                            wd[s.num] = v
                    ins = rec.fn(e)
                    if rec.dma:
                        ins.then_inc(rec.dsem, 16)
                    elif rec.sig:
                        ins.then_inc(esem[ename], 1)
                for o in ENGS:
                    if o != ename and wd.get(esem[o].num, 0) < final_e[o]:
                        e.wait_ge(esem[o], final_e[o]); wd[esem[o].num] = final_e[o]
                for i, s in enumerate(dsems):
                    if wd.get(s.num, 0) < final_d[i]:
                        e.wait_ge(s, final_d[i]); wd[s.num] = final_d[i]
            return run

        with nc.Block() as block:
            block.sync(mk('sp'))
            block.scalar(mk('act'))
            block.vector(mk('dve'))
            block.gpsimd(mk('pool'))
            block.tensor(mk('pe'))
        for e in ENGS:
            self.ninstr += len(self.recs[e])
        self.recs = {e: [] for e in ENGS}
        self.lastw = {}
        self.rd = {}


def _l(x):
    return [x] if isinstance(x, str) else list(x)


def mm(S, out, lhsT, rhs, start, stop, r, w, tp=None):
    kw = {}
    if tp is not None:
        kw['tile_position'] = tp
    return S.op('pe', lambda e: e.matmul(out, lhsT=lhsT, rhs=rhs, start=start, stop=stop, **kw), r=_l(r), w=_l(w),
                rows=(lhsT.base_partition(), lhsT.shape[0]))


def tr(S, out, in_, ident, r, w):
    return S.op('pe', lambda e: e.transpose(out, in_, ident), r=_l(r) + ['ident'], w=_l(w),
                rows=(in_.base_partition(), in_.shape[0]))


def act(S, out, in_, func, r, w, bias=None, scale=None, accum=None):
    kw = {}
    if bias is not None: kw['bias'] = bias
    if scale is not None: kw['scale'] = scale
    if accum is not None: kw['accum_out'] = accum
    return S.op('act', lambda e: e.activation(out=out, in_=in_, func=func, **kw), r=_l(r), w=_l(w))


def tt(S, eng, out, a, b, op, r, w):
    return S.op(eng, lambda e: e.tensor_tensor(out=out, in0=a, in1=b, op=op), r=_l(r), w=_l(w))


def ts(S, eng, out, a, s1, s2, op0, op1, r, w, accum=None):
    kw = {}
    if op1 is not None: kw['op1'] = op1
    if accum is not None: kw['accum_out'] = accum
    return S.op(eng, lambda e: e.tensor_scalar(out=out, in0=a, scalar1=s1, scalar2=s2, op0=op0, **kw), r=_l(r), w=_l(w))


def stt(S, eng, out, a, s, b, op0, op1, r, w, accum=None):
    kw = {}
    if accum is not None: kw['accum_out'] = accum
    return S.op(eng, lambda e: e.scalar_tensor_tensor(out=out, in0=a, scalar=s, in1=b, op0=op0, op1=op1, **kw), r=_l(r), w=_l(w))


def cp(S, eng, out, in_, r, w):
    if eng == 'act':
        return S.op('act', lambda e: e.activation(out=out, in_=in_, func=AF.Copy), r=_l(r), w=_l(w))
    return S.op(eng, lambda e: e.tensor_copy(out=out, in_=in_), r=_l(r), w=_l(w))


def mset(S, eng, ap, val, w):
    return S.op(eng, lambda e: e.memset(ap, val), w=_l(w))


class Ctx:
    _n = [0]

    def __init__(self, nc):
        self.nc = nc; self.ctxs = []
        Ctx._n[0] += 1
        self.pfx = "c%d_" % Ctx._n[0]

    def sb(self, name, shape, dt):
        c = self.nc.sbuf_tensor(self.pfx + "s_" + name, shape, dt); self.ctxs.append(c); return c.__enter__()

    def ps(self, name, shape, dt):
        c = self.nc.psum_tensor(self.pfx + "p_" + name, shape, dt); self.ctxs.append(c); return c.__enter__()

    def close(self):
        for c in reversed(self.ctxs):
            c.__exit__(None, None, None)
        self.ctxs = []


C0 = 0.6065306597126334
O_R, O_K, O_V, O_WD, O_AD, O_GD = 0, 512, 1024, 1536, 1568, 1600
O_FQ, O_FK, O_FV, O_FF = 1696, 2208, 2720, 3232


def phase12(S, nc, W, D):
    import os
    STOP = float(os.environ.get('P12_STOP', '99'))
    OWN = W // 2
    NT = W // 512
    Y0 = OWN - 128
    cx = Ctx(nc)
    win = cx.sb("win", [128, 8, 3240], BF16)
    xt = [cx.sb("xt%d" % i, [128, 8, 512], BF16) for i in range(2)]
    c12 = cx.sb("c12", [128, 48], F32)
    om = cx.sb("om", [128, 14], F32)
    omka = cx.sb("omka", [128, 4], F32)
    cm32 = cx.sb("cm32", [128, 5 * 128 + 512], F32)
    cm16 = cx.sb("cm16", [128, 384], BF16)
    lw = cx.sb("lw", [64, 512], BF16)
    gu = cx.sb("gu", [96, 512], BF16)
    bfc = cx.sb("bfc", [8, 2], F32)
    carry = cx.sb("carry", [128, 16], F32)
    SH = [cx.sb("SH%d" % i, [128, 513], F32) for i in range(2)]
    LT = cx.sb("LT", [128, 512], F32)
    LA = cx.sb("LA", [64, 512], BF16)
    SG = cx.sb("SG", [96, 512], BF16)
    fsb = [cx.sb("fsb%d" % i, [128, 512], BF16) for i in range(2)]
    fcb = [cx.sb("fcb%d" % i, [8, 512], F32) for i in range(2)]
    fsq = cx.sb("fsq", [128, 512], BF16)
    fnr = cx.sb("fnr", [128, 512], F32)
    fct = cx.sb("fct", [8, 512], F32)
    ones8 = cx.sb("ones8", [8, 512], F32)
    names32 = ['RS', 'KS', 'VS', 'CL', 'LD', 'E1', 'A', 'KK', 'T1']
    w32 = {k: cx.sb(k, [128, 512], F32) for k in names32}
    names16d = ['Rt', 'At', 'Bt', 'Kt', 'Bb', 'Kb', 'Vb']
    w16d = {k: [cx.sb("%s%d" % (k, i), [128, 512], BF16) for i in range(2)] for k in names16d}
    w16 = {k: cx.sb(k, [128, 512], BF16) for k in ['RKb', 'SQb']}
    PL3 = [cx.sb("PL3_%d" % i, [128, 8], F32) for i in range(3)]
    GF3 = [cx.sb("GF3_%d" % i, [128, 512], F32) for i in range(3)]
    BON3 = [cx.sb("BON3_%d" % i, [128, 512], F32) for i in range(3)]
    P = []
    for s in range(2):
        P.append(dict(
            TM4=cx.sb("TM4_%d" % s, [128, 4, 4, 128], BF16),
            X6=cx.sb("X6_%d" % s, [128, 4, 2, 128], BF16),
            QT=cx.sb("QT_%d" % s, [128, 512], BF16),
            Y0T=cx.sb("Y0T_%d" % s, [128, 512], F32),
            MTs=cx.sb("MTs_%d" % s, [128, 8, 128], BF16),
            YF=cx.sb("YF_%d" % s, [128, 512], F32),
            G1=cx.sb("G1_%d" % s, [128, 512], F32),
            G2=cx.sb("G2_%d" % s, [128, 512], BF16),
            YO=cx.sb("YO_%d" % s, [128, 512], BF16),
        ))
    GS = []
    for gi in range(2):
        GS.append(dict(
            NN=[cx.sb("NN%d_g%d" % (i, gi), [128, 4, 2, 128], BF16) for i in range(2)],
            A3=cx.sb("A3_g%d" % gi, [128, 3, 4, 128], BF16),
            X=[cx.sb("X%d_g%d" % (i, gi), [128, 4, 128], BF16) for i in range(2)],
        ))
    H32 = [cx.sb("H32_%d" % g, [128, 128], F32) for g in range(4)]
    Hb = [cx.sb("Hb_%d" % g, [128, 128], BF16) for g in range(4)]
    HT = cx.sb("HT", [128, 128], F32)
    HA = [cx.sb("HA_%d" % g, [128, 128], F32) for g in range(4)]
    Gs = [cx.sb("Gs_%d" % i, [128, 8, 128], F32) for i in range(2)]
    pp = cx.ps("pp", [128, 512], F32)
    pmisc = cx.ps("pmisc", [128, 512], F32)
    phb = cx.ps("phb", [128, 512], F32)
    pb3 = cx.ps("pb3", [128, 512], F32)
    ptrb = cx.ps("ptrb", [128, 1024], BF16)
    pb5 = cx.ps("pb5", [128, 512], F32)
    pb6 = cx.ps("pb6", [128, 512], F32)
    pb7 = cx.ps("pb7", [128, 512], F32)
    ph = phb[:, 0:128]
    py = phb[:, 128:256]
    GB = [(pp, pmisc, pb5), (pb6, pb7, pb3)]
    GBN = [('pp', 'pmisc', 'pb5'), ('pb6', 'pb7', 'pb3')]
    S.set_alias({'pp': 'B0', 'pmisc': 'B1', 'ph': 'B2', 'pb3': 'B3', 'py': 'B2', 'ptr': 'B4', 'pb5': 'B5', 'pb6': 'B6', 'pb7': 'B7'})

    m2 = cm32[:, 0:256]
    m3 = cm32[:, 128:512]
    blk = cm32[:, 512:640]
    seg = cm32[:, 640:1152]
    ident = cm16[:, 0:128]
    onesb = cm16[:, 128:256]
    ones64 = cm16[:, 256:384]

    for c in range(8):
        S.dma_in('pool', win[:, c, :], D['w_in'][c * 128:(c + 1) * 128, :], 'win')
    S.dma_in('sp', c12[:], D['c12'], 'c12')
    S.dma_in('sp', cm32[:], D['cm32'], 'cm32')
    S.dma_in('pool', cm16[:], D['cm16'], ['ident', 'cm16'])
    S.dma_in('pool', lw[:], D['lw'], 'lw')
    S.dma_in('pool', gu[:], D['gu'], 'gu')
    S.dma_in('sp', bfc[:, 0:1], D['bf'], 'bfc')
    S.small = True
    ts(S, 'dve', bfc[:, 1:2], bfc[:, 0:1], -1.0, None, ALU.mult, None, 'bfc', 'bfc')
    ts(S, 'dve', om[:], c12[:, 0:14], -1.0, 1.0, ALU.mult, ALU.add, 'c12', 'om')
    ts(S, 'dve', omka[:], c12[:, 26:30], -1.0, 1.0, ALU.mult, ALU.add, 'c12', 'omka')
    mset(S, 'pool', carry[:], 0.0, 'carry')
    mset(S, 'pool', ones8[:], 1.0, 'ones8')
    mset(S, 'pool', fcb[1][:], 0.0, 'fcb1')
    for g in range(4):
        mset(S, 'pool', H32[g][:], 0.0, 'H32_%d' % g)
        mset(S, 'pool', Hb[g][:], 0.0, 'Hb_%d' % g)
    S.small = False
    mu = c12[:, 0:14]
    shc = [0]

    def shift_evac(psum_ap, R, grp, out_ap, pname, oname):
        i = shc[0] % 2; shc[0] += 1
        sh = SH[i]; shn = 'SH%d' % i
        cp(S, 'pool', sh[0:R, 0:1], carry[0:R, grp:grp + 1], 'carry', shn)
        act(S, sh[0:R, 1:513], psum_ap, AF.Copy, [pname, 'c12'], shn, scale=mu[0:R, grp:grp + 1])
        cp(S, 'pool', carry[0:R, grp:grp + 1], sh[0:R, 512:513], shn, 'carry')
        stt(S, 'dve', out_ap, psum_ap, om[0:R, grp:grp + 1], sh[0:R, 0:512], ALU.mult, ALU.add, [pname, shn, 'om'], oname)

    def proj(ps_ap, M, col0, xtile, xname, pname):
        for c in range(8):
            mm(S, ps_ap, win[:, c, col0:col0 + M], xtile[:, c, :], c == 0, c == 7, ['win', xname], pname)

    def tile_prologue(n):
        xs = n % 2
        xtile = xt[xs]; xname = 'xt%d' % xs
        S.dma_in('pool', xtile[:], D['xT'][:, n * 512:(n + 1) * 512].rearrange("(c p) t -> p c t", p=128), xname)
        t0 = n * 512
        fi = 0
        for kind, col0, dst in (('q', O_FQ, 'fq'), ('k', O_FK, 'fk')):
            if kind == 'q' and t0 + 512 <= Y0:
                continue
            for g in range(4):
                proj(pp[:, :], 128, col0 + g * 128, xtile, xname, 'pp')
                fb = fsb[fi % 2]; fn = 'fsb%d' % (fi % 2); fi += 1
                if kind == 'q':
                    act(S, fb[:], pp[:, :], AF.Copy, 'pp', fn, scale=0.125)
                    c0 = max(0, Y0 - t0)
                    tq = t0 + c0 - Y0
                else:
                    cp(S, 'dve', fb[:], pp[:, :], 'pp', fn)
                    c0 = 0
                    tq = t0
                tt(S, 'pool', fsq[:], fb[:], fb[:], ALU.mult, fn, 'fsq')
                mm(S, pmisc[:, :], onesb, fsq[:], True, True, ['cm16', 'fsq'], 'pmisc')
                cp(S, 'act', fnr[:], pmisc[:, :], 'pmisc', 'fnr')
                for h in range(2):
                    S.dma_out('sp', D[dst][2 * g + h, :, tq:tq + 512 - c0], fb[h * 64:(h + 1) * 64, c0:512], fn, '%s_%d' % (dst, n))
                    S.dma_out('sp', D[dst[1] + 'n'][2 * g + h:2 * g + h + 1, tq:tq + 512 - c0], fnr[h * 64:h * 64 + 1, c0:512], 'fnr', '%sn_%d' % (dst, n))
                yield
        for j in range(4):
            for c in range(8):
                mm(S, pp[:, :], xtile[:, c, j * 128:(j + 1) * 128], win[:, c, O_FV:O_FV + 512], c == 0, c == 7, ['win', xname], 'pp')
            fb = fsb[fi % 2]; fn = 'fsb%d' % (fi % 2); fi += 1
            cp(S, 'act', fb[:], pp[:, :], 'pp', fn)
            S.dma_out('sp', D['fv'][t0 + j * 128:t0 + (j + 1) * 128, :], fb[:], fn, 'fv_%d' % n)
            yield
        proj(pmisc[0:8, :], 8, O_FF, xtile, xname, 'pmisc')
        act(S, fct[:], pmisc[0:8, :], AF.Exp, ['pmisc', 'bfc'], 'fct', bias=bfc[:, 1:2], scale=-1.0)
        act(S, fct[:], fct[:], AF.Ln, 'fct', 'fct', bias=1.0)
        cs = n % 2
        S.op('dve', lambda e: e.tensor_tensor_scan(out=fcb[cs][:], data0=ones8[:], data1=fct[:],
                                                   initial=fcb[1 - cs][:, 511:512], op0=ALU.mult, op1=ALU.subtract),
             r=['ones8', 'fct', 'fcb%d' % (1 - cs)], w=['fcb%d' % cs])
        S.dma_out('sp', D['fc'][:, t0:t0 + 512], fcb[cs][:], 'fcb%d' % cs, 'fc_%d' % n)
        yield
        proj(pp[0:64, :], 64, O_WD, xtile, xname, 'pp')
        shift_evac(pp[0:64, :], 64, 12, LT[0:64, :], 'pp', 'LT')
        act(S, LA[0:32, :], LT[0:32, :], AF.Tanh, 'LT', 'LA')
        cp(S, 'pool', LA[32:64, :], LT[32:64, :], 'LT', 'LA')
        yield
        proj(pp[0:96, :], 96, O_GD, xtile, xname, 'pp')
        shift_evac(pp[0:96, :], 96, 13, LT[0:96, :], 'pp', 'LT')
        act(S, SG[:], LT[0:96, :], AF.Sigmoid, 'LT', 'SG')
        yield

    def unit_prep(n, g, ps, u3):
        Pd = dict(P[ps]); sfx = '_%d' % ps
        Pd['PL'] = PL3[u3]; Pd['GF'] = GF3[u3]; Pd['BON'] = BON3[u3]
        s3 = '_t%d' % u3
        xtile = xt[n % 2]; xname = 'xt%d' % (n % 2)
        RS, KS, VS, CL, LD, E1, A, KK, T1 = [w32[k] for k in names32]
        Rt, At, Bt, Kt, Bb, Kb, Vb = [w16d[k][ps] for k in names16d]
        RKb, SQb = w16['RKb'], w16['SQb']
        t0 = n * 512
        gs = slice(g * 128, (g + 1) * 128)
        for (col0, grp, dst, dn) in ((O_R, g, RS, 'RS'), (O_K, 4 + g, KS, 'KS'), (O_V, 8 + g, VS, 'VS')):
            proj(pp[:, :], 128, col0 + g * 128, xtile, xname, 'pp')
            shift_evac(pp[:, :], 128, grp, dst[:], 'pp', dn)
            yield
        mm(S, pmisc[:, :], lw[0:32, gs], LA[0:32, :], True, True, ['lw', 'LA'], 'pmisc')
        act(S, LD[:], pmisc[:, :], AF.Sigmoid, ['pmisc', 'c12'], 'LD', bias=c12[:, 14 + g:15 + g])
        mm(S, pmisc[:, :], lw[32:64, gs], LA[32:64, :], True, True, ['lw', 'LA'], 'pmisc')
        act(S, A[:], pmisc[:, :], AF.Sigmoid, ['pmisc', 'c12'], 'A', bias=c12[:, 18 + g:19 + g])
        mm(S, pmisc[:, :], gu[0:96, gs], SG[0:96, :], True, True, ['gu', 'SG'], 'pmisc')
        cp(S, 'act', Pd['GF'][:], pmisc[:, :], 'pmisc', 'GF' + s3)
        yield
        S.op('dve', lambda e: e.tensor_tensor_scan(out=CL[:], data0=seg, data1=LD[:], initial=0.0, op0=ALU.mult, op1=ALU.add),
             r=['cm32', 'LD'], w=['CL'])
        tt(S, 'pool', LD[:], CL[:], LD[:], ALU.subtract, ['CL', 'LD'], 'LD')
        act(S, E1[:], CL[:], AF.Exp, 'CL', 'E1', scale=-C0)
        act(S, LD[:], LD[:], AF.Exp, 'LD', 'LD', scale=-C0)
        act(S, CL[:], CL[:], AF.Exp, 'CL', 'CL', scale=C0)
        cp(S, 'pool', Pd['PL'][:], E1[:].rearrange("p (c t) -> p c t", t=64)[:, :, 63], 'E1', 'PL' + s3)
        yield
        act(S, KK[:], KS[:], AF.Copy, ['KS', 'c12'], 'KK', scale=c12[:, 22 + g:23 + g])
        tt(S, 'pool', SQb[:], KK[:], KK[:], ALU.mult, 'KK', 'SQb')
        mm(S, pmisc[:, :], onesb, SQb[:], True, True, ['cm16', 'SQb'], 'pmisc')
        act(S, T1[:], pmisc[:, :], AF.Ln, 'pmisc', 'T1', bias=1e-24)
        act(S, T1[:], T1[:], AF.Exp, 'T1', 'T1', scale=-0.5)
        tt(S, 'dve', KK[:], KK[:], T1[:], ALU.mult, ['KK', 'T1'], 'KK')
        yield
        ts(S, 'dve', T1[:], A[:], c12[:, 26 + g:27 + g], omka[:, g:g + 1], ALU.mult, ALU.add, ['A', 'c12', 'omka'], 'T1')
        tt(S, 'pool', KS[:], KS[:], T1[:], ALU.mult, ['KS', 'T1'], 'KS')
        stt(S, 'dve', RKb[:], RS[:], c12[:, 30 + g:31 + g], KS[:], ALU.mult, ALU.mult, ['RS', 'KS', 'c12'], 'RKb')
        mm(S, pmisc[:, :], onesb, RKb[:], True, True, ['cm16', 'RKb'], 'pmisc')
        tt(S, 'dve', Pd['BON'][:], pmisc[:, :], VS[:], ALU.mult, ['pmisc', 'VS'], 'BON' + s3)
        yield
        tt(S, 'pool', A[:], KK[:], A[:], ALU.mult, ['KK', 'A'], 'A')
        tt(S, 'dve', Rt[:], RS[:], E1[:], ALU.mult, ['RS', 'E1'], 'Rt' + str(ps))
        stt(S, 'dve', At[:], KK[:], -1.0, LD[:], ALU.mult, ALU.mult, ['KK', 'LD'], 'At' + str(ps))
        tt(S, 'dve', Bt[:], A[:], CL[:], ALU.mult, ['A', 'CL'], 'Bt' + str(ps))
        tt(S, 'pool', Kt[:], KS[:], CL[:], ALU.mult, ['KS', 'CL'], 'Kt' + str(ps))
        yield
        tt(S, 'dve', E1[:].rearrange("p (c t) -> p c t", t=64), CL[:].rearrange("p (c t) -> p c t", t=64),
           Pd['PL'][:, :].unsqueeze(2).to_broadcast([128, 8, 64]), ALU.mult, ['CL', 'PL' + s3], 'E1')
        tt(S, 'pool', Bb[:], A[:], E1[:], ALU.mult, ['A', 'E1'], 'Bb' + str(ps))
        tt(S, 'dve', Kb[:], KS[:], E1[:], ALU.mult, ['KS', 'E1'], 'Kb' + str(ps))
        cp(S, 'pool', Vb[:], VS[:], 'VS', 'Vb' + str(ps))

    def unit_stages(n, g, ps, u3):
        Pd = dict(P[ps]); sfx = '_%d' % ps
        Pd['PL'] = PL3[u3]; Pd['GF'] = GF3[u3]; Pd['BON'] = BON3[u3]
        Rt, At, Bt, Kt, Bb, Kb, Vb = [w16d[k][ps] for k in names16d]
        RtN, AtN, BtN, KtN, BbN, KbN, VbN = [k + str(ps) for k in names16d]
        t0 = n * 512
        TM4, X6 = Pd['TM4'], Pd['X6']
        if STOP <= 3:
            return
        need_g = [(t0 + (2 * gi + 1) * 128) >= Y0 for gi in range(2)]

        def chains(gi):
            for h in range(2):
                for pl in range(2):
                    yield h * 2 + pl, 2 * gi + pl, h

        def stage_a(gi):
            ba, bb_, bc = GB[gi]; an_, bn_, cn_ = GBN[gi]
            G = GS[gi]; gn_ = '_g%d' % gi
            for pl in range(2):
                p = 2 * gi + pl
                pc = slice(p * 128, (p + 1) * 128)
                for i, (src, sn) in enumerate(((At, AtN), (Vb, VbN), (Bb, BbN), (Kb, KbN))):
                    tr(S, ptrb[:, pl * 512 + i * 128:pl * 512 + (i + 1) * 128], src[:, pc], ident, sn, 'ptr')
            cp(S, 'act', TM4[:, 2 * gi:2 * gi + 2, :, :], ptrb[:, :].rearrange("q (a i t) -> q a i t", a=2, i=4), 'ptr', 'TM4' + sfx)
            STGA = float(os.environ.get('P12_STAGE', '9'))
            if STGA <= 0.2:
                return
            for ch, p, h in chains(gi):
                hs = slice(h * 64, (h + 1) * 64); pc = slice(p * 128, (p + 1) * 128)
                bk, bkn = (ba, an_) if ch < 2 else (bb_, bn_)
                off = (ch % 2) * 256
                mm(S, bk[:, off:off + 128], At[hs, pc], Bt[hs, pc], True, True, [AtN, BtN], bkn)
                mm(S, bk[:, off + 128:off + 256], Bt[hs, pc], At[hs, pc], True, True, [AtN, BtN], bkn)
            EV = int(os.environ.get('P12_EV', '1'))
            if EV == 1:
                for c_ in range(2):
                    tt(S, 'dve', G['NN'][0][:, c_].rearrange("q a t -> q (a t)"), ba[:, c_ * 256:(c_ + 1) * 256], m2, ALU.mult, [an_, 'cm32'], 'NN0' + gn_)
                    tt(S, 'dve', G['NN'][0][:, 2 + c_].rearrange("q a t -> q (a t)"), bb_[:, c_ * 256:(c_ + 1) * 256], m2, ALU.mult, [bn_, 'cm32'], 'NN0' + gn_)
            ntyp = 3 if need_g[gi] else 1
            if STGA <= 0.4:
                return
            for ch, p, h in chains(gi):
                hs = slice(h * 64, (h + 1) * 64); pc = slice(p * 128, (p + 1) * 128)
                cc = slice(ch * 128, (ch + 1) * 128)
                if need_g[gi]:
                    mm(S, ba[:, cc], Kt[hs, pc], At[hs, pc], True, True, [KtN, AtN], an_)
                    mm(S, bb_[:, cc], Bt[hs, pc], Rt[hs, pc], True, True, [BtN, RtN], bn_)
                    mm(S, bc[:, cc], Kt[hs, pc], Rt[hs, pc], True, True, [KtN, RtN], cn_)
                else:
                    bk, bkn = (ba, an_) if h == 0 else (bb_, bn_)
                    mm(S, bk[:, cc], Kt[hs, pc], At[hs, pc], True, True, [KtN, AtN], bkn)
            if need_g[gi]:
                for k_, (bk, bkn) in enumerate(((ba, an_), (bb_, bn_), (bc, cn_))):
                    mk = m3[:, 0:128] if k_ == 0 else m3[:, 128:256]
                    tt(S, 'dve', G['A3'][:, k_, :, :], bk[:, :].rearrange("q (c t) -> q c t", c=4), mk.unsqueeze(1).to_broadcast([128, 4, 128]),
                       ALU.mult, [bkn, 'cm32'], 'A3' + gn_)
            else:
                for h_, (bk, bkn) in enumerate(((ba, an_), (bb_, bn_))):
                    tt(S, 'dve', G['A3'][:, 0, 2 * h_:2 * h_ + 2, :], bk[:, h_ * 256:(h_ + 1) * 256].rearrange("q (c t) -> q c t", c=2),
                       m3[:, 0:128].unsqueeze(1).to_broadcast([128, 2, 128]), ALU.mult, [bkn, 'cm32'], 'A3' + gn_)
            if STGA <= 0.6:
                return
            for ch, p, h in chains(gi):
                hs = slice(h * 64, (h + 1) * 64)
                mm(S, ba[:, ch * 128 + 64:(ch + 1) * 128], G['A3'][:, 0, ch, :], TM4[:, p, 1, hs], True, True, ['A3' + gn_, 'TM4' + sfx], an_)
            cp(S, 'act', G['X'][0][:, :, 64:128], ba[:, :].rearrange("q (c t) -> q c t", c=4)[:, :, 64:128], an_, 'X0' + gn_)
            cp(S, 'dve', G['X'][0][:, :, 0:64].rearrange("q (h a) c -> q a h c", h=2),
               TM4[:, 2 * gi:2 * gi + 2, 0, :].rearrange("q a (h c) -> q a h c", h=2), 'TM4' + sfx, 'X0' + gn_)

        def stage_b(gi, lvl):
            ba, bb_, bc = GB[gi]; an_, bn_, cn_ = GBN[gi]
            G = GS[gi]; gn_ = '_g%d' % gi
            cur = lvl % 2
            nn = G['NN'][cur]; nnn = 'NN%d%s' % (cur, gn_)
            xc = G['X'][cur]; xcn = 'X%d%s' % (cur, gn_)
            for ch, p, h in chains(gi):
                cc = slice(ch * 128, (ch + 1) * 128)
                mm(S, ba[:, cc], ident, xc[:, ch, :], True, False, [xcn, 'ident'], an_)
                mm(S, ba[:, cc], nn[:, ch, 1, :], xc[:, ch, :], False, True, [xcn, nnn], an_)
            if lvl < 5:
                cp(S, 'act', G['X'][1 - cur][:, :, :], ba[:, :].rearrange("q (c t) -> q c t", c=4), an_, 'X%d%s' % (1 - cur, gn_))
                for ch, p, h in chains(gi):
                    bk, bkn = (bb_, bn_) if ch < 2 else (bc, cn_)
                    off = (ch % 2) * 256
                    mm(S, bk[:, off:off + 128], nn[:, ch, 1, :], nn[:, ch, 0, :], True, True, nnn, bkn)
                    mm(S, bk[:, off + 128:off + 256], nn[:, ch, 0, :], nn[:, ch, 1, :], True, True, nnn, bkn)
                nx = G['NN'][1 - cur]; nxn = 'NN%d%s' % (1 - cur, gn_)
                cp(S, 'dve', nx[:, 0:2].rearrange("q c a t -> q c (a t)"), bb_[:, :].rearrange("q (c x) -> q c x", c=2), bn_, nxn)
                cp(S, 'dve', nx[:, 2:4].rearrange("q c a t -> q c (a t)"), bc[:, :].rearrange("q (c x) -> q c x", c=2), cn_, nxn)
            else:
                for pl in range(2):
                    p = 2 * gi + pl
                    cp(S, 'act', X6[:, p, :, :].rearrange("q k (h c) -> q h k c", h=2),
                       ba[:, :].rearrange("q (h a k c) -> q h a k c", h=2, a=2, k=2)[:, :, pl, :, :], an_, 'X6' + sfx)

        def stage_c(gi):
            ba, bb_, bc = GB[gi]; an_, bn_, cn_ = GBN[gi]
            G = GS[gi]; gn_ = '_g%d' % gi
            gc = slice(gi * 256, (gi + 1) * 256)
            if need_g[gi]:
                for ch, p, h in chains(gi):
                    hs = slice(h * 64, (h + 1) * 64)
                    pl = p - 2 * gi
                    mm(S, ba[hs, pl * 128:(pl + 1) * 128], X6[:, p, 0, hs], G['A3'][:, 1, ch, :], True, True, ['X6' + sfx, 'A3' + gn_], an_, tp=(0, h * 64))
                    mm(S, ba[hs, 256 + pl * 128:256 + (pl + 1) * 128], X6[:, p, 1, hs], G['A3'][:, 1, ch, :], True, False,
                       ['X6' + sfx, 'A3' + gn_], an_, tp=(0, h * 64))
                    mm(S, ba[hs, 256 + pl * 128:256 + (pl + 1) * 128], TM4[:, p, 1, hs], G['A3'][:, 2, ch, :], False, True,
                       ['TM4' + sfx, 'A3' + gn_], an_, tp=(0, h * 64))
                tt(S, 'dve', Pd['QT'][:, gc], ba[:, 0:256], Rt[:, gc], ALU.add, [an_, RtN], 'QT' + sfx)
                cp(S, 'act', Pd['Y0T'][:, gc], ba[:, 256:512], an_, 'Y0T' + sfx)
            for c2 in range(2):
                bk, bkn = (bb_, bn_) if c2 == 0 else (bc, cn_)
                cs_ = slice(c2 * 64, (c2 + 1) * 64)
                for pl in range(2):
                    p = 2 * gi + pl
                    mm(S, bk[:, pl * 128:(pl + 1) * 128], X6[cs_, p, 0, :], TM4[cs_, p, 2, :], True, True, ['X6' + sfx, 'TM4' + sfx], bkn)
            blk2 = blk.unsqueeze(1).to_broadcast([128, 2, 128])
            for c2 in range(2):
                bk, bkn = (bb_, bn_) if c2 == 0 else (bc, cn_)
                tt(S, 'dve', Pd['MTs'][:, 4 * gi:4 * gi + 4, :].rearrange("q (a c) t -> q a c t", c=2)[:, :, c2, :],
                   bk[:, 0:256].rearrange("q (a t) -> q a t", a=2), blk2, ALU.mult, [bkn, 'cm32'], 'MTs' + sfx)
            for c2 in range(2):
                bk, bkn = (bb_, bn_) if c2 == 0 else (bc, cn_)
                cs_ = slice(c2 * 64, (c2 + 1) * 64)
                for pl in range(2):
                    p = 2 * gi + pl
                    mm(S, bk[:, pl * 128:(pl + 1) * 128], TM4[cs_, p, 2, :], X6[cs_, p, 1, :], True, False, ['TM4' + sfx, 'X6' + sfx], bkn)
                    mm(S, bk[:, pl * 128:(pl + 1) * 128], TM4[cs_, p, 3, :], TM4[cs_, p, 1, :], False, True, ['TM4' + sfx], bkn)
            for c2 in range(2):
                bk, bkn = (bb_, bn_) if c2 == 0 else (bc, cn_)
                tt(S, 'dve', Gs[ps][:, 4 * gi:4 * gi + 4, :].rearrange("q (a c) t -> q a c t", c=2)[:, :, c2, :],
                   bk[:, 0:256].rearrange("q (a t) -> q a t", a=2), blk2, ALU.mult, [bkn, 'cm32'], 'Gs' + sfx)

        STG = float(os.environ.get('P12_STAGE', '9'))
        stage_a(0)
        yield
        stage_a(1)
        yield
        for lvl in range(6 if STG > 1 else 0):
            stage_b(0, lvl)
            stage_b(1, lvl)
            yield
        if STG > 2:
            stage_c(0)
            stage_c(1)

    def unit_serial(n, g, ps, u3):
        Pd = dict(P[ps]); sfx = '_%d' % ps
        Pd['PL'] = PL3[u3]; Pd['GF'] = GF3[u3]; Pd['BON'] = BON3[u3]
        s3 = '_t%d' % u3
        TM4, X6 = Pd['TM4'], Pd['X6']
        t0 = n * 512
        hn = 'Hb_%d' % g; h32n = 'H32_%d' % g
        any_y = False
        for p in range(4):
            pc = slice(p * 128, (p + 1) * 128)
            need_y = (t0 + p * 128) >= Y0
            for c2 in range(2):
                cs_ = slice(c2 * 64, (c2 + 1) * 64)
                ci = p * 2 + c2
                if need_y:
                    mm(S, py[:, c2 * 64:(c2 + 1) * 64], Hb[g][:], Pd['QT'][:, ci * 64:(ci + 1) * 64], True, True, [hn, 'QT' + sfx], 'py')
                S.small = True
                stt(S, 'dve', HA[g][:], H32[g][:], Pd['PL'][:, ci:ci + 1], Gs[ps][:, ci, :], ALU.mult, ALU.add, [h32n, 'PL' + s3, 'Gs' + sfx], 'HA%d' % g)
                mm(S, ph, Pd['MTs'][:, ci, :], Hb[g][:], True, True, ['MTs' + sfx, hn], 'ph')
                tt(S, 'dve', Hb[g][:], HA[g][:], ph, ALU.add, ['HA%d' % g, 'ph'], hn)
                tt(S, 'dve', H32[g][:], HA[g][:], ph, ALU.add, ['HA%d' % g, 'ph'], h32n)
                S.small = False
                if c2 == 1 and need_y:
                    tt(S, 'dve', Pd['YF'][:, pc], py, Pd['Y0T'][:, pc], ALU.add, ['py', 'Y0T' + sfx], 'YF' + sfx)
                    any_y = True
                yield
        if any_y:
            c0 = max(0, Y0 - t0)
            cc = slice(c0, 512)
            YF, G1, G2, YO = Pd['YF'], Pd['G1'], Pd['G2'], Pd['YO']
            cp(S, 'act', G2[:, cc], YF[:, cc], 'YF' + sfx, 'G2' + sfx)
            mm(S, pmisc[:, cc], ones64, G2[:, cc], True, True, ['cm16', 'G2' + sfx], 'pmisc')
            tt(S, 'dve', YF[:, cc], YF[:, cc], pmisc[:, cc], ALU.subtract, ['YF' + sfx, 'pmisc'], 'YF' + sfx)
            tt(S, 'pool', G2[:, cc], YF[:, cc], YF[:, cc], ALU.mult, 'YF' + sfx, 'G2' + sfx)
            mm(S, pmisc[:, cc], ones64, G2[:, cc], True, True, ['cm16', 'G2' + sfx], 'pmisc')
            act(S, G1[:, cc], pmisc[:, cc], AF.Ln, 'pmisc', 'G1' + sfx, bias=64e-5)
            act(S, G1[:, cc], G1[:, cc], AF.Exp, 'G1' + sfx, 'G1' + sfx, scale=-0.5)
            tt(S, 'dve', YF[:, cc], YF[:, cc], G1[:, cc], ALU.mult, ['YF' + sfx, 'G1' + sfx], 'YF' + sfx)
            ts(S, 'pool', YF[:, cc], YF[:, cc], c12[:, 34 + g:35 + g], c12[:, 38 + g:39 + g], ALU.mult, ALU.add, ['YF' + sfx, 'c12'], 'YF' + sfx)
            tt(S, 'pool', YF[:, cc], YF[:, cc], Pd['BON'][:, cc], ALU.add, ['YF' + sfx, 'BON' + s3], 'YF' + sfx)
            tt(S, 'dve', YO[:, cc], YF[:, cc], Pd['GF'][:, cc], ALU.mult, ['YF' + sfx, 'GF' + s3], 'YO' + sfx)
            S.dma_out('sp', D['ymix'][g * 128:(g + 1) * 128, t0 + c0 - Y0:t0 + 512 - Y0], YO[:, cc], 'YO' + sfx, 'ymix_%d_%d' % (n, g))

    conv = []
    if 'moe_wg16' in D:
        for e_ in range(8):
            for fc in range(28):
                conv.append((D['moe_wg16'][e_, fc], D['moe_wg'][e_, fc]))
                conv.append((D['moe_wu16'][e_, fc], D['moe_wu'][e_, fc]))
            for fo in range(8):
                conv.append((D['moe_wd16'][e_, fo], D['moe_wd'][e_, fo]))
    per_tile = (len(conv) + NT - 1) // NT
    cvi = [0]
    units = [(n, g) for n in range(NT) for g in range(4)]

    def prep_gen(ui):
        n, g = units[ui]
        if g == 0:
            for _ in tile_prologue(n):
                yield
        for _ in unit_prep(n, g, ui % 2, ui % 3):
            yield

    def step(gen, k):
        if gen is None:
            return None
        for _ in range(k):
            try:
                next(gen)
            except StopIteration:
                return None
        return gen

    KPREP = 3
    prev = None
    pr = prep_gen(0)
    for _ in pr:
        pass
    for ui, (n, g) in enumerate(units):
        stg = unit_stages(n, g, ui % 2, ui % 3)
        pr = prep_gen(ui + 1) if ui + 1 < len(units) else None
        for _ in stg:
            prev = step(prev, 1)
            pr = step(pr, KPREP)
        if prev is not None:
            for _ in prev:
                pass
        if pr is not None:
            for _ in pr:
                pass
        prev = unit_serial(n, g, ui % 2, ui % 3)
    if prev is not None:
        for _ in prev:
            pass
    S.flush()
    cx.close()


def _masks():
    t = np.arange(128)
    same = (t[:, None] // 64) == (t[None, :] // 64)
    m_sl = (same & (t[None, :] < t[:, None])).astype(np.float32)
    m_su = m_sl.T.copy()
    m_u = (same & (t[:, None] <= t[None, :])).astype(np.float32)
    blk = same.astype(np.float32)
    seg = np.ones((128, 512), np.float32); seg[:, ::64] = 0.0
    cm32 = np.concatenate([m_sl, m_su, m_u, m_u, blk, seg], 1)
    cm16 = np.concatenate([np.eye(128, dtype=np.float32), blk, blk / 64.0], 1)
    return np.ascontiguousarray(cm32), np.ascontiguousarray(cm16)


def _c12(inp):
    c = np.zeros((128, 48), np.float32)
    mu = inp["rwkv_mu"][0]
    c[:, 0:12] = mu[0:1536].reshape(12, 128).T
    c[0:64, 12] = mu[1536:1600]
    c[0:96, 13] = mu[1600:1696]
    c[:, 14:18] = inp["rwkv_w0"][0].reshape(4, 128).T
    c[:, 18:22] = inp["rwkv_a0"][0].reshape(4, 128).T
    for i, k in enumerate(["rwkv_k_k", "rwkv_k_a", "rwkv_r_k", "rwkv_gn_g", "rwkv_gn_b"]):
        c[:, 22 + 4 * i:26 + 4 * i] = inp[k][0].reshape(4, 128).T
    return c


def phase3(S, nc, W, D):
    OWN = W // 2; OT = OWN + 128; Y0 = OWN - 128; NKB = W // 128
    cx = Ctx(nc)
    vall = cx.sb("vall", [128, NKB, 512], BF16)
    ka = [cx.sb("ka%d" % i, [128, W], BF16) for i in range(2)]
    qa = [cx.sb("qa%d" % i, [128, OT], BF16) for i in range(2)]
    va = [cx.sb("va%d" % i, [128, NKB, 128], BF16) for i in range(2)]
    PT = [cx.sb("PT%d" % i, [128, 512], BF16) for i in range(4)]
    tri = cx.sb("tri", [128, 128], BF16)
    ones1 = cx.sb("ones1", [65, 64], F32)
    rinv = cx.sb("rinv", [65, 512], F32)
    osb = cx.sb("osb", [64, 512], F32)
    yo = [cx.sb("yo%d" % i, [64, 512], BF16) for i in range(2)]
    CW = min(1024, W)
    CF = cx.sb("CF", [8, CW], F32)
    R1 = cx.sb("R1", [8, CW], F32)
    C16 = cx.sb("C16", [8, 6, CW], BF16)
    KN = cx.sb("KN", [8, CW], F32)
    kmx = cx.sb("kmx", [8, 16], F32)
    kms = cx.sb("kms", [8, 2], F32)
    QW = OT // 4 if OT % 4 == 0 and OT > 1056 else OT
    QN = cx.sb("QN", [8, QW], F32)
    NMb = cx.sb("NMb", [8, QW], BF16)
    st = [cx.ps("st%d" % i, [128, 512], F32) for i in range(4)]
    accb = [cx.ps("acc%d" % i, [128, 512], F32) for i in range(2)]
    pbc = cx.ps("pbc", [128, 512], F32)
    S.set_alias({'st0': 'B0', 'st1': 'B1', 'st2': 'B2', 'st3': 'B3', 'acc0': 'B4', 'acc1': 'B5', 'pbc': 'B6'})
    S.small = True
    S.dma_in('pool', tri[:], D['ctri'], 'tri')
    mset(S, 'dve', ones1[:], 1.0, 'ones1')
    mset(S, 'dve', kmx[:], 0.0, 'kmx')
    for i in range(2):
        mset(S, 'pool', va[i][:, :, 65:128], 0.0, 'va%d' % i)
        mset(S, 'pool', va[i][:, :, 64:65], 1.0, 'va%d' % i)
        mset(S, 'dve', ka[i][64:128, :], 0.0, 'ka%d' % i)
        mset(S, 'dve', qa[i][64:128, :], 1.0, 'qa%d' % i)
    nch = W // CW
    for ch in range(nch):
        cs_ = slice(ch * CW, (ch + 1) * CW)
        S.dma_in('sp', CF[:], D['fc'][:, cs_], 'CF', 'fc')
        cp(S, 'dve', C16[:, 0, :], CF[:], 'CF', 'C16')
        tt(S, 'dve', R1[:], CF[:], C16[:, 0, :], ALU.subtract, ['CF', 'C16'], 'R1')
        cp(S, 'dve', C16[:, 1, :], R1[:], 'R1', 'C16')
        tt(S, 'dve', R1[:], R1[:], C16[:, 1, :], ALU.subtract, ['R1', 'C16'], 'R1')
        cp(S, 'dve', C16[:, 2, :], R1[:], 'R1', 'C16')
        ts(S, 'dve', C16[:, 3:6, :], C16[:, 0:3, :], -1.0, None, ALU.mult, None, 'C16', 'C16')
        S.dma_out('sp', D['cs'][:, :, cs_], C16[:], 'C16', 'cs')
        S.dma_in('sp', KN[:], D['kn'][:, cs_], 'KN', 'kn')
        S.op('dve', lambda e, ch=ch: e.reduce_max(out=kmx[:, ch:ch + 1], in_=KN[:], axis=AX.X), r=['KN'], w=['kmx'])
    S.op('dve', lambda e: e.reduce_max(out=kms[:, 0:1], in_=kmx[:], axis=AX.X), r=['kmx'], w=['kms'])
    ts(S, 'dve', kms[:, 1:2], kms[:, 0:1], 1.0 / 16.0, None, ALU.mult, None, 'kms', 'kms')
    for qc in range(OT // QW):
        S.dma_in('sp', QN[:], D['qn'][:, qc * QW:(qc + 1) * QW], 'QN', 'qn')
        ts(S, 'dve', NMb[:], QN[:], -4.0, kms[:, 1:2], ALU.mult, ALU.subtract, ['QN', 'kms'], 'NMb')
        S.dma_out('sp', D['negm'][:, qc * QW:(qc + 1) * QW], NMb[:], 'NMb', 'negm')
    S.small = False
    fvv = D['fv'].rearrange("(b p) d -> p b d", p=128)
    nvc = max(1, NKB // 8)
    for i in range(0, NKB, nvc):
        S.dma_in('sp', vall[:, i:i + nvc, :], fvv[:, i:i + nvc, :], 'vall', 'fv')
    qtiles = [(0, 128)] + [(128 + i * 512, 512) for i in range(OWN // 512)]
    import os
    STOP3 = float(os.environ.get('P3_STOP', '99'))
    it = [0]
    ti = 0
    conv = []
    if 'moe_wg16' in D:
        for e_ in range(8):
            for fc in range(28):
                conv.append((D['moe_wg16'][e_, fc], D['moe_wg'][e_, fc]))
                conv.append((D['moe_wu16'][e_, fc], D['moe_wu'][e_, fc]))
            for fo in range(8):
                conv.append((D['moe_wd16'][e_, fo], D['moe_wd'][e_, fo]))
    cvi = [0]
    n_items = 8 * sum((Y0 + q0 + nq) // 128 for (q0, nq) in qtiles)
    cv_every = max(1, n_items // (len(conv) + 1)) if conv else 1
    for h in range(8 if STOP3 > 1 else 0):
        sl = h % 2
        kan, qan, van = 'ka%d' % sl, 'qa%d' % sl, 'va%d' % sl
        V3 = int(os.environ.get('P3_VAR', '31'))
        if V3 & 1:
            mset(S, 'dve', ka[sl][64:72, :], 1.0, kan)
            mset(S, 'dve', qa[sl][64:72, :], 1.0, qan)
        if V3 & 2:
            S.dma_in('sp', ka[sl][0:64, :], D['fk'][h], kan, 'fk')
            S.dma_in('sp', qa[sl][0:64, :], D['fq'][h], qan, 'fq')
        if V3 & 4:
            S.dma_in('sp', ka[sl][67:70, :], D['cs'][h, 3:6, :], kan, 'cs')
            S.dma_in('sp', qa[sl][64:67, :], D['cs'][h, 0:3, Y0:W], qan, 'cs')
            S.dma_in('sp', qa[sl][70:71, :], D['negm'][h:h + 1, :], qan, 'negm')
        if V3 & 8:
            S.dma_in('pool', ka[sl][71:72, :], D['kmask'], kan)
        if V3 & 16:
            cp(S, 'pool', va[sl][:, :, 0:64], vall[:, :, h * 64:(h + 1) * 64], 'vall', van)
        items = []
        for (q0, nq) in (qtiles if STOP3 > 2 else []):
            kb_end = (Y0 + q0 + nq) // 128
            kb_d0 = (Y0 + q0) // 128
            tix = ti; ti += 1
            for kb in range(kb_end):
                items.append((q0, nq, kb, kb_end, kb_d0, tix))
        LOOK = 2
        pend = []

        def emit_scores(item):
            q0, nq, kb, kb_end, kb_d0, tix = item
            if it[0] % cv_every == 0 and cvi[0] < len(conv):
                S.dma_in('pool', conv[cvi[0]][0], conv[cvi[0]][1], 'cv%d' % (cvi[0] % 8))
                cvi[0] += 1
            j = kb - kb_d0
            c0 = max(j, 0) * 128
            i4 = it[0] % 4; it[0] += 1
            stb = st[i4]; ptb = PT[i4]; sn = 'st%d' % i4; pn = 'PT%d' % i4
            mm(S, stb[:, c0:nq], ka[sl][0:128, kb * 128:(kb + 1) * 128], qa[sl][0:128, q0 + c0:q0 + nq], True, True, [kan, qan], sn)
            act(S, ptb[:, c0:nq], stb[:, c0:nq], AF.Exp, sn, pn)
            if j >= 0:
                tt(S, 'pool', ptb[:, c0:c0 + 128], ptb[:, c0:c0 + 128], tri[:], ALU.mult, [pn, 'tri'], pn)
            return (ptb, pn, c0)

        def emit_pv(item, sc_):
            q0, nq, kb, kb_end, kb_d0, tix = item
            ptb, pn, c0 = sc_
            an = 'acc%d' % (tix % 2)
            acc = accb[tix % 2]
            mm(S, acc[0:128, c0:nq], va[sl][:, kb, :], ptb[:, c0:nq], kb == 0, kb == kb_end - 1, [van, pn], an)
            if kb == kb_end - 1:
                S.small = True
                ts(S, 'dve', rinv[64:65, 0:nq], acc[64:65, 0:nq], 1e-30, None, ALU.add, None, an, 'rinv')
                S.op('dve', lambda e, nq=nq: e.reciprocal(out=rinv[64:65, 0:nq], in_=rinv[64:65, 0:nq]), r=['rinv'], w=['rinv'])
                S.small = False
                mm(S, pbc[0:64, 0:nq], ones1[64:65, 0:64], rinv[64:65, 0:nq], True, True, ['ones1', 'rinv'], 'pbc')
                cp(S, 'act', osb[:, 0:nq], acc[0:64, 0:nq], an, 'osb')
                yb = yo[tix % 2]; yn = 'yo%d' % (tix % 2)
                tt(S, 'dve', yb[:, 0:nq], osb[:, 0:nq], pbc[0:64, 0:nq], ALU.mult, ['osb', 'pbc'], yn)
                S.dma_out('sp', D['ymix'][512 + h * 64:512 + (h + 1) * 64, q0:q0 + nq], yb[:, 0:nq], yn, 'ymixf_%d_%d' % (h, q0))

        for idx, item in enumerate(items):
            pend.append((item, emit_scores(item)))
            if len(pend) > LOOK:
                i0, s0 = pend.pop(0)
                emit_pv(i0, s0)
        for i0, s0 in pend:
            emit_pv(i0, s0)
    while cvi[0] < len(conv):
        S.dma_in('pool', conv[cvi[0]][0], conv[cvi[0]][1], 'cv%d' % (cvi[0] % 8))
        cvi[0] += 1
    S.flush()
    cx.close()


ALPHA = 4.0 ** 0.25
L_MIXG, L_MIXB, L_FFNG, L_FFNB, L_BPW1, L_BDW, L_CLG, L_CLB, L_BPW2, L_CPG, L_CPB, L_MOEG, L_MOEB, L_WDW = \
    0, 8, 16, 24, 32, 48, 56, 64, 72, 80, 88, 96, 104, 112
CLN_COLS = 112 + 31 * 8


def ln_fm_gen(S, x, xn, NQ, g, b, T16, mean, var, ones_k, psA, psB, out16, o16n, eps=1e-5, silu=False):
    cp(S, 'act', T16, x, xn, 'T16')
    for c in range(8):
        mm(S, psA, ones_k, T16[:, c, :], c == 0, c == 7, ['ones_k', 'T16'], 'psA')
    yield
    cp(S, 'act', mean, psA, 'psA', 'mean')
    tt(S, 'pool', T16, x, x, ALU.mult, xn, 'T16')
    for c in range(8):
        mm(S, psB, ones_k, T16[:, c, :], c == 0, c == 7, ['ones_k', 'T16'], 'psB')
    yield
    tt(S, 'pool', var, mean, mean, ALU.mult, 'mean', 'var')
    tt(S, 'dve', var, psB, var, ALU.subtract, ['psB', 'var'], 'var')
    act(S, var, var, AF.Ln, 'var', 'var', bias=eps)
    act(S, var, var, AF.Exp, 'var', 'var', scale=-0.5)
    yield
    mb = mean.unsqueeze(1).to_broadcast([128, 8, NQ])
    vb = var.unsqueeze(1).to_broadcast([128, 8, NQ])
    tt(S, 'dve', x, x, mb, ALU.subtract, [xn, 'mean'], xn)
    tt(S, 'pool', x, x, vb, ALU.mult, [xn, 'var'], xn)
    yield
    for c in range(8):
        if silu:
            ts(S, 'dve', x[:, c, :], x[:, c, :], g[:, c:c + 1], b[:, c:c + 1], ALU.mult, ALU.add, [xn, 'cln'], xn)
            tb_, tn_ = (mean, 'mean') if c % 2 == 0 else (var, 'var')
            act(S, tb_, x[:, c, :], AF.Sigmoid, xn, tn_)
            tt(S, 'dve', out16[:, c, :], x[:, c, :], tb_, ALU.mult, [xn, tn_], o16n)
        else:
            act(S, out16[:, c, :], x[:, c, :], AF.Identity, [xn, 'cln'], o16n, bias=b[:, c:c + 1], scale=g[:, c:c + 1])
            ts(S, 'dve', x[:, c, :], x[:, c, :], g[:, c:c + 1], b[:, c:c + 1], ALU.mult, ALU.add, [xn, 'cln'], xn)
        if c % 2 == 1:
            yield


def ln_fm(*a, **k):
    for _ in ln_fm_gen(*a, **k):
        pass


def gstep(gen, k=1):
    if gen is None:
        return None
    for _ in range(k):
        try:
            next(gen)
        except StopIteration:
            return None
    return gen


def gdrain(gen):
    if gen is not None:
        for _ in gen:
            pass
    return None


def tiles_of(total, size):
    out = []
    t = 0
    while t < total:
        n = min(size, total - t)
        out.append((t, n)); t += n
    return out


def load_w_bf16(S, dst, src_ap, name, kchunks):
    for c in range(kchunks):
        S.dma_in('pool', dst[:, c, :], src_ap[c * 128:(c + 1) * 128, :], name)


def phase4a(S, nc, W, D):
    OWN = W // 2; OT = OWN + 128; Y0 = OWN - 128
    cx = Ctx(nc)
    wo = cx.sb("wo", [128, 8, 1024], BF16)
    cln = cx.sb("cln", [128, CLN_COLS], F32)
    ones_k = cx.sb("ones_k", [128, 128], BF16)
    ym = [cx.sb("ym%d" % i, [128, 8, 512], BF16) for i in range(2)]
    xr = [cx.sb("xr%d" % i, [128, 8, 512], F32) for i in range(2)]
    xo = [cx.sb("xo%d" % i, [128, 8, 512], BF16) for i in range(2)]
    T16 = cx.sb("T16", [128, 8, 512], BF16)
    mean = cx.sb("mean", [128, 512], F32)
    var = cx.sb("var", [128, 512], F32)
    pm = [cx.ps("pm%d" % i, [128, 512], F32) for i in range(4)]
    psA = cx.ps("psA", [128, 512], F32)
    psB = cx.ps("psB", [128, 512], F32)
    S.set_alias({'pm0': 'B0', 'pm1': 'B1', 'pm2': 'B2', 'pm3': 'B3', 'psA': 'B4', 'psB': 'B5'})
    load_w_bf16(S, wo, D['w_out'], 'wo', 8)
    S.dma_in('sp', cln[:], D['cln'], 'cln')
    mset(S, 'dve', ones_k[:], 1.0 / 1024.0, 'ones_k')
    k = 0
    for ti, (t0, NQ) in enumerate(tiles_of(OT, 512)):
        s = ti % 2
        ymn, xrn, xon = 'ym%d' % s, 'xr%d' % s, 'xo%d' % s
        S.dma_in('sp', ym[s][:, :, 0:NQ], D['ymix'][:, t0:t0 + NQ].rearrange("(c p) t -> p c t", p=128), ymn, 'ymix')
        S.dma_in('sp', xr[s][:, :, 0:NQ], D['xT'][:, Y0 + t0:Y0 + t0 + NQ].rearrange("(c p) t -> p c t", p=128), xrn)
        for fo in range(8):
            pb = pm[k % 4]; pn = 'pm%d' % (k % 4); k += 1
            for c in range(8):
                mm(S, pb[:, 0:NQ], wo[:, c, fo * 128:(fo + 1) * 128], ym[s][:, c, 0:NQ], c == 0, c == 7, ['wo', ymn], pn)
            stt(S, 'dve', xr[s][:, fo, 0:NQ], xr[s][:, fo, 0:NQ], ALPHA, pb[:, 0:NQ], ALU.mult, ALU.add, [xrn, pn], xrn)
        ln_fm(S, xr[s][:, :, 0:NQ], xrn, NQ, cln[:, L_MIXG:L_MIXG + 8], cln[:, L_MIXB:L_MIXB + 8], T16[:, :, 0:NQ], mean[:, 0:NQ], var[:, 0:NQ],
              ones_k[:], psA[:, 0:NQ], psB[:, 0:NQ], xo[s][:, :, 0:NQ], xon)
        S.dma_out('sp', D['xaT'][:, t0:t0 + NQ].rearrange("(c p) t -> p c t", p=128), xr[s][:, :, 0:NQ], xrn, 'xaT_%d' % ti)
        S.dma_out('sp', D['xabT'][:, t0:t0 + NQ].rearrange("(c p) t -> p c t", p=128), xo[s][:, :, 0:NQ], xon, 'xabT_%d' % ti)
    S.flush()
    cx.close()


def phase4b(S, nc, W, D):
    OWN = W // 2; OT = OWN + 128
    import os
    TS = int(os.environ.get('TS_OVR', '384'))
    cx = Ctx(nc)
    wg = cx.sb("wg", [128, 8, 2816], BF16)
    wu = cx.sb("wu", [128, 8, 2816], BF16)
    wd = cx.sb("wd", [128, 22, 1024], BF16)
    cln = cx.sb("cln", [128, CLN_COLS], F32)
    ones_k = cx.sb("ones_k", [128, 128], BF16)
    xr = [cx.sb("xr%d" % i, [128, 8, TS], F32) for i in range(2)]
    xb = [cx.sb("xb%d" % i, [128, 8, TS], BF16) for i in range(2)]
    xo = cx.sb("xo", [128, 8, TS], BF16)
    hT = cx.sb("hT", [128, 22, TS], BF16)
    T16 = cx.sb("T16", [128, 8, TS], BF16)
    sg = [cx.sb("sg%d" % i, [128, TS], F32) for i in range(2)]
    mean = cx.sb("mean", [128, TS], F32)
    var = cx.sb("var", [128, TS], F32)
    pg = [cx.ps("pg%d" % i, [128, 512], F32) for i in range(3)]
    pu = [cx.ps("pu%d" % i, [128, 512], F32) for i in range(3)]
    psA = cx.ps("psA", [128, 512], F32)
    psB = cx.ps("psB", [128, 512], F32)
    S.set_alias({'pg0': 'B0', 'pg1': 'B1', 'pg2': 'B2', 'pu0': 'B3', 'pu1': 'B4', 'pu2': 'B5', 'psA': 'B6', 'psB': 'B7'})
    load_w_bf16(S, wg, D['w_gate'], 'wg', 8)
    load_w_bf16(S, wu, D['w_up'], 'wu', 8)
    load_w_bf16(S, wd, D['w_down'], 'wd', 22)
    S.dma_in('sp', cln[:], D['cln'], 'cln')
    mset(S, 'dve', ones_k[:], 1.0 / 1024.0, 'ones_k')
    k = [0]
    tl = tiles_of(OT, TS)

    def load(ti):
        t0, NQ = tl[ti]; s_ = ti % 2
        S.dma_in('sp', xr[s_][:, :, 0:NQ], D['xaT'][:, t0:t0 + NQ].rearrange("(c p) t -> p c t", p=128), 'xr%d' % s_, 'xaT')
        S.dma_in('sp', xb[s_][:, :, 0:NQ], D['xabT'][:, t0:t0 + NQ].rearrange("(c p) t -> p c t", p=128), 'xb%d' % s_, 'xabT')

    def gate_up(ti):
        t0, NQ = tl[ti]; s_ = ti % 2
        xbn = 'xb%d' % s_
        for fc in range(22):
            i3 = k[0] % 3; k[0] += 1
            gb, ub = pg[i3], pu[i3]; gn, un = 'pg%d' % i3, 'pu%d' % i3
            for c in range(8):
                mm(S, gb[:, 0:NQ], wg[:, c, fc * 128:(fc + 1) * 128], xb[s_][:, c, 0:NQ], c == 0, c == 7, ['wg', xbn], gn)
            for c in range(8):
                mm(S, ub[:, 0:NQ], wu[:, c, fc * 128:(fc + 1) * 128], xb[s_][:, c, 0:NQ], c == 0, c == 7, ['wu', xbn], un)
            sgb = sg[fc % 2]; sgn = 'sg%d' % (fc % 2)
            act(S, sgb[:, 0:NQ], gb[:, 0:NQ], AF.Silu, gn, sgn)
            tt(S, 'dve', hT[:, fc, 0:NQ], sgb[:, 0:NQ], ub[:, 0:NQ], ALU.mult, [sgn, un], 'hT')
            if fc % 2 == 1:
                yield

    def down(ti):
        t0, NQ = tl[ti]; s_ = ti % 2
        xrn = 'xr%d' % s_
        for fo in range(8):
            i3 = k[0] % 3; k[0] += 1
            gb = pg[i3]; gn = 'pg%d' % i3
            for fc in range(22):
                mm(S, gb[:, 0:NQ], wd[:, fc, fo * 128:(fo + 1) * 128], hT[:, fc, 0:NQ], fc == 0, fc == 21, ['wd', 'hT'], gn)
            stt(S, 'dve', xr[s_][:, fo, 0:NQ], xr[s_][:, fo, 0:NQ], ALPHA, gb[:, 0:NQ], ALU.mult, ALU.add, [xrn, gn], xrn)

    def ln_store(ti):
        t0, NQ = tl[ti]; s_ = ti % 2
        xrn = 'xr%d' % s_
        for _ in ln_fm_gen(S, xr[s_][:, :, 0:NQ], xrn, NQ, cln[:, L_FFNG:L_FFNG + 8], cln[:, L_FFNB:L_FFNB + 8], T16[:, :, 0:NQ],
                           mean[:, 0:NQ], var[:, 0:NQ], ones_k[:], psA[:, 0:NQ], psB[:, 0:NQ], xo[:, :, 0:NQ], 'xo'):
            yield
        S.dma_out('sp', D['x1T'][:, t0:t0 + NQ].rearrange("(c p) t -> p c t", p=128), xr[s_][:, :, 0:NQ], xrn, 'x1T_%d' % ti)
        S.dma_out('sp', D['x1bT'][:, t0:t0 + NQ].rearrange("(c p) t -> p c t", p=128), xo[:, :, 0:NQ], 'xo', 'x1bT_%d' % ti)

    load(0)
    lnp = None
    for ti in range(len(tl)):
        for _ in gate_up(ti):
            lnp = gstep(lnp, 1)
        lnp = gdrain(lnp)
        if ti + 1 < len(tl):
            load(ti + 1)
        down(ti)
        lnp = ln_store(ti)
    gdrain(lnp)
    S.flush()
    cx.close()


def _cln(inp):
    c = np.zeros((128, CLN_COLS), np.float32)
    def col8(v): return v.reshape(-1, 128).T
    c[:, L_MIXG:L_MIXG + 8] = col8(inp["mix_ln_g"][0]); c[:, L_MIXB:L_MIXB + 8] = col8(inp["mix_ln_b"][0])
    c[:, L_FFNG:L_FFNG + 8] = col8(inp["ffn_ln_g"][0]); c[:, L_FFNB:L_FFNB + 8] = col8(inp["ffn_ln_b"][0])
    c[:, L_BPW1:L_BPW1 + 16] = col8(inp["conv_b_pw1"][0]); c[:, L_BDW:L_BDW + 8] = col8(inp["conv_b_dw"][0])
    c[:, L_CLG:L_CLG + 8] = col8(inp["conv_ln_g"][0]); c[:, L_CLB:L_CLB + 8] = col8(inp["conv_ln_b"][0])
    c[:, L_BPW2:L_BPW2 + 8] = col8(inp["conv_b_pw2"][0])
    c[:, L_CPG:L_CPG + 8] = col8(inp["conv_post_ln_g"][0]); c[:, L_CPB:L_CPB + 8] = col8(inp["conv_post_ln_b"][0])
    c[:, L_MOEG:L_MOEG + 8] = col8(inp["moe_ln_g"][0]); c[:, L_MOEB:L_MOEB + 8] = col8(inp["moe_ln_b"][0])
    wdw = inp["conv_w_dw"][0]
    for j in range(31):
        c[:, L_WDW + j * 8:L_WDW + (j + 1) * 8] = col8(wdw[j])
    return c


def phase5(S, nc, W, D):
    OWN = W // 2; OT = OWN + 128
    import os
    TS = int(os.environ.get('TS_OVR', '384'))
    cx = Ctx(nc)
    w1 = cx.sb("w1", [128, 8, 2048], BF16)
    w2 = cx.sb("w2", [128, 8, 1024], BF16)
    dg = cx.sb("dg", [128, 31, 8, 128], BF16)
    idb = cx.sb("idb", [128, 128], BF16)
    cln = cx.sb("cln", [128, CLN_COLS], F32)
    hm = cx.sb("hm", [128, 1], F32)
    ones_k = cx.sb("ones_k", [128, 128], BF16)
    hb = [cx.sb("hb%d" % i, [128, 8, 30 + TS], BF16) for i in range(2)]
    xb = [cx.sb("xb%d" % i, [128, 8, TS], BF16) for i in range(2)]
    x1 = [cx.sb("x1_%d" % i, [128, 8, TS], F32) for i in range(3)]
    xc = cx.sb("xc", [128, 8, TS], F32)
    hc = cx.sb("hc", [128, 8, TS], BF16)
    T16 = cx.sb("T16", [128, 8, TS], BF16)
    sg = [cx.sb("sg%d" % i, [128, TS], F32) for i in range(2)]
    mean = cx.sb("mean", [128, TS], F32)
    var = cx.sb("var", [128, TS], F32)
    pv = [cx.ps("pv%d" % i, [128, 512], F32) for i in range(3)]
    pg = [cx.ps("pg%d" % i, [128, 512], F32) for i in range(3)]
    psA = cx.ps("psA", [128, 512], F32)
    psB = cx.ps("psB", [128, 512], F32)
    S.set_alias({'pv0': 'B0', 'pv1': 'B1', 'pv2': 'B2', 'pg0': 'B3', 'pg1': 'B4', 'pg2': 'B5', 'psA': 'B6', 'psB': 'B7'})
    load_w_bf16(S, w1, D['w_pw1'], 'w1', 8)
    load_w_bf16(S, w2, D['w_pw2'], 'w2', 8)
    S.dma_in('pool', idb[:], D['cm16'][:, 0:128], 'idb')
    S.dma_in('sp', cln[:], D['cln'], 'cln')
    S.dma_in('sp', hm[:], D['hmask'], 'hm')
    mset(S, 'dve', ones_k[:], 1.0 / 1024.0, 'ones_k')
    q = 0
    for j in range(31):
        for c in range(8):
            if q % 2 == 0:
                ts(S, 'dve', dg[:, j, c, :], idb[:], cln[:, L_WDW + j * 8 + c:L_WDW + j * 8 + c + 1], None, ALU.mult, None, ['idb', 'cln'], 'dg')
            else:
                act(S, dg[:, j, c, :], idb[:], AF.Copy, ['idb', 'cln'], 'dg', scale=cln[:, L_WDW + j * 8 + c:L_WDW + j * 8 + c + 1])
            q += 1
    k = [0]
    tl = tiles_of(OWN, TS)

    def glu(xbt, xbn, NQ, hbt, hbn):
        for fo in range(8):
            i3 = k[0] % 3; k[0] += 1
            vb, gb = pv[i3], pg[i3]; vn, gn = 'pv%d' % i3, 'pg%d' % i3
            for c in range(8):
                mm(S, vb[:, 0:NQ], w1[:, c, fo * 128:(fo + 1) * 128], xbt[:, c, 0:NQ], c == 0, c == 7, ['w1', xbn], vn)
            for c in range(8):
                mm(S, gb[:, 0:NQ], w1[:, c, 1024 + fo * 128:1024 + (fo + 1) * 128], xbt[:, c, 0:NQ], c == 0, c == 7, ['w1', xbn], gn)
            sgb = sg[fo % 2]; sgn = 'sg%d' % (fo % 2)
            act(S, sgb[:, 0:NQ], gb[:, 0:NQ], AF.Sigmoid, [gn, 'cln'], sgn, bias=cln[:, L_BPW1 + 8 + fo:L_BPW1 + 9 + fo])
            stt(S, 'dve', hbt[:, fo, 30:30 + NQ], vb[:, 0:NQ], cln[:, L_BPW1 + fo:L_BPW1 + fo + 1], sgb[:, 0:NQ], ALU.add, ALU.mult,
                [vn, sgn, 'cln'], hbn)
            yield

    def load(ti):
        t0, NQ = tl[ti]; s_ = ti % 2; s3_ = ti % 3
        S.dma_in('sp', xb[s_][:, :, 0:NQ], D['x1bT'][:, 128 + t0:128 + t0 + NQ].rearrange("(c p) t -> p c t", p=128), 'xb%d' % s_, 'x1bT')
        S.dma_in('sp', x1[s3_][:, :, 0:NQ], D['x1T'][:, 128 + t0:128 + t0 + NQ].rearrange("(c p) t -> p c t", p=128), 'x1_%d' % s3_, 'x1T')

    def glu_tile(ti):
        t0, NQ = tl[ti]; s_ = ti % 2
        for _ in glu(xb[s_], 'xb%d' % s_, NQ, hb[s_], 'hb%d' % s_):
            yield
        cp(S, 'pool', hb[1 - s_][:, :, 0:30], hb[s_][:, :, NQ:NQ + 30], 'hb%d' % s_, 'hb%d' % (1 - s_))

    def dw(ti):
        t0, NQ = tl[ti]; s_ = ti % 2
        hbn = 'hb%d' % s_
        for c in range(8):
            i3 = k[0] % 3; k[0] += 1
            vb = pv[i3]; vn = 'pv%d' % i3
            for j in range(31):
                mm(S, vb[:, 0:NQ], dg[:, j, c, :], hb[s_][:, c, j:j + NQ], j == 0, j == 30, ['dg', hbn], vn)
            act(S, xc[:, c, 0:NQ], vb[:, 0:NQ], AF.Identity, [vn, 'cln'], 'xc', bias=cln[:, L_BDW + c:L_BDW + c + 1])
            yield

    def ln1(ti):
        t0, NQ = tl[ti]
        return ln_fm_gen(S, xc[:, :, 0:NQ], 'xc', NQ, cln[:, L_CLG:L_CLG + 8], cln[:, L_CLB:L_CLB + 8], T16[:, :, 0:NQ], mean[:, 0:NQ],
                         var[:, 0:NQ], ones_k[:], psA[:, 0:NQ], psB[:, 0:NQ], hc[:, :, 0:NQ], 'hc', silu=True)

    def pw2(ti):
        t0, NQ = tl[ti]; s_ = ti % 3
        x1n = 'x1_%d' % s_
        act(S, x1[s_][:, :, 0:NQ], x1[s_][:, :, 0:NQ], AF.Copy, x1n, x1n, scale=ALPHA)
        for fo in range(8):
            i3 = k[0] % 3; k[0] += 1
            gb = pg[i3]; gn = 'pg%d' % i3
            for c in range(8):
                mm(S, gb[:, 0:NQ], w2[:, c, fo * 128:(fo + 1) * 128], hc[:, c, 0:NQ], c == 0, c == 7, ['w2', 'hc'], gn)
            stt(S, 'dve', x1[s_][:, fo, 0:NQ], gb[:, 0:NQ], cln[:, L_BPW2 + fo:L_BPW2 + fo + 1], x1[s_][:, fo, 0:NQ], ALU.add, ALU.add,
                [gn, 'cln', x1n], x1n)

    def ln2_store(ti):
        t0, NQ = tl[ti]; s_ = ti % 3
        x1n = 'x1_%d' % s_
        for _ in ln_fm_gen(S, x1[s_][:, :, 0:NQ], x1n, NQ, cln[:, L_CPG:L_CPG + 8], cln[:, L_CPB:L_CPB + 8], T16[:, :, 0:NQ], mean[:, 0:NQ],
                           var[:, 0:NQ], ones_k[:], psA[:, 0:NQ], psB[:, 0:NQ], hc[:, :, 0:NQ], 'hc'):
            yield
        S.dma_out('sp', D['x2T'][:, t0:t0 + NQ].rearrange("(c p) t -> p c t", p=128), x1[s_][:, :, 0:NQ], x1n, 'x2T_%d' % ti)
        S.dma_out('sp', D['x2bT'][:, t0:t0 + NQ].rearrange("(c p) t -> p c t", p=128), hc[:, :, 0:NQ], 'hc', 'x2bT_%d' % ti)

    S.dma_in('sp', xb[1][:, :, 0:128], D['x1bT'][:, 0:128].rearrange("(c p) t -> p c t", p=128), 'xb1', 'x1bT')
    for _ in glu(xb[1], 'xb1', 128, hb[1], 'hb1'):
        pass
    ts(S, 'dve', hb[0][:, :, 0:30], hb[1][:, :, 128:158], hm[:, 0:1], None, ALU.mult, None, ['hb1', 'hm'], 'hb0')
    load(0)
    gdrain(glu_tile(0))
    l2 = None
    for ti in range(len(tl)):
        if ti + 1 < len(tl):
            load(ti + 1)
        for _ in dw(ti):
            l2 = gstep(l2, 1)
        l2 = gdrain(l2)
        l1 = ln1(ti)
        if ti + 1 < len(tl):
            for _ in glu_tile(ti + 1):
                l1 = gstep(l1, 1)
        gdrain(l1)
        pw2(ti)
        l2 = ln2_store(ti)
    gdrain(l2)
    S.flush()
    cx.close()


def phase6(S, nc, W, D):
    OWN = W // 2
    TB = min(1024, OWN)
    NH = TB // 512
    cx = Ctx(nc)
    cln = cx.sb("cln", [128, CLN_COLS], F32)
    ones_k = cx.sb("ones_k", [128, 128], BF16)
    id32 = cx.sb("id32", [128, 128], F32)
    sel = cx.sb("sel", [8, 8, 128], F32)
    wr = cx.sb("wr", [128, 8, 8], F32)
    xb = cx.sb("xb", [128, 8, TB], BF16)
    hT = cx.sb("hT", [128, 28, TB], BF16)
    yacc = cx.sb("yacc", [128, 8, TB], F32)
    gF = cx.sb("gF", [8, TB], F32)
    gbc = [cx.sb("gbc%d" % i, [128, TB], F32) for i in range(2)]
    NWB = 4
    wgb = [cx.sb("wgb%d" % i, [128, 8, 128], BF16) for i in range(NWB)]
    wub = [cx.sb("wub%d" % i, [128, 8, 128], BF16) for i in range(NWB)]
    wdb = [cx.sb("wdb%d" % i, [128, 28, 128], BF16) for i in range(3)]
    sg = [cx.sb("sg%d" % i, [128, 512], F32) for i in range(2)]
    tmp = [cx.sb("tmp%d" % i, [128, 512], F32) for i in range(2)]
    mean = cx.sb("mean", [128, 512], F32)
    var = cx.sb("var", [128, 512], F32)
    lt = cx.sb("lt", [128, 8], F32)
    m8 = cx.sb("m8", [128, 8], F32)
    msk = cx.sb("msk", [128, 8], F32)
    ex = cx.sb("ex", [128, 8], F32)
    sc = cx.sb("sc", [128, 4], F32)
    pg = [cx.ps("pg%d" % i, [128, 512], F32) for i in range(3)]
    pu = [cx.ps("pu%d" % i, [128, 512], F32) for i in range(3)]
    psA = cx.ps("psA", [128, 512], F32)
    psB = cx.ps("psB", [128, 512], F32)
    S.set_alias({'pg0': 'B0', 'pg1': 'B1', 'pg2': 'B2', 'pu0': 'B3', 'pu1': 'B4', 'pu2': 'B5', 'psA': 'B6', 'psB': 'B7'})
    S.dma_in('sp', cln[:], D['cln'], 'cln')
    S.dma_in('sp', id32[:], D['id32'], 'id32')
    S.dma_in('sp', sel[:], D['sel'], 'sel')
    S.dma_in('sp', wr[:], D['w_router'].rearrange("(c p) e -> p c e", p=128), 'wr')
    mset(S, 'dve', ones_k[:], 1.0 / 1024.0, 'ones_k')
    xr32 = hT[:].rearrange("p a t -> p (a t)").bitcast(F32)[:, 0:8 * 512].rearrange("p (c t) -> p c t", t=512)
    k = [0]
    wi = [0]
    wj = [0]
    for ti, (t0, NT_) in enumerate(tiles_of(OWN, TB)):
        nh = NT_ // 512
        for nb in range(nh):
            c0 = t0 + nb * 512
            S.dma_in('sp', xr32, D['x2T'][:, c0:c0 + 512].rearrange("(c p) t -> p c t", p=128), 'hT', 'x2T')
            for tb_ in range(4):
                S.small = True
                for c in range(8):
                    mm(S, psA[:, 0:8], xr32[:, c, tb_ * 128:(tb_ + 1) * 128], wr[:, c, :], c == 0, c == 7, ['hT', 'wr'], 'psA')
                cp(S, 'dve', lt[:], psA[:, 0:8], 'psA', 'lt')
                S.op('dve', lambda e: e.max(out=m8[:], in_=lt[:]), r=['lt'], w=['m8'])
                ts(S, 'dve', msk[:], lt[:], m8[:, 1:2], None, ALU.is_ge, None, ['lt', 'm8'], 'msk')
                ts(S, 'dve', sc[:, 0:1], m8[:, 0:1], -1.0, None, ALU.mult, None, 'm8', 'sc')
                act(S, ex[:], lt[:], AF.Exp, ['lt', 'sc'], 'ex', bias=sc[:, 0:1])
                tt(S, 'dve', ex[:], ex[:], msk[:], ALU.mult, ['ex', 'msk'], 'ex')
                S.op('dve', lambda e: e.reduce_sum(out=sc[:, 1:2], in_=ex[:], axis=AX.X), r=['ex'], w=['sc'])
                S.op('dve', lambda e: e.reciprocal(out=sc[:, 2:3], in_=sc[:, 1:2]), r=['sc'], w=['sc'])
                ts(S, 'dve', ex[:], ex[:], sc[:, 2:3], None, ALU.mult, None, ['ex', 'sc'], 'ex')
                S.op('pe', lambda e: e.transpose(psB[0:8, 0:128], ex[:], id32[:]), r=['ex', 'id32'], w=['psB'])
                col = nb * 512 + tb_ * 128
                cp(S, 'dve', gF[:, col:col + 128], psB[0:8, 0:128], 'psB', 'gF')
                S.small = False
            act(S, yacc[:, :, nb * 512:(nb + 1) * 512], xr32, AF.Copy, 'hT', 'yacc', scale=ALPHA)
        if 'dbg_gF' in D:
            S.dma_out('sp', D['dbg_gF'][:, t0:t0 + NT_], gF[:, 0:NT_], 'gF', 'dbg_gF')
        S.dma_in('sp', xb[:, :, 0:NT_], D['x2bT'][:, t0:t0 + NT_].rearrange("(c p) t -> p c t", p=128), 'xb', 'x2bT')
        for e_ in range(8):
            gb_ = gbc[e_ % 2]; gbn = 'gbc%d' % (e_ % 2)
            for nb in range(nh):
                mm(S, psA[:, 0:512], sel[:, e_, :], gF[:, nb * 512:(nb + 1) * 512], True, True, ['sel', 'gF'], 'psA')
                cp(S, 'act', gb_[:, nb * 512:(nb + 1) * 512], psA[:, 0:512], 'psA', gbn)
            for fc in range(28):
                s2 = wi[0] % NWB; wi[0] += 1
                S.dma_in('sp', wgb[s2][:], D['moe_wg16'][e_, fc], 'wgb%d' % s2)
                S.dma_in('sp', wub[s2][:], D['moe_wu16'][e_, fc], 'wub%d' % s2)
                for nb in range(nh):
                    i3 = k[0] % 3; k[0] += 1
                    gb, ub = pg[i3], pu[i3]; gn, un = 'pg%d' % i3, 'pu%d' % i3
                    ns = slice(nb * 512, (nb + 1) * 512)
                    for c in range(8):
                        mm(S, gb[:, :], wgb[s2][:, c, :], xb[:, c, ns], c == 0, c == 7, ['wgb%d' % s2, 'xb'], gn)
                    for c in range(8):
                        mm(S, ub[:, :], wub[s2][:, c, :], xb[:, c, ns], c == 0, c == 7, ['wub%d' % s2, 'xb'], un)
                    sgb = sg[k[0] % 2]; sgn = 'sg%d' % (k[0] % 2)
                    act(S, sgb[:], gb[:, :], AF.Silu, gn, sgn)
                    tt(S, 'dve', hT[:, fc, ns], sgb[:], ub[:, :], ALU.mult, [sgn, un], 'hT')
            for fo in range(8):
                s2 = wj[0] % 3; wj[0] += 1
                S.dma_in('sp', wdb[s2][:], D['moe_wd16'][e_, fo], 'wdb%d' % s2)
                for nb in range(nh):
                    i3 = k[0] % 3; k[0] += 1
                    gb = pg[i3]; gn = 'pg%d' % i3
                    ns = slice(nb * 512, (nb + 1) * 512)
                    for fc in range(28):
                        mm(S, gb[:, :], wdb[s2][:, fc, :], hT[:, fc, ns], fc == 0, fc == 27, ['wdb%d' % s2, 'hT'], gn)
                    tb2 = tmp[k[0] % 2]; tn = 'tmp%d' % (k[0] % 2)
                    tt(S, 'dve', tb2[:], gb[:, :], gb_[:, ns], ALU.mult, [gn, gbn], tn)
                    tt(S, 'pool', yacc[:, fo, ns], yacc[:, fo, ns], tb2[:], ALU.add, ['yacc', tn], 'yacc')
        if 'dbg_y' in D:
            S.dma_out('sp', D['dbg_y'][:, t0:t0 + NT_].rearrange("(c p) t -> p c t", p=128), yacc[:, :, 0:NT_], 'yacc', 'dbg_y')
        for nb in range(nh):
            ns = slice(nb * 512, (nb + 1) * 512)
            ln_fm(S, yacc[:, :, ns], 'yacc', 512, cln[:, L_MOEG:L_MOEG + 8], cln[:, L_MOEB:L_MOEB + 8], xb[:, :, ns], mean[:], var[:],
                  ones_k[:], psA[:, 0:512], psB[:, 0:512], xb[:, :, ns], 'xb')
        S.dma_out('sp', D['outT'][:, t0:t0 + NT_].rearrange("(c p) t -> p c t", p=128), yacc[:, :, 0:NT_], 'yacc', 'outT_%d' % ti)
    S.flush()
    cx.close()


def _moe_layout(inp):
    wg = inp["moe_w_gate"][0]; wu = inp["moe_w_up"][0]; wd = inp["moe_w_down"][0]
    wg_l = np.ascontiguousarray(wg.reshape(8, 8, 128, 28, 128).transpose(0, 3, 2, 1, 4))
    wu_l = np.ascontiguousarray(wu.reshape(8, 8, 128, 28, 128).transpose(0, 3, 2, 1, 4))
    wd_l = np.ascontiguousarray(wd.reshape(8, 28, 128, 8, 128).transpose(0, 3, 2, 1, 4))
    return wg_l, wu_l, wd_l


def _sel():
    s = np.zeros((8, 8, 128), np.float32)
    for e in range(8):
        s[e, e, :] = 1.0
    return s


def build_program(W):
    OWN = W // 2; OT = OWN + 128
    nc = bass.Bass("TRN2", target_bir_lowering=False)

    def dt_(name, shape, dt, kind="Internal"):
        return nc.dram_tensor(name, shape, dt, kind=kind).ap()
    EI = "ExternalInput"
    D = {}
    D['xT'] = dt_("xT", [1024, W], F32, EI); D['w_in'] = dt_("w_in", [1024, 3240], F32, EI)
    D['c12'] = dt_("c12", [128, 48], F32, EI); D['cm32'] = dt_("cm32", [128, 1152], F32, EI); D['cm16'] = dt_("cm16", [128, 384], F32, EI)
    D['lw'] = dt_("lw", [64, 512], F32, EI); D['gu'] = dt_("gu", [96, 512], F32, EI); D['bf'] = dt_("bf", [8, 1], F32, EI)
    D['ctri'] = dt_("ctri", [128, 128], F32, EI); D['kmask'] = dt_("kmask", [1, W], F32, EI)
    D['w_out'] = dt_("w_out", [1024, 1024], F32, EI); D['cln'] = dt_("cln", [128, CLN_COLS], F32, EI)
    D['w_gate'] = dt_("w_gate", [1024, 2816], F32, EI); D['w_up'] = dt_("w_up", [1024, 2816], F32, EI); D['w_down'] = dt_("w_down", [2816, 1024], F32, EI)
    D['w_pw1'] = dt_("w_pw1", [1024, 2048], F32, EI); D['w_pw2'] = dt_("w_pw2", [1024, 1024], F32, EI); D['hmask'] = dt_("hmask", [128, 1], F32, EI)
    D['id32'] = dt_("id32", [128, 128], F32, EI); D['sel'] = dt_("sel", [8, 8, 128], F32, EI); D['w_router'] = dt_("w_router", [1024, 8], F32, EI)
    D['moe_wg'] = dt_("moe_wg", [8, 28, 128, 8, 128], F32, EI); D['moe_wu'] = dt_("moe_wu", [8, 28, 128, 8, 128], F32, EI)
    D['moe_wd'] = dt_("moe_wd", [8, 8, 128, 28, 128], F32, EI)
    D['moe_wg16'] = dt_("moe_wg16", [8, 28, 128, 8, 128], BF16); D['moe_wu16'] = dt_("moe_wu16", [8, 28, 128, 8, 128], BF16)
    D['moe_wd16'] = dt_("moe_wd16", [8, 8, 128, 28, 128], BF16)
    D['fq'] = dt_("fq", [8, 64, OT], BF16); D['fk'] = dt_("fk", [8, 64, W], BF16); D['fv'] = dt_("fv", [W, 512], BF16)
    D['qn'] = dt_("qn", [8, OT], F32); D['kn'] = dt_("kn", [8, W], F32); D['cs'] = dt_("cs", [8, 6, W], BF16); D['negm'] = dt_("negm", [8, OT], BF16)
    D['fc'] = dt_("fc", [8, W], F32); D['ymix'] = dt_("ymix", [1024, OT], BF16)
    D['xaT'] = dt_("xaT", [1024, OT], F32); D['xabT'] = dt_("xabT", [1024, OT], BF16)
    D['x1T'] = dt_("x1T", [1024, OT], F32); D['x1bT'] = dt_("x1bT", [1024, OT], BF16)
    D['x2T'] = dt_("x2T", [1024, OWN], F32); D['x2bT'] = dt_("x2bT", [1024, OWN], BF16)
    D['outT'] = dt_("outT", [1024, OWN], F32, "ExternalOutput")
    S = Sched(nc)
    phase12(S, nc, W, D)
    phase3(S, nc, W, D)
    phase4a(S, nc, W, D)
    phase4b(S, nc, W, D)
    phase5(S, nc, W, D)
    phase6(S, nc, W, D)
    S.close()
    return nc


def make_in_maps(inputs, W, cores):
    OWN = W // 2
    inp = {k: np.asarray(v) for k, v in inputs.items()}
    cm32, cm16 = _masks()
    t = np.arange(128)
    ctri = (t[:, None] <= t[None, :]).astype(np.float32)
    wg_l, wu_l, wd_l = _moe_layout(inp)
    shared = dict(
        w_in=np.ascontiguousarray(inp["mix_w_in"][0]), c12=_c12(inp), cm32=cm32, cm16=cm16,
        lw=np.ascontiguousarray(np.concatenate([inp["rwkv_w_up"][0], inp["rwkv_a_up"][0]], 0)),
        gu=np.ascontiguousarray(inp["rwkv_g_up"][0]), bf=np.ascontiguousarray(inp["fox_b_f"][0].reshape(8, 1)),
        ctri=ctri, w_out=np.ascontiguousarray(inp["mix_w_out"][0]), cln=_cln(inp),
        w_gate=np.ascontiguousarray(inp["ffn_w_gate"][0]), w_up=np.ascontiguousarray(inp["ffn_w_up"][0]),
        w_down=np.ascontiguousarray(inp["ffn_w_down"][0]), w_pw1=np.ascontiguousarray(inp["conv_w_pw1"][0]),
        w_pw2=np.ascontiguousarray(inp["conv_w_pw2"][0]), id32=np.eye(128, dtype=np.float32), sel=_sel(),
        w_router=np.ascontiguousarray(inp["moe_w_router"][0]), moe_wg=wg_l, moe_wu=wu_l, moe_wd=wd_l)
    maps = []
    x = inp["x"]
    for (b, hf) in cores:
        km = np.zeros((1, W), np.float32)
        if hf == 1:
            xw = x[b, 0:W]
        else:
            xw = np.concatenate([np.zeros((OWN, 1024), np.float32), x[b, 0:OWN]], 0)
            km[:, :OWN] = -30000.0
        m = dict(shared)
        m['xT'] = np.ascontiguousarray(xw.T)
        m['kmask'] = km
        m['hmask'] = np.full((128, 1), float(hf), np.float32)
        maps.append(m)
    return maps


_NC_CACHE = {}


def kernel(**inputs):
    x = np.asarray(inputs["x"])
    B, T, Dm = x.shape
    W = T
    OWN = W // 2
    if W not in _NC_CACHE:
        _NC_CACHE[W] = build_program(W)
    nc = _NC_CACHE[W]
    cores = [(b, hf) for b in range(B) for hf in range(2)]
    maps = make_in_maps(inputs, W, cores)
    res = run_bass_kernel_spmd(nc, maps, core_ids=list(range(len(cores))))
    out = np.empty((B, T, Dm), np.float32)
    for i, (b, hf) in enumerate(cores):
        out[b, hf * OWN:(hf + 1) * OWN, :] = np.asarray(res.results[i]["outT"]).T
    return out
```
